# Optimizing a Trainium2 kernel written in Bass

```python
import math
import jax, jax.numpy as jnp
from jax import lax
import numpy as np


D_MODEL = 1024
BATCH = 1
SEQ = 16384
DEPTH = 4

CHUNK = 64
N_MIXERS = 3
N_S5_LAYERS = (DEPTH + 2) // 3
N_GLA_LAYERS = (DEPTH + 1) // 3
N_DIFF_LAYERS = DEPTH // 3
NORM_EPS = 1e-6
S5_GROUP = 16
S5_GROUPS = D_MODEL // S5_GROUP
S5_STATE = 64
DT_MIN = 1e-3
DT_MAX = 1e-1
GLA_HEADS = 4
GLA_DK = D_MODEL // 2
GLA_DV = D_MODEL
GLA_DKH = GLA_DK // GLA_HEADS
GLA_DVH = GLA_DV // GLA_HEADS
GLA_GATE_RANK = 16
GLA_TEMP = 16.0
GLA_IN = 2 * GLA_DK + 2 * GLA_DV + GLA_GATE_RANK
DIFF_HD = 64
DIFF_HEADS = D_MODEL // (2 * DIFF_HD)
ROT_DIMS = DIFF_HD // 4
ROPE_THETA = 500000.0
Q_BLOCK = 128
NEG_INF = -1e30
FFN_HIDDEN = -(-8 * D_MODEL // (3 * 256)) * 256

kernel_name = 'hybrid_s5_gla_diffattn_trunk'


def rmsnorm(x, g):
    xf = x.astype(jnp.float32)
    y = xf * lax.rsqrt(jnp.mean(xf * xf, axis=-1, keepdims=True) + NORM_EPS)
    return (y * g.astype(jnp.float32)).astype(x.dtype)


def swiglu_ffn(h, w_gate_up, w_down):
    g, u = jnp.split(h @ w_gate_up, 2, axis=-1)
    return (jax.nn.silu(g) * u) @ w_down


def _cdiag_combine(e1, e2):
    a1r, a1i, b1r, b1i = e1
    a2r, a2i, b2r, b2i = e2
    ar = a2r * a1r - a2i * a1i
    ai = a2r * a1i + a2i * a1r
    br = a2r * b1r - a2i * b1i + b2r
    bi = a2r * b1i + a2i * b1r + b2i
    return (ar, ai, br, bi)


def s5_mixer(h, lam_re, lam_im, log_dt, b_re, b_im, c_re, c_im, d_skip, w_glu):
    bsz, seq, _ = h.shape
    n_chunks = seq // CHUNK
    f32 = jnp.float32
    lr = lam_re.astype(f32)
    li = lam_im.astype(f32)
    dt = jnp.exp(log_dt.astype(f32))[:, None]
    ab_mag = jnp.exp(lr * dt)
    ab_ang = li * dt
    ab_re = ab_mag * jnp.cos(ab_ang)
    ab_im = ab_mag * jnp.sin(ab_ang)
    den = lr * lr + li * li
    f_re = ((ab_re - 1.0) * lr + ab_im * li) / den
    f_im = (ab_im * lr - (ab_re - 1.0) * li) / den
    br = b_re.astype(f32)
    bi = b_im.astype(f32)
    bb_re = f_re[..., None] * br - f_im[..., None] * bi
    bb_im = f_re[..., None] * bi + f_im[..., None] * br
    j = jnp.arange(1, CHUNK + 1, dtype=f32)[:, None, None]
    p_mag = jnp.exp(lr * dt * j)
    p_ang = li * dt * j
    p_re = p_mag * jnp.cos(p_ang)
    p_im = p_mag * jnp.sin(p_ang)
    cr = c_re.astype(f32)
    ci = c_im.astype(f32)
    u = h.astype(f32).reshape(bsz, n_chunks, CHUNK, S5_GROUPS, S5_GROUP)
    u = jnp.moveaxis(u, 1, 0)

    def chunk_step(carry, u_c):
        s_re0, s_im0 = carry
        bu_re = jnp.einsum('bcgh,gph->bcgp', u_c, bb_re)
        bu_im = jnp.einsum('bcgh,gph->bcgp', u_c, bb_im)
        a_re = jnp.broadcast_to(ab_re, bu_re.shape)
        a_im = jnp.broadcast_to(ab_im, bu_re.shape)
        _, _, s_re, s_im = lax.associative_scan(_cdiag_combine, (a_re, a_im, bu_re, bu_im), axis=1)
        s_re = s_re + p_re * s_re0[:, None] - p_im * s_im0[:, None]
        s_im = s_im + p_re * s_im0[:, None] + p_im * s_re0[:, None]
        y = jnp.einsum('bcgp,ghp->bcgh', s_re, cr) - jnp.einsum('bcgp,ghp->bcgh', s_im, ci)
        return (s_re[:, -1], s_im[:, -1]), y

    zeros = jnp.zeros((bsz, S5_GROUPS, S5_STATE), f32)
    _, y = lax.scan(chunk_step, (zeros, zeros), u)
    y = jnp.moveaxis(y, 0, 1).reshape(bsz, seq, D_MODEL)
    y = y + d_skip.astype(f32) * h.astype(f32)
    z = jax.nn.gelu(y).astype(h.dtype)
    a, b = jnp.split(z @ w_glu, 2, axis=-1)
    return a * jax.nn.sigmoid(b)


def gla_mixer(h, w_in, w_a2, b_a, norm_g, w_o):
    bsz, seq, _ = h.shape
    nc = seq // CHUNK
    f32 = jnp.float32
    proj = h @ w_in
    q, k, v, g, a_lo = jnp.split(proj, [GLA_DK, 2 * GLA_DK, 2 * GLA_DK + GLA_DV, 2 * GLA_DK + 2 * GLA_DV], axis=-1)
    log_a = jax.nn.log_sigmoid((a_lo @ w_a2 + b_a).astype(f32)) / GLA_TEMP
    shp_k = (bsz, nc, CHUNK, GLA_HEADS, GLA_DKH)
    q = q.astype(f32).reshape(shp_k) * GLA_DKH ** -0.5
    k = k.astype(f32).reshape(shp_k)
    v = v.astype(f32).reshape(bsz, nc, CHUNK, GLA_HEADS, GLA_DVH)
    cum = jnp.cumsum(log_a.reshape(shp_k), axis=2)
    tot = cum[:, :, -1]
    k_dec = k * jnp.exp(tot[:, :, None] - cum)
    upd = jnp.einsum('bnchk,bnchv->bnhkv', k_dec, v)
    decay = jnp.exp(tot)

    def chunk_step(state, xs):
        dec_c, upd_c = xs
        state = dec_c[..., None] * state + upd_c
        return state, state

    init = jnp.zeros((bsz, GLA_HEADS, GLA_DKH, GLA_DVH), f32)
    _, states = lax.scan(chunk_step, init, (jnp.moveaxis(decay, 1, 0), jnp.moveaxis(upd, 1, 0)))
    states = jnp.moveaxis(states, 0, 1)
    o = jnp.einsum('bnchk,bnhkv->bnchv', q, states).reshape(bsz, seq, GLA_HEADS, GLA_DVH)
    o = o * lax.rsqrt(jnp.mean(o * o, axis=-1, keepdims=True) + NORM_EPS) * norm_g.astype(f32).reshape(GLA_HEADS, GLA_DVH)
    o = o.reshape(bsz, seq, GLA_DV) * jax.nn.silu(g.astype(f32))
    return o.astype(h.dtype) @ w_o


def partial_rope(x, positions):
    f32 = jnp.float32
    half = ROT_DIMS // 2
    inv_freq = ROPE_THETA ** (-jnp.arange(half, dtype=f32) / half)
    ang = positions.astype(f32)[..., None] * inv_freq
    cos = jnp.cos(ang)[:, :, None]
    sin = jnp.sin(ang)[:, :, None]
    xf = x.astype(f32)
    x1 = xf[..., :half]
    x2 = xf[..., half:ROT_DIMS]
    return jnp.concatenate([x1 * cos - x2 * sin, x2 * cos + x1 * sin, xf[..., ROT_DIMS:]], axis=-1)


def diff_attn_mixer(h, positions, w_qkv, lam_q1, lam_k1, lam_q2, lam_k2, subln_g, w_o, lambda_init):
    bsz, seq, _ = h.shape
    f32 = jnp.float32
    q, k, v = jnp.split(h @ w_qkv, 3, axis=-1)
    q = partial_rope(q.reshape(bsz, seq, 2 * DIFF_HEADS, DIFF_HD), positions) * DIFF_HD ** -0.5
    k = partial_rope(k.reshape(bsz, seq, 2 * DIFF_HEADS, DIFF_HD), positions)
    v = v.astype(f32).reshape(bsz, seq, DIFF_HEADS, 2 * DIFF_HD)
    lam = (jnp.exp(jnp.sum(lam_q1.astype(f32) * lam_k1.astype(f32)))
           - jnp.exp(jnp.sum(lam_q2.astype(f32) * lam_k2.astype(f32))) + lambda_init)
    chunk_id = positions // CHUNK
    nqb = seq // Q_BLOCK
    q_blk = jnp.moveaxis(q.reshape(bsz, nqb, Q_BLOCK, 2 * DIFF_HEADS, DIFF_HD), 1, 0)
    cid_blk = jnp.moveaxis(chunk_id.reshape(bsz, nqb, Q_BLOCK), 1, 0)

    def attend_block(args):
        qb, qcid = args
        s = jnp.einsum('bqhd,bkhd->bhqk', qb, k)
        mask = chunk_id[:, None, None, :] <= qcid[:, None, :, None]
        p = jax.nn.softmax(jnp.where(mask, s, NEG_INF), axis=-1)
        p = p.reshape(bsz, DIFF_HEADS, 2, Q_BLOCK, seq)
        a = p[:, :, 0] - lam * p[:, :, 1]
        return jnp.einsum('bhqk,bkhd->bqhd', a, v)

    o = lax.map(attend_block, (q_blk, cid_blk))
    o = jnp.moveaxis(o, 0, 1).reshape(bsz, seq, DIFF_HEADS, 2 * DIFF_HD)
    o = o * lax.rsqrt(jnp.mean(o * o, axis=-1, keepdims=True) + 1e-5) * subln_g.astype(f32)
    o = o * (1.0 - lambda_init)
    return o.reshape(bsz, seq, D_MODEL).astype(h.dtype) @ w_o


def setup_inputs(seed: int = 0) -> dict:
    key = jax.random.key(seed)
    ks = iter(jax.random.split(key, 32))
    nrm = lambda shape, scale: scale * jax.random.normal(next(ks), shape, jnp.float32)
    x = jax.random.normal(next(ks), (BATCH, SEQ, D_MODEL), jnp.float32)
    positions = jnp.broadcast_to(jnp.arange(SEQ, dtype=jnp.int32), (BATCH, SEQ))
    norm_mix = 1.0 + nrm((DEPTH, D_MODEL), 0.02)
    norm_ffn = 1.0 + nrm((DEPTH, D_MODEL), 0.02)
    norm_final = 1.0 + nrm((D_MODEL,), 0.02)
    s5_lam_re = -0.5 * jnp.exp(nrm((N_S5_LAYERS, S5_GROUPS, S5_STATE), 0.05))
    s5_lam_im = jnp.pi * jnp.arange(S5_STATE, dtype=jnp.float32) + nrm((N_S5_LAYERS, S5_GROUPS, S5_STATE), 0.01)
    s5_log_dt = jax.random.uniform(next(ks), (N_S5_LAYERS, S5_GROUPS), jnp.float32, math.log(DT_MIN), math.log(DT_MAX))
    s5_b_re = nrm((N_S5_LAYERS, S5_GROUPS, S5_STATE, S5_GROUP), (2 * S5_GROUP) ** -0.5)
    s5_b_im = nrm((N_S5_LAYERS, S5_GROUPS, S5_STATE, S5_GROUP), (2 * S5_GROUP) ** -0.5)
    s5_c_re = nrm((N_S5_LAYERS, S5_GROUPS, S5_GROUP, S5_STATE), S5_STATE ** -0.5)
    s5_c_im = nrm((N_S5_LAYERS, S5_GROUPS, S5_GROUP, S5_STATE), S5_STATE ** -0.5)
    s5_d = nrm((N_S5_LAYERS, D_MODEL), 1.0)
    s5_w_glu = nrm((N_S5_LAYERS, D_MODEL, 2 * D_MODEL), D_MODEL ** -0.5)
    gla_w_in = nrm((N_GLA_LAYERS, D_MODEL, GLA_IN), D_MODEL ** -0.5)
    gla_w_a2 = nrm((N_GLA_LAYERS, GLA_GATE_RANK, GLA_DK), GLA_GATE_RANK ** -0.5)
    gla_b_a = nrm((N_GLA_LAYERS, GLA_DK), 0.1)
    gla_norm = 1.0 + nrm((N_GLA_LAYERS, GLA_DV), 0.02)
    gla_w_o = nrm((N_GLA_LAYERS, GLA_DV, D_MODEL), GLA_DV ** -0.5)
    diff_w_qkv = nrm((N_DIFF_LAYERS, D_MODEL, 3 * D_MODEL), D_MODEL ** -0.5)
    diff_lam_q1 = nrm((N_DIFF_LAYERS, DIFF_HD), 0.1)
    diff_lam_k1 = nrm((N_DIFF_LAYERS, DIFF_HD), 0.1)
    diff_lam_q2 = nrm((N_DIFF_LAYERS, DIFF_HD), 0.1)
    diff_lam_k2 = nrm((N_DIFF_LAYERS, DIFF_HD), 0.1)
    diff_subln = 1.0 + nrm((N_DIFF_LAYERS, 2 * DIFF_HD), 0.02)
    diff_w_o = nrm((N_DIFF_LAYERS, D_MODEL, D_MODEL), D_MODEL ** -0.5)
    ffn_w_gate_up = nrm((DEPTH, D_MODEL, 2 * FFN_HIDDEN), D_MODEL ** -0.5)
    ffn_w_down = nrm((DEPTH, FFN_HIDDEN, D_MODEL), FFN_HIDDEN ** -0.5)
    return {'x': x, 'positions': positions, 'norm_mix': norm_mix, 'norm_ffn': norm_ffn, 'norm_final': norm_final,
            's5_lam_re': s5_lam_re, 's5_lam_im': s5_lam_im, 's5_log_dt': s5_log_dt, 's5_b_re': s5_b_re, 's5_b_im': s5_b_im,
            's5_c_re': s5_c_re, 's5_c_im': s5_c_im, 's5_d': s5_d, 's5_w_glu': s5_w_glu,
            'gla_w_in': gla_w_in, 'gla_w_a2': gla_w_a2, 'gla_b_a': gla_b_a, 'gla_norm': gla_norm, 'gla_w_o': gla_w_o,
            'diff_w_qkv': diff_w_qkv, 'diff_lam_q1': diff_lam_q1, 'diff_lam_k1': diff_lam_k1, 'diff_lam_q2': diff_lam_q2,
            'diff_lam_k2': diff_lam_k2, 'diff_subln': diff_subln, 'diff_w_o': diff_w_o,
            'ffn_w_gate_up': ffn_w_gate_up, 'ffn_w_down': ffn_w_down}


def reference(x, positions, norm_mix, norm_ffn, norm_final,
              s5_lam_re, s5_lam_im, s5_log_dt, s5_b_re, s5_b_im, s5_c_re, s5_c_im, s5_d, s5_w_glu,
              gla_w_in, gla_w_a2, gla_b_a, gla_norm, gla_w_o,
              diff_w_qkv, diff_lam_q1, diff_lam_k1, diff_lam_q2, diff_lam_k2, diff_subln, diff_w_o,
              ffn_w_gate_up, ffn_w_down):
    for layer in range(DEPTH):
        kind = layer % N_MIXERS
        idx = layer // N_MIXERS
        h = rmsnorm(x, norm_mix[layer])
        if kind == 0:
            mix = s5_mixer(h, s5_lam_re[idx], s5_lam_im[idx], s5_log_dt[idx], s5_b_re[idx], s5_b_im[idx],
                           s5_c_re[idx], s5_c_im[idx], s5_d[idx], s5_w_glu[idx])
        elif kind == 1:
            mix = gla_mixer(h, gla_w_in[idx], gla_w_a2[idx], gla_b_a[idx], gla_norm[idx], gla_w_o[idx])
        else:
            lambda_init = 0.8 - 0.6 * math.exp(-0.3 * layer)
            mix = diff_attn_mixer(h, positions, diff_w_qkv[idx], diff_lam_q1[idx], diff_lam_k1[idx],
                                  diff_lam_q2[idx], diff_lam_k2[idx], diff_subln[idx], diff_w_o[idx], lambda_init)
        x = x + mix.astype(x.dtype)
        h = rmsnorm(x, norm_ffn[layer])
        x = x + swiglu_ffn(h, ffn_w_gate_up[layer], ffn_w_down[layer]).astype(x.dtype)
    return rmsnorm(x, norm_final)
```

```python
import numpy as np
from concourse.bass_utils import run_bass_kernel_spmd
import numpy as np
from contextlib import ExitStack
import concourse.bass as bass
import concourse.mybir as mybir

F32 = mybir.dt.float32
BF16 = mybir.dt.bfloat16
I32 = mybir.dt.int32
AF = mybir.ActivationFunctionType
ALU = mybir.AluOpType
AX = mybir.AxisListType

COMPUTE = ("pe", "act", "dve", "pool")


class Buf:
    def __init__(self, name, ap0, space):
        self.name = name
        self.ap0 = ap0
        self.space = space
        self.w = []
        self.r = []

    def __getitem__(self, idx):
        return V(self, self.ap0[idx])

    @property
    def v(self):
        return V(self, self.ap0)


class V:
    def __init__(self, buf, ap):
        self.buf = buf
        self.ap = ap

    def __getitem__(self, idx):
        return V(self.buf, self.ap[idx])

    def rr(self, pat, **kw):
        return V(self.buf, self.ap.rearrange(pat, **kw))


class Op:
    __slots__ = ("id", "eng", "fn", "deps", "is_dma", "is_cc", "inc", "semval", "sem", "raw_same", "prev")

    def __init__(self, id, eng, fn, is_dma=False, is_cc=False):
        self.id = id
        self.eng = eng
        self.fn = fn
        self.deps = set()
        self.raw_same = set()
        self.is_dma = is_dma
        self.is_cc = is_cc
        self.inc = False
        self.semval = None
        self.sem = None


class KB:
    def __init__(self):
        self.nc = bass.Bass("TRN2", target_bir_lowering=False)
        self.ops = []
        self.stack = ExitStack()
        self.nbuf = 0
        self.sb_bytes = 0
        self._bar_from = 0

    def dram(self, name, shape, dtype, kind="Internal"):
        if kind == "Internal":
            t = self.nc.dram_tensor(name, list(shape), dtype)
        else:
            t = self.nc.dram_tensor(name, list(shape), dtype, kind=kind)
        return Buf(name, t.ap(), "dram")

    def sb(self, name, shape, dtype):
        t = self.stack.enter_context(self.nc.sbuf_tensor(name, list(shape), dtype))
        n = 1
        for s in shape[1:]:
            n *= s
        self.sb_bytes += n * (4 if dtype in (F32, I32) else 2)
        return Buf(name, t[:], "sbuf")

    def ps(self, name, shape, dtype=F32):
        t = self.stack.enter_context(self.nc.psum_tensor(name, list(shape), dtype))
        return Buf(name, t[:], "psum")

    def rec(self, eng, fn, reads=(), writes=(), is_dma=False, is_cc=False):
        op = Op(len(self.ops), eng, fn, is_dma, is_cc)
        rb = []
        for x in reads:
            if x is None or isinstance(x, (int, float)):
                continue
            b = x.buf if isinstance(x, V) else x
            if b not in rb:
                rb.append(b)
        wb = []
        for x in writes:
            if x is None:
                continue
            b = x.buf if isinstance(x, V) else x
            if b not in wb:
                wb.append(b)
        for b in rb:
            for d in b.w:
                op.deps.add(d)
                op.raw_same.add(d)
            if b.space == "psum":
                for d in b.r:
                    if self.ops[d].eng != eng:
                        op.deps.add(d)
        for b in wb:
            for d in b.w:
                op.deps.add(d)
            for d in b.r:
                op.deps.add(d)
        for b in rb:
            if b not in wb:
                b.r.append(op.id)
        for b in wb:
            b.w = [op.id]
            b.r = []
        op.deps.discard(op.id)
        self.ops.append(op)
        return op

    def barrier(self):
        last = {}
        pend = []
        for op in self.ops[self._bar_from:]:
            if op.fn is None:
                continue
            if op.is_dma or op.is_cc:
                pend.append(op.id)
            else:
                last[op.eng] = op.id
        self._bar_from = len(self.ops)
        deps = set(pend) | set(last.values())
        for e in ("pe", "act", "dve", "pool", "sp"):
            op = Op(len(self.ops), e, None)
            op.deps = set(deps)
            op.raw_same = set(deps)
            self.ops.append(op)
        self._bar_from = len(self.ops) - 5

    def take(self, name, shape, dtype):
        n = 1
        for s in shape[1:]:
            n *= s
        words = n if dtype in (F32, I32) else (n + 1) // 2
        assert self.aoff + words <= self.awords, (name, self.aoff, words, self.awords)
        ap = self.arena.ap0[0:shape[0], self.aoff:self.aoff + words]
        self.aoff += words
        self.amax = max(self.amax, self.aoff)
        if dtype not in (F32,):
            ap = ap.bitcast(dtype)
        ap = ap[:, 0:n]
        if len(shape) == 3:
            ap = ap.rearrange("p (a b) -> p a b", a=shape[1])
        elif len(shape) == 4:
            ap = ap.rearrange("p (a b c) -> p a b c", a=shape[1], b=shape[2])
        elif len(shape) == 5:
            ap = ap.rearrange("p (a b c d) -> p a b c d", a=shape[1], b=shape[2], c=shape[3])
        return Buf(name, ap, "sbuf")

    def take_at(self, name, shape, dtype, off, pbase=0):
        n = 1
        for x in shape[1:]:
            n *= x
        words = n if dtype in (F32, I32) else (n + 1) // 2
        assert off + words <= self.awords, (name, off, words, self.awords)
        self.amax = max(self.amax, off + words)
        ap = self.arena.ap0[pbase:pbase + shape[0], off:off + words]
        if dtype not in (F32,):
            ap = ap.bitcast(dtype)
        ap = ap[:, 0:n]
        if len(shape) == 3:
            ap = ap.rearrange("p (a b) -> p a b", a=shape[1])
        elif len(shape) == 4:
            ap = ap.rearrange("p (a b c) -> p a b c", a=shape[1], b=shape[2])
        elif len(shape) == 5:
            ap = ap.rearrange("p (a b c d) -> p a b c d", a=shape[1], b=shape[2], c=shape[3])
        b = Buf(name, ap, "sbuf")
        b.words = words
        b.off = off
        return b

    def init_arena(self, words):
        self.arena = self.sb("arena", [128, words], F32)
        self.awords = words
        self.aoff = 0
        self.amax = 0

    @staticmethod
    def _a(x):
        return x.ap if isinstance(x, V) else x

    def mm(self, out, lhsT, rhs, start=True, stop=True):
        a = self._a
        return self.rec("pe", lambda e: e.matmul(a(out), a(lhsT), a(rhs), start=start, stop=stop),
                        reads=[lhsT, rhs], writes=[out])

    def transpose(self, out, in_, ident):
        a = self._a
        return self.rec("pe", lambda e: e.transpose(a(out), a(in_), a(ident)), reads=[in_, ident], writes=[out])

    def act(self, out, in_, func, bias=0.0, scale=1.0, accum=None, eng="act"):
        a = self._a
        kw = {}
        if accum is not None:
            kw["accum_out"] = a(accum)
        return self.rec(eng, lambda e: e.activation(a(out), a(in_), func, bias=a(bias), scale=a(scale), **kw),
                        reads=[in_, bias, scale], writes=[out, accum])

    def tt(self, out, in0, in1, op, eng="dve"):
        a = self._a
        return self.rec(eng, lambda e: e.tensor_tensor(a(out), a(in0), a(in1), op), reads=[in0, in1], writes=[out])

    def ts(self, out, in0, s1, s2=None, op0=ALU.mult, op1=None, accum=None, eng="dve"):
        a = self._a
        kw = {}
        if op1 is not None:
            kw["op1"] = op1
        if accum is not None:
            kw["accum_out"] = a(accum)
        return self.rec(eng, lambda e: e.tensor_scalar(a(out), a(in0), a(s1), a(s2) if s2 is not None else None, op0, **kw),
                        reads=[in0, s1, s2], writes=[out, accum])

    def stt(self, out, in0, scalar, in1, op0, op1, eng="dve"):
        a = self._a
        return self.rec(eng, lambda e: e.scalar_tensor_tensor(a(out), a(in0), a(scalar), a(in1), op0, op1),
                        reads=[in0, scalar, in1], writes=[out])

    def copy(self, out, in_, eng="dve"):
        a = self._a
        if eng == "act":
            return self.rec(eng, lambda e: e.copy(a(out), a(in_)), reads=[in_], writes=[out])
        return self.rec(eng, lambda e: e.tensor_copy(a(out), a(in_)), reads=[in_], writes=[out])

    def memset(self, out, val, eng="dve"):
        a = self._a
        return self.rec(eng, lambda e: e.memset(a(out), val), reads=[], writes=[out])

    def recip(self, out, in_):
        a = self._a
        return self.rec("dve", lambda e: e.reciprocal(a(out), a(in_)), reads=[in_], writes=[out])

    def scan(self, out, d0, d1, initial, op0, op1):
        a = self._a
        return self.rec("dve", lambda e: e.tensor_tensor_scan(a(out), a(d0), a(d1), a(initial), op0, op1),
                        reads=[d0, d1, initial], writes=[out])

    def dma(self, out, in_, q="sp", **kw):
        a = self._a
        return self.rec(q, lambda e: e.dma_start(out=a(out), in_=a(in_), **kw), reads=[in_], writes=[out], is_dma=True)

    def cc(self, kind, ins, outs, op=ALU.bypass, groups=None):
        a = self._a
        g = groups or [list(range(8))]
        return self.rec("pool", lambda e: e.collective_compute(kind, op, replica_groups=g,
                                                                  ins=[a(i) for i in ins], outs=[a(o) for o in outs]),
                        reads=list(ins), writes=list(outs), is_cc=True)

    def emit(self, final_wait_ops=()):
        nc = self.nc
        ops = self.ops
        engs = ["pe", "act", "dve", "pool", "sp"]
        for op in ops:
            for d in op.deps:
                dop = ops[d]
                if dop.eng != op.eng:
                    dop.inc = True
                elif dop.is_dma or dop.is_cc:
                    dop.inc = True
                elif op.eng in ("act", "dve", "pool") and not op.is_dma:
                    dop.inc = True
        for d in final_wait_ops:
            d.inc = True
        NDS = {"sp": 24, "pool": 12, "act": 8, "pe": 1, "dve": 1}
        st = self.stack
        csem = {e: st.enter_context(nc.semaphore("c_" + e)) for e in COMPUTE}
        dsem = {e: [st.enter_context(nc.semaphore("d_%s%d" % (e, i))) for i in range(NDS[e])] for e in engs}
        ccsem = st.enter_context(nc.semaphore("ccsem"))
        ccnt = {e: 0 for e in COMPUTE}
        dcnt = {e: [0] * NDS[e] for e in engs}
        drr = {e: 0 for e in engs}
        cccnt = 0
        for op in ops:
            if op.is_dma:
                i = drr[op.eng] % NDS[op.eng]
                drr[op.eng] += 1
                op.sem = ("d", op.eng, i)
                op.prev = dcnt[op.eng][i]
                dcnt[op.eng][i] += 16
                op.semval = dcnt[op.eng][i]
            elif op.is_cc:
                cccnt += 1
                op.sem = ("cc",)
                op.semval = cccnt
            elif op.inc:
                ccnt[op.eng] += 1
                op.sem = ("c", op.eng)
                op.semval = ccnt[op.eng]

        def semh(s):
            if s[0] == "d":
                return dsem[s[1]][s[2]]
            if s[0] == "cc":
                return ccsem
            return csem[s[1]]

        by_eng = {e: [op for op in ops if op.eng == e] for e in engs}
        self.stats = {e: len(by_eng[e]) for e in engs}
        nwaits = {e: 0 for e in engs}

        def run(eng_name, e):
            seen = {}
            for op in by_eng[eng_name]:
                need = {}
                for d in op.deps:
                    dop = ops[d]
                    if dop.sem is None:
                        continue
                    if dop.eng == eng_name and not (dop.is_dma or dop.is_cc):
                        if not (eng_name in ("act", "dve", "pool") and not op.is_dma):
                            continue
                    if need.get(dop.sem, 0) < dop.semval:
                        need[dop.sem] = dop.semval
                if op.is_dma and op.prev > 0:
                    if need.get(op.sem, 0) < op.prev:
                        need[op.sem] = op.prev
                for s, v in need.items():
                    if seen.get(s, 0) >= v:
                        continue
                    e.wait_ge(semh(s), v)
                    nwaits[eng_name] += 1
                    seen[s] = v
                if op.fn is None:
                    continue
                ins = op.fn(e)
                if op.is_dma:
                    ins.then_inc(semh(op.sem), 16)
                elif op.is_cc:
                    ins.then_inc(semh(op.sem), 1)
                elif op.inc:
                    ins.then_inc(semh(op.sem), 1)
            if eng_name == "sp":
                for d in final_wait_ops:
                    e.wait_ge(semh(d.sem), d.semval)

        with nc.Block() as block:
            @block.tensor
            def _(e):
                run("pe", e)

            @block.scalar
            def _(e):
                run("act", e)

            @block.vector
            def _(e):
                run("dve", e)

            @block.gpsimd
            def _(e):
                run("pool", e)

            @block.sync
            def _(e):
                run("sp", e)
        self.stats["waits"] = nwaits
        self.stack.close()
        return nc
D = 1024
SEQ = 16384
NCORE = 8
T = SEQ // NCORE
NT = T // 512
KT = D // 128
FH = 2816
FM = FH // 128
EPS = 1e-6


class StopBuild(Exception):
    pass


class Prog:
    def __init__(self, cfg):
        self.cfg = cfg
        self.k = KB()
        self.ins = {}
        self.outs = []
        self.out_ops = []
        self.dbg_ops = []
        k = self.k
        k.init_arena(52000)
        if cfg.get("kind") == "attncore":
            self.PS2 = [k.ps("pss%d" % i, [128, 1024], F32) for i in range(2)]
            self.PS = [None] * 4 + [k.ps("ps%d" % i, [128, 512], F32) for i in range(4, 8)]
        else:
            self.PS = [k.ps("ps%d" % i, [128, 512], F32) for i in range(8)]

    def din(self, name, shape, dtype=F32):
        b = self.k.dram(name, shape, dtype, kind="ExternalInput")
        self.ins[name] = (tuple(shape), dtype)
        return b

    def dout(self, name, shape, dtype=F32):
        b = self.k.dram(name, shape, dtype, kind="ExternalOutput")
        self.outs.append(name)
        return b

    def checkpoint(self, name, dump=None):
        if self.cfg.get("stop") != name:
            return
        k = self.k
        k.barrier()
        if dump is not None:
            n = dump.ap.shape[1]
            np_ = dump.ap.shape[0]
            dbg = self.dout("dbg", [128, 4096], F32)
            tmp = k.take_at("dbgtmp", [128, 4096], F32, k.awords - 4096)
            k.memset(tmp.v, 0.0)
            k.copy(tmp[0:np_, 0:n], dump)
            self.dbg_ops.append(k.dma(dbg.v, tmp.v))
        raise StopBuild()

    def finish(self):
        self.nc = self.k.emit(final_wait_ops=self.out_ops + self.dbg_ops)
        return self


class Tok:
    def __init__(self, P):
        self.P = P
        k = P.k
        self.k = k
        xin = P.din("xT", [KT, 128, T])
        gains = P.din("gains", [128, 9, KT])
        self.x = k.take("x", [128, KT, T], F32)
        self.gn = k.take("gn", [128, 9, KT], F32)
        self.ones = k.take("ones", [128, 128], BF16)
        k.dma(self.gn.v, gains.v)
        for kt in range(KT):
            k.dma(self.x[:, kt, :], xin[kt], q="sp" if kt % 2 == 0 else "act")
        k.memset(self.ones.v, 1.0)

    def store_x(self, name="outT"):
        out = self.P.dout(name, [KT, 128, T], F32)
        for kt in range(KT):
            self.P.out_ops.append(self.k.dma(out[kt], self.x[:, kt, :], q="sp" if kt % 2 == 0 else "act"))

    def rstd_tile(self, n, tag):
        k, x, PS = self.k, self.x, self.P.PS
        if not hasattr(self, "_nrm_" + tag):
            setattr(self, "_nrm_" + tag, ([k.take(tag + "sq%d" % i, [128, 512], BF16) for i in range(2)],
                                         [k.take(tag + "rstd%d" % i, [128, 512], F32) for i in range(2)]))
        sq, rstd = getattr(self, "_nrm_" + tag)
        ts = slice(n * 512, (n + 1) * 512)
        ps = PS[n % 2]
        for kt in range(KT):
            s = sq[kt % 2]
            k.act(s.v, x[:, kt, ts], AF.Square)
            k.mm(ps.v, self.ones.v, s.v, start=(kt == 0), stop=(kt == KT - 1))
        r = rstd[n % 2]
        k.ts(r.v, ps.v, 1.0 / D, EPS, op0=ALU.mult, op1=ALU.add)
        k.act(r.v, r.v, AF.Sqrt)
        k.recip(r.v, r.v)
        return r

    def rmsnorm(self, which, h, tag):
        k = self.k
        for n in range(NT):
            ts = slice(n * 512, (n + 1) * 512)
            r = self.rstd_tile(n, tag)
            for kt in range(KT):
                k.stt(h[:, kt, ts], self.x[:, kt, ts], self.gn[:, which, kt:kt + 1], r.v, ALU.mult, ALU.mult)

    def norm_out(self, which, name):
        k = self.k
        mark = k.aoff
        h = k.take("h_" + name, [128, KT, T], BF16)
        self.rmsnorm(which, h, "no" + name)
        out = self.P.dout(name, [KT, 128, T], BF16)
        for kt in range(KT):
            self.P.out_ops.append(k.dma(out[kt], h[:, kt, :], q="sp" if kt % 2 == 0 else "act"))
        k.barrier()
        k.aoff = mark

    def final_norm(self):
        k = self.k
        mark = k.aoff
        for n in range(NT):
            ts = slice(n * 512, (n + 1) * 512)
            r = self.rstd_tile(n, "fin")
            for kt in range(KT):
                k.stt(self.x[:, kt, ts], self.x[:, kt, ts], self.gn[:, 8, kt:kt + 1], r.v, ALU.mult, ALU.mult)
        k.barrier()
        k.aoff = mark

    def proj_gated(self, name, src, nk, nmo, combine, w2=None):
        k, PS = self.k, self.P.PS
        wd = self.P.din(name, [nmo, 128, nk * 256])
        if w2 is None:
            w2 = [k.take(name + "w%d" % i, [128, nk, 2, 128], BF16) for i in range(2)]
        cnt = 0
        for mo in range(nmo):
            w = w2[mo % 2]
            k.dma(w.v.rr("p a b c -> p (a b c)"), wd[mo], q="pool")
            for n in range(NT):
                ts = slice(n * 512, (n + 1) * 512)
                pa = PS[2 + 2 * (cnt % 2)]
                pb = PS[3 + 2 * (cnt % 2)]
                for kt in range(nk):
                    k.mm(pa.v, w[:, kt, 0, :], src[:, kt, ts], start=(kt == 0), stop=(kt == nk - 1))
                for kt in range(nk):
                    k.mm(pb.v, w[:, kt, 1, :], src[:, kt, ts], start=(kt == 0), stop=(kt == nk - 1))
                combine(mo, n, ts, pa, pb, cnt)
                cnt += 1

    def proj_acc(self, name, src, nk, nmo, sink, w2=None):
        k, PS = self.k, self.P.PS
        wd = self.P.din(name, [nmo, 128, nk * 128])
        if w2 is None:
            w2 = [k.take(name + "w%d" % i, [128, nk, 128], BF16) for i in range(2)]
        cnt = 0
        for mo in range(nmo):
            w = w2[mo % 2]
            k.dma(w.v.rr("p a b -> p (a b)"), wd[mo], q="pool")
            for n in range(NT):
                ts = slice(n * 512, (n + 1) * 512)
                po = PS[6 + (cnt % 2)]
                for kt in range(nk):
                    k.mm(po.v, w[:, kt, :], src[:, kt, ts], start=(kt == 0), stop=(kt == nk - 1))
                sink(mo, n, ts, po, cnt)
                cnt += 1

    def add_to_x(self, mo, n, ts, po, cnt):
        self.k.tt(self.x[:, mo, ts], self.x[:, mo, ts], po.v, ALU.add)

    def ffn(self, l):
        k = self.k
        mark = k.aoff
        HG = FM // 2
        h = k.take("h", [128, KT, T], BF16)
        a = k.take("a", [128, HG, T], BF16)
        sg = [k.take("sg%d" % i, [128, 512], F32) for i in range(2)]
        self.rmsnorm(4 + l, h, "ffn%d" % l)
        wg2 = [k.take("wg2_%d" % i, [128, KT, 2, 128], BF16) for i in range(2)]
        wd2 = [k.take("wd2_%d" % i, [128, HG, 128], BF16) for i in range(2)]
        for grp in range(2):
            def comb(mi, n, ts, pg, pu, cnt):
                s = sg[cnt % 2]
                k.act(s.v, pg.v, AF.Silu)
                k.tt(a[:, mi, ts], s.v, pu.v, ALU.mult)
            self.proj_gated("wgu%d_%d" % (l, grp), h, KT, HG, comb, wg2)
            self.proj_acc("wdn%d_%d" % (l, grp), a, HG, KT, self.add_to_x, wd2)
        k.barrier()
        k.aoff = mark

    def s5post(self, idx):
        k = self.k
        mark = k.aoff
        zin = self.P.din("zf", [KT, 128, T], BF16)
        zf = k.take("zf", [128, KT, T], BF16)
        for kt in range(KT):
            k.dma(zf[:, kt, :], zin[kt], q="sp" if kt % 2 == 0 else "act")
        sg = [k.take("sgl%d" % i, [128, 512], F32) for i in range(2)]

        def comb(mo, n, ts, pa, pb, cnt):
            s = sg[cnt % 2]
            k.act(s.v, pb.v, AF.Sigmoid)
            k.tt(s.v, s.v, pa.v, ALU.mult)
            k.tt(self.x[:, mo, ts], self.x[:, mo, ts], s.v, ALU.add)
        self.proj_gated("s5_%d_wglu" % idx, zf, KT, KT, comb)
        k.barrier()
        k.aoff = mark


STEPS = {}


def build(cfg):
    P = Prog(cfg)
    try:
        if cfg["kind"] == "tok":
            tk = Tok(P)
            for st in cfg["steps"]:
                nm, _, arg = st.partition(":")
                if nm == "norm_out":
                    which, name = arg.split(",")
                    tk.norm_out(int(which), name)
                elif nm == "ffn":
                    tk.ffn(int(arg))
                elif nm == "s5post":
                    tk.s5post(int(arg))
                elif nm == "final":
                    tk.final_norm()
                elif nm == "store_x":
                    tk.store_x(arg or "outT")
                else:
                    STEPS[nm](tk, arg)
        else:
            STEPS[cfg["kind"]](P, cfg)
    except StopBuild:
        pass
    return P.finish()


def _pair_tiles(w, nk, nmo):
    w = w.reshape(nk, 128, 2, nmo, 128).transpose(3, 1, 0, 2, 4)
    return np.ascontiguousarray(w).reshape(nmo, 128, nk * 256)


def _acc_tiles(w, nk, nmo):
    w = w.reshape(nk, 128, nmo, 128).transpose(2, 1, 0, 3)
    return np.ascontiguousarray(w).reshape(nmo, 128, nk * 128)


class Host:
    def __init__(self, inp):
        f = np.float32
        self.inp = inp
        g = np.stack([np.asarray(inp["norm_mix"], f)[i] for i in range(4)]
                     + [np.asarray(inp["norm_ffn"], f)[i] for i in range(4)]
                     + [np.asarray(inp["norm_final"], f)], 0)
        self.common = {"gains": np.ascontiguousarray(g.reshape(9, KT, 128).transpose(2, 0, 1))}
        self.percore = [dict() for _ in range(NCORE)]
        X = np.asarray(inp["x"], f)[0]
        for c in range(NCORE):
            self.percore[c]["xT"] = np.ascontiguousarray(X[c * T:(c + 1) * T].T).reshape(KT, 128, T)
        for fn in HOST_EXTRA:
            fn(self)

    def weight(self, name):
        f = np.float32
        inp = self.inp
        if name.startswith("wgu"):
            l, grp = int(name[3]), int(name[5])
            w = np.asarray(inp["ffn_w_gate_up"], f)[l]
            HG = FM // 2
            cols = np.concatenate([np.arange(grp * HG * 128, (grp + 1) * HG * 128),
                                   FH + np.arange(grp * HG * 128, (grp + 1) * HG * 128)])
            return _pair_tiles(w[:, cols], KT, HG)
        if name.startswith("wdn"):
            l, grp = int(name[3]), int(name[5])
            w = np.asarray(inp["ffn_w_down"], f)[l]
            HG = FM // 2
            return _acc_tiles(w[grp * HG * 128:(grp + 1) * HG * 128], HG, KT)
        if name.startswith("s5_") and name.endswith("wglu"):
            idx = int(name[3])
            return _pair_tiles(np.asarray(inp["s5_w_glu"], f)[idx], KT, KT)
        for pre, fn in WEIGHT_EXTRA.items():
            if name.startswith(pre):
                return fn(self, name)
        raise KeyError(name)

    def set(self, name, per_core_list):
        for c in range(NCORE):
            self.percore[c][name] = per_core_list[c]

    def get(self, name, c):
        if name in self.percore[c]:
            return self.percore[c][name]
        if name not in self.common:
            self.common[name] = self.weight(name)
        return self.common[name]


_PROGS = {}


def launch(host, cfg):
    key = repr(sorted(cfg.items(), key=str))
    if key not in _PROGS:
        _PROGS[key] = build(cfg)
    P = _PROGS[key]
    maps = [{n: host.get(n, c) for n in P.ins} for c in range(NCORE)]
    res = run_bass_kernel_spmd(P.nc, maps, core_ids=list(range(NCORE)))
    return [res.results[c] for c in range(NCORE)]


def tok_to_chan(outs, name):
    return [np.ascontiguousarray(np.concatenate([np.asarray(outs[r][name])[c] for r in range(NCORE)], axis=1))
            for c in range(NCORE)]


def chan_to_tok(outs, name):
    return [np.ascontiguousarray(np.stack([np.asarray(outs[c][name])[:, r * T:(r + 1) * T] for c in range(NCORE)], 0))
            for r in range(NCORE)]


HOST_EXTRA = []
WEIGHT_EXTRA = {}


import math as _math
ATT_LAYER = 2
LAMBDA_INIT = 0.8 - 0.6 * _math.exp(-0.3 * ATT_LAYER)
ROPE_THETA_ = 500000.0


def _sin_reduced(k, dst, src, tmpi, tmpf, scr):
    k.ts(tmpi, src, 1.0 / TWO_PI, None, op0=ALU.mult)
    k.copy(scr, tmpi)
    k.stt(scr, scr, -TWO_PI, src, ALU.mult, ALU.add)
    k.ts(tmpf, scr, PI, TWO_PI, op0=ALU.is_gt, op1=ALU.mult)
    k.tt(scr, scr, tmpf, ALU.subtract)
    k.ts(tmpf, scr, -PI, -TWO_PI, op0=ALU.is_lt, op1=ALU.mult)
    k.tt(scr, scr, tmpf, ALU.subtract)
    k.act(dst, scr, AF.Sin)


def attn_pre(tk, arg):
    P, k, PS = tk.P, tk.k, tk.P.PS
    mark = k.aoff
    h = k.take("h", [128, KT, T], BF16)
    tk.rmsnorm(2, h, "att")
    o_q = P.dout("aqT", [8, 128, T], BF16)
    o_k = P.dout("akT", [8, 128, T], BF16)
    o_v = P.dout("avtm", [16, 128, 1024], BF16)
    d_pos = P.din("pos_rep", [128, T], I32)
    d_invf = P.din("rope_invf", [128, 1])
    d_perm = P.din("rope_perm", [128, 128])
    invf = k.take("invf", [128, 1], F32)
    perm = k.take("perm", [128, 128], BF16)
    k.dma(invf.v, d_invf.v)
    k.dma(perm.v, d_perm.v, q="pool")
    COS = k.take("COS", [128, T], F32)
    SIN = k.take("SIN", [128, T], F32)
    COSq = k.take("COSq", [128, T], F32)
    SINq = k.take("SINq", [128, T], F32)
    m2 = k.aoff
    posi = k.take("posi", [128, T], I32)
    k.dma(posi.v, d_pos.v)
    ang = k.take("ang", [128, T], F32)
    tmpf = k.take("rtmp", [128, T], F32)
    scr = k.take("rscr", [128, T], F32)
    tmpi = V(tmpf, tmpf.ap0.bitcast(I32))
    k.copy(ang.v, posi.v)
    k.ts(ang.v, ang.v, invf[:, 0:1], None, op0=ALU.mult)
    _sin_reduced(k, SIN.v, ang.v, tmpi, tmpf.v, scr.v)
    k.ts(ang.v, ang.v, 0.5 * PI, None, op0=ALU.add)
    _sin_reduced(k, COS.v, ang.v, tmpi, tmpf.v, scr.v)
    k.ts(COSq.v, COS.v, 0.125, None, op0=ALU.mult)
    k.ts(SINq.v, SIN.v, 0.125, None, op0=ALU.mult)
    k.barrier()
    k.aoff = m2
    P.checkpoint("a_tab", COS.v)
    stq = [k.take("astq%d" % i, [128, T], BF16) for i in range(2)]
    qb = [k.take("aqb%d" % i, [128, 512], BF16) for i in range(2)]
    t1 = [k.take("at1%d" % i, [128, 512], F32) for i in range(2)]
    t2 = [k.take("at2%d" % i, [128, 512], F32) for i in range(2)]

    def sink_qk(mo, n, ts, po, cnt):
        st = stq[mo % 2]
        b = qb[cnt % 2]
        k.copy(b.v, po.v, eng="act")
        pp = PS[cnt % 2]
        k.mm(pp.v, perm.v, b.v)
        c_, s_ = (COSq, SINq) if mo < 8 else (COS, SIN)
        a1, a2 = t1[cnt % 2], t2[cnt % 2]
        k.tt(a1.v, b.v, c_[:, ts], ALU.mult)
        k.tt(a2.v, pp.v, s_[:, ts], ALU.mult)
        k.tt(st[:, ts], a1.v, a2.v, ALU.add, eng="pool")
        if n == NT - 1:
            dst = o_q[mo] if mo < 8 else o_k[mo - 8]
            P.out_ops.append(k.dma(dst, st.v, q="sp"))
    tk.proj_acc("att_wqk", h, KT, 16, sink_qk)
    P.checkpoint("a_qk", stq[1].v)
    d_wv = P.din("att_wv", [2, 128, KT * 512])
    wv = [k.take("awv%d" % i, [128, KT, 512], BF16) for i in range(2)]
    tst = [k.take("atst%d" % i, [128, 512], BF16) for i in range(4)]
    cnt = 0
    for ci in range(2):
        w = wv[ci]
        k.dma(w.v.rr("p a b -> p (a b)"), d_wv[ci], q="pool")
        for tt in range(16):
            ps = PS[2 + cnt % 2]
            for kt in range(KT):
                k.mm(ps.v, h[:, kt, tt * 128:(tt + 1) * 128], w[:, kt, :], start=(kt == 0), stop=(kt == KT - 1))
            st = tst[cnt % 4]
            k.copy(st.v, ps.v, eng="act" if cnt % 2 == 0 else "dve")
            P.out_ops.append(k.dma(o_v[tt][:, ci * 512:(ci + 1) * 512], st.v, q="sp" if cnt % 2 == 0 else "act"))
            cnt += 1
    k.barrier()
    k.aoff = mark


STEPS["attnpre"] = attn_pre


def attn_core(P, cfg):
    k, PS = P.k, P.PS
    d_q = P.din("a_qT", [128, SEQ], BF16)
    d_k = P.din("a_kT", [128, SEQ], BF16)
    d_v = P.din("a_v", [128, 128 * 128], BF16)
    d_lam = P.din("a_lamv", [128, 4, 64])
    d_sg = P.din("a_subg", [128, 1])
    d_pq = P.din("a_posq", [128, 512], I32)
    d_pk = P.din("a_posk", [128, 4], I32)
    d_o = P.dout("aoT", [128, SEQ], BF16)
    Q = k.take("Q", [128, SEQ], BF16)
    K_ = k.take("K", [128, SEQ], BF16)
    Vv = k.take("V", [128, 128, 128], BF16)
    oT = k.take("oT", [128, 2, 512], BF16)
    ones = k.take("ones", [128, 128], BF16)
    k.memset(ones.v, 1.0)
    for i in range(4):
        sl = slice(i * 4096, (i + 1) * 4096)
        k.dma(Q[:, sl], d_q[:, sl], q="sp")
        k.dma(K_[:, sl], d_k[:, sl], q="act")
        k.dma(Vv.v.rr("p a b -> p (a b)")[:, sl], d_v[:, sl], q="sp")
    lv = k.take("lv", [128, 4, 64], F32)
    sgl = k.take("sgl", [128, 1], F32)
    k.dma(lv.v, d_lam.v)
    k.dma(sgl.v, d_sg.v)
    lp = k.take("lp", [128, 2, 64], F32)
    ls = k.take("ls", [128, 2], F32)
    lam = k.take("lam", [128, 1], F32)
    k.tt(lp[:, 0, :], lv[:, 0, :], lv[:, 1, :], ALU.mult)
    k.tt(lp[:, 1, :], lv[:, 2, :], lv[:, 3, :], ALU.mult)
    k.ts(lp[:, 0, :], lp[:, 0, :], 1.0, 0.0, op0=ALU.mult, op1=ALU.add, accum=ls[:, 0:1])
    k.ts(lp[:, 1, :], lp[:, 1, :], 1.0, 0.0, op0=ALU.mult, op1=ALU.add, accum=ls[:, 1:2])
    k.act(ls.v, ls.v, AF.Exp)
    k.tt(lam.v, ls[:, 0:1], ls[:, 1:2], ALU.subtract)
    k.ts(lam.v, lam.v, LAMBDA_INIT, None, op0=ALU.add)
    k.ts(sgl.v, sgl.v, 1.0 - LAMBDA_INIT, None, op0=ALU.mult)
    pq = k.take("pq", [128, 512], I32)
    pk = k.take("pk", [128, 4], I32)
    k.dma(pq.v, d_pq.v)
    k.dma(pk.v, d_pk.v)
    k.ts(pq.v, pq.v, 6, None, op0=ALU.arith_shift_right)
    k.ts(pk.v, pk.v, 6, None, op0=ALU.arith_shift_right)
    cq = k.take("cq", [128, 512], F32)
    ck = k.take("ck", [128, 4], F32)
    k.copy(cq.v, pq.v)
    k.copy(ck.v, pk.v)
    M = [k.take("M%d" % t, [128, 512], BF16) for t in range(4)]
    for t in range(4):
        k.ts(M[t].v, cq.v, ck[:, t:t + 1], None, op0=ALU.is_ge)
    E = [k.take("E%d" % i, [128, 2, 512], BF16) for i in range(3)]
    M2 = [k.take("M2_%d" % t, [128, 2, 512], BF16) for t in range(4)]
    for t in range(4):
        k.copy(M2[t][:, 0, :], M[t].v)
        k.copy(M2[t][:, 1, :], M[t].v)
    rinv = [[k.take("rinv%d_%d" % (i, s_), [128, 512], F32) for s_ in range(2)] for i in range(2)]
    Os = [[k.take("Os%d_%d" % (i, s_), [128, 512], F32) for s_ in range(2)] for i in range(2)]
    ofs = [k.take("of%d" % i, [128, 512], F32) for i in range(2)]
    sqb = [k.take("sqb%d" % i, [128, 512], BF16) for i in range(2)]
    pending = {}
    Eacc = [k.take("Eacc%d" % i, [128, 512], F32) for i in range(2)]
    ones32 = k.take("ones32", [128, 128], F32)
    k.memset(ones32.v, 1.0)
    O = [PS[4], PS[5]]
    R = [PS[6], PS[7]]
    steps = [(qi, kj) for qi in range(SEQ // 512) for kj in range(4 * qi + 4)]

    def emit_qk(si):
        qi, kj = steps[si]
        qs = slice(qi * 512, (qi + 1) * 512)
        ks = slice(kj * 128, (kj + 1) * 128)
        for s in range(2):
            rows = slice(64 * s, 64 * s + 64)
            k.mm(P.PS2[si % 2][:, s * 512:(s + 1) * 512], K_[rows, ks], Q[rows, qs])

    emit_qk(0)
    for si, (qi, kj) in enumerate(steps):
        qs = slice(qi * 512, (qi + 1) * 512)
        nk = 4 * qi + 4
        if si + 1 < len(steps):
            emit_qk(si + 1)
        Ex = E[si % 3]
        k.act(Ex.v.rr("p a b -> p (a b)"), P.PS2[si % 2].v, AF.Exp)
        if kj >= 4 * qi:
            k.tt(Ex.v, Ex.v, M2[kj - 4 * qi].v, ALU.mult)
        for s in range(2):
            k.mm(O[s].v, Vv[:, kj, :], Ex[:, s, :], start=(kj == 0), stop=(kj == nk - 1))
            if s == 0:
                if kj == 0:
                    k.copy(Eacc[0].v, Ex[:, 0, :])
                else:
                    k.tt(Eacc[0].v, Eacc[0].v, Ex[:, 0, :], ALU.add)
            else:
                k.mm(R[1].v, ones.v, Ex[:, 1, :], start=(kj == 0), stop=(kj == nk - 1))
        for fn in pending.pop(si, []):
            fn()
        if kj != nk - 1:
            continue
        par = qi % 2
        Osb = [Os[par][0], Os[par][1]]
        rv = [rinv[par][0], rinv[par][1]]
        k.mm(R[0].v, ones32.v, Eacc[0].v)
        k.copy(Osb[0].v, O[0].v, eng="act")
        k.copy(Osb[1].v, O[1].v, eng="act")
        k.act(rv[1].v, R[1].v, AF.Ln)
        k.act(rv[0].v, R[0].v, AF.Ln)
        k.act(rv[1].v, rv[1].v, AF.Exp, scale=-1.0)
        k.act(rv[0].v, rv[0].v, AF.Exp, scale=-1.0)

        def stage_b(qi=qi, par=par, Osb=Osb, rv=rv):
            of = ofs[par]
            k.tt(of.v, Osb[0].v, rv[0].v, ALU.mult)
            k.stt(Osb[1].v, Osb[1].v, lam[:, 0:1], rv[1].v, ALU.mult, ALU.mult)
            k.tt(of.v, of.v, Osb[1].v, ALU.subtract)
            k.act(sqb[par].v, of.v, AF.Square)
            k.mm(R[0].v, ones.v, sqb[par].v)
            k.ts(rv[0].v, R[0].v, 1.0 / 128.0, 1e-5, op0=ALU.mult, op1=ALU.add)

        def stage_c(qi=qi, par=par, rv=rv):
            of = ofs[par]
            qs_ = slice(qi * 512, (qi + 1) * 512)
            k.act(rv[0].v, rv[0].v, AF.Ln)
            k.act(rv[0].v, rv[0].v, AF.Exp, scale=-0.5)
            ob = oT[:, par, :]
            k.stt(ob, of.v, sgl[:, 0:1], rv[0].v, ALU.mult, ALU.mult)
            P.out_ops.append(k.dma(d_o[:, qs_], ob, q="sp" if par == 0 else "act"))
        if si + 4 < len(steps):
            pending.setdefault(si + 2, []).append(stage_b)
            pending.setdefault(si + 4, []).append(stage_c)
        else:
            stage_b()
            stage_c()


STEPS["attncore"] = attn_core


def attn_post(tk, arg):
    P, k = tk.P, tk.k
    mark = k.aoff
    d_o = P.din("aoTt", [8, 128, T], BF16)
    o = k.take("ao", [128, 8, T], BF16)
    for kt in range(8):
        k.dma(o[:, kt, :], d_o[kt], q="sp" if kt % 2 == 0 else "act")
    tk.proj_acc("att_wo", o, KT, KT, tk.add_to_x)
    k.barrier()
    k.aoff = mark


STEPS["attnpost"] = attn_post


def host_attn(host):
    f = np.float32
    inp = host.inp
    w = np.asarray(inp["diff_w_qkv"], f)[0]
    host.common["att_wqk"] = _acc_tiles(w[:, 0:2048], KT, 16)
    host.common["att_wv"] = np.ascontiguousarray(w[:, 2048:3072].reshape(KT, 128, 2, 512).transpose(2, 1, 0, 3)).reshape(2, 128, KT * 512)
    host.common["att_wo"] = _acc_tiles(np.asarray(inp["diff_w_o"], f)[0], KT, KT)
    d = np.arange(128) % 64
    half = 8
    invf = np.where(d < 16, ROPE_THETA_ ** (-(d % half).astype(np.float64) / half), 0.0).astype(f)
    host.common["rope_invf"] = np.ascontiguousarray(invf[:, None])
    perm = np.zeros((128, 128), f)
    for p in range(128):
        dd = p % 64
        if dd < 8:
            perm[p + 8, p] = -1.0
        elif dd < 16:
            perm[p - 8, p] = 1.0
    host.common["rope_perm"] = perm
    pos = np.asarray(inp["positions"])[0].astype(np.int32)
    for c in range(NCORE):
        host.percore[c]["pos_rep"] = np.ascontiguousarray(np.broadcast_to(pos[None, c * T:(c + 1) * T], (128, T)))
        lamv = np.stack([np.asarray(inp[n], f)[0] for n in ("diff_lam_q1", "diff_lam_k1", "diff_lam_q2", "diff_lam_k2")], 0)
        host.percore[c]["a_lamv"] = np.ascontiguousarray(np.broadcast_to(lamv[None], (128, 4, 64)))
        host.percore[c]["a_subg"] = np.ascontiguousarray(np.asarray(inp["diff_subln"], f)[0][:, None])
        host.percore[c]["a_posq"] = np.ascontiguousarray(np.broadcast_to(pos[None, 0:512], (128, 512)))
        host.percore[c]["a_posk"] = np.ascontiguousarray(pos[0:512].reshape(4, 128).T)


HOST_EXTRA.append(host_attn)


GLA_H = 4
GLA_DKH = 128
GLA_DVH = 256


def gla_pre(tk, arg):
    P, k, PS = tk.P, tk.k, tk.P.PS
    mark = k.aoff
    h = k.take("h", [128, KT, T], BF16)
    tk.rmsnorm(1, h, "gla")
    o_q = P.dout("qT", [4, 128, T], BF16)
    o_g = P.dout("gT", [8, 128, T], BF16)
    o_k = P.dout("ktm", [16, 128, 512], BF16)
    o_v = P.dout("vtm", [16, 128, 1024], BF16)
    o_la = P.dout("latm", [16, 128, 512], BF16)
    stq = [k.take("stq%d" % i, [128, T], BF16) for i in range(2)]

    def sink_qg(mo, n, ts, po, cnt):
        st = stq[mo % 2]
        k.copy(st[:, ts], po.v, eng="act" if cnt % 2 == 0 else "dve")
        if n == NT - 1:
            dst = o_q[mo] if mo < 4 else o_g[mo - 4]
            P.out_ops.append(k.dma(dst, st.v, q="sp"))
    tk.proj_acc("gla_wqg", h, KT, 12, sink_qg)

    d_wa = P.din("gla_walo", [128, KT * 16])
    wa = k.take("wa", [128, KT, 16], BF16)
    k.dma(wa.v.rr("p a b -> p (a b)"), d_wa.v, q="pool")
    alo = k.take("alo", [16, T], BF16)
    for n in range(NT):
        ts = slice(n * 512, (n + 1) * 512)
        ps = PS[n % 2]
        for kt in range(KT):
            k.mm(ps[0:16, :], wa[:, kt, :], h[:, kt, ts], start=(kt == 0), stop=(kt == KT - 1))
        k.copy(alo[:, ts], ps[0:16, :])

    d_wkv = P.din("gla_wkv", [3, 128, KT * 512])
    d_wa2 = P.din("gla_wa2", [16, 512])
    d_ba = P.din("gla_ba", [1, 512])
    wa2 = k.take("wa2", [16, 512], BF16)
    ba = k.take("ba", [1, 512], BF16)
    k.dma(wa2.v, d_wa2.v, q="pool")
    k.dma(ba.v, d_ba.v, q="pool")
    wkv = [k.take("wkv%d" % i, [128, KT, 512], BF16) for i in range(2)]
    tst = [k.take("tst%d" % i, [128, 512], BF16) for i in range(4)]
    cnt = 0
    for ci in range(3):
        w = wkv[ci % 2]
        k.dma(w.v.rr("p a b -> p (a b)"), d_wkv[ci], q="pool")
        for tt in range(16):
            ps = PS[2 + cnt % 2]
            for kt in range(KT):
                k.mm(ps.v, h[:, kt, tt * 128:(tt + 1) * 128], w[:, kt, :], start=(kt == 0), stop=(kt == KT - 1))
            st = tst[cnt % 4]
            k.copy(st.v, ps.v, eng="act" if cnt % 2 == 0 else "dve")
            dst = o_k[tt] if ci == 0 else o_v[tt][:, (ci - 1) * 512:ci * 512]
            P.out_ops.append(k.dma(dst, st.v, q="sp" if cnt % 2 == 0 else "act"))
            cnt += 1
    lt = [k.take("lt%d" % i, [128, 512], F32) for i in range(2)]
    for tt in range(16):
        ps = PS[4 + tt % 2]
        k.mm(ps.v, alo[:, tt * 128:(tt + 1) * 128], wa2.v, start=True, stop=False)
        k.mm(ps.v, tk.ones[0:1, 0:128], ba.v, start=False, stop=True)
        t_ = lt[tt % 2]
        k.act(t_.v, ps.v, AF.Exp, scale=-1.0)
        k.ts(t_.v, t_.v, 1.0, None, op0=ALU.add)
        k.act(t_.v, t_.v, AF.Ln)
        st = tst[tt % 4]
        k.ts(st.v, t_.v, -1.0 / 16.0, None, op0=ALU.mult)
        P.out_ops.append(k.dma(o_la[tt], st.v, q="sp" if tt % 2 == 0 else "act"))
    k.barrier()
    k.aoff = mark


STEPS["glapre"] = gla_pre


def gla_core(P, cfg):
    k, PS = P.k, P.PS
    d_q = P.din("g_qT", [128, SEQ], BF16)
    d_k = P.din("g_k", [128, 128 * 128], BF16)
    d_la = P.din("g_la", [128, 128 * 128], BF16)
    d_v = P.din("g_v", [128, 128 * 128], BF16)
    d_u2 = P.din("g_u2", [128, 128])
    d_ind = P.din("g_ind", [128, 2])
    d_o = P.dout("oT", [128, SEQ], BF16)
    q = k.take("q", [128, SEQ], BF16)
    kk = k.take("kk", [128, 128, 128], BF16)
    la = k.take("la", [128, 128, 128], BF16)
    v = k.take("v", [128, 128, 128], BF16)
    oT = k.take("oT", [128, SEQ], BF16)
    u2 = k.take("u2", [128, 128], BF16)
    ind = k.take("ind", [128, 2], BF16)
    state = k.take("state", [128, 128], F32)
    stb = [k.take("stb%d" % i, [128, 128], BF16) for i in range(2)]
    er = [k.take("er%d" % i, [128, 128], F32) for i in range(2)]
    kd = [k.take("kd%d" % i, [128, 128], BF16) for i in range(2)]
    dec = [k.take("dec%d" % i, [128, 2], F32) for i in range(2)]
    k.dma(u2.v, d_u2.v, q="pool")
    k.dma(ind.v, d_ind.v, q="pool")
    for i in range(4):
        sl = slice(i * 4096, (i + 1) * 4096)
        k.dma(q[:, sl], d_q[:, sl], q="sp")
        k.dma(kk.v.rr("p a b -> p (a b)")[:, sl], d_k[:, sl], q="act")
        k.dma(la.v.rr("p a b -> p (a b)")[:, sl], d_la[:, sl], q="sp")
        k.dma(v.v.rr("p a b -> p (a b)")[:, sl], d_v[:, sl], q="act")
    k.memset(state.v, 0.0)
    SC = float(GLA_DKH) ** -0.5
    for j in range(128):
        prv = PS[j % 2]
        ptt = PS[2 + j % 2]
        k.mm(prv[:, 0:128], u2.v, la[:, j, :])
        k.mm(ptt[:, 0:2], la[:, j, :], ind.v)
        e = er[j % 2]
        k.act(e.v, prv[:, 0:128], AF.Exp)
        kdj = kd[j % 2]
        k.tt(kdj.v, e.v, kk[:, j, :], ALU.mult)
        dj = dec[j % 2]
        k.act(dj.v, ptt[:, 0:2], AF.Exp)
        po = PS[6 + (j // 4) % 2]
        for ch in range(2):
            c = 2 * j + ch
            rows = slice(64 * ch, 64 * ch + 64)
            pu = PS[4 + c % 2]
            k.mm(pu[:, 0:128], kdj[rows, :], v[rows, j, :])
            k.stt(state.v, state.v, dj[:, ch:ch + 1], pu[:, 0:128], ALU.mult, ALU.add)
            sb_ = stb[c % 2]
            k.copy(sb_.v, state.v, eng="act")
            off = (c % 8) * 64
            k.mm(po[:, off:off + 64], sb_.v, q[:, c * 64:(c + 1) * 64])
        if j % 4 == 3:
            n = j // 4
            k.ts(oT[:, n * 512:(n + 1) * 512], po.v, SC, None, op0=ALU.mult)
    for i in range(4):
        sl = slice(i * 4096, (i + 1) * 4096)
        P.out_ops.append(k.dma(d_o[:, sl], oT[:, sl], q="sp" if i % 2 == 0 else "act"))


STEPS["glacore"] = gla_core


def gla_post(tk, arg):
    P, k, PS = tk.P, tk.k, tk.P.PS
    mark = k.aoff
    d_o = P.din("goT", [8, 128, T], BF16)
    d_g = P.din("gT", [8, 128, T], BF16)
    d_ng = P.din("gla_ng", [128, 8])
    o = k.take("o", [128, 8, T], BF16)
    g = k.take("g", [128, 8, T], BF16)
    og = k.take("og", [128, 8, T], BF16)
    ng = k.take("ng", [128, 8], F32)
    k.dma(ng.v, d_ng.v)
    for kt in range(8):
        k.dma(o[:, kt, :], d_o[kt], q="sp")
        k.dma(g[:, kt, :], d_g[kt], q="act")
    sq = [k.take("gsq%d" % i, [128, 512], BF16) for i in range(2)]
    rs = [k.take("grs%d" % i, [128, 512], F32) for i in range(2)]
    sg = [k.take("gsg%d" % i, [128, 512], F32) for i in range(2)]
    cnt = 0
    for hd in range(4):
        for n in range(NT):
            ts = slice(n * 512, (n + 1) * 512)
            ps = PS[cnt % 2]
            for e in range(2):
                s = sq[e]
                k.act(s.v, o[:, 2 * hd + e, ts], AF.Square)
                k.mm(ps.v, tk.ones.v, s.v, start=(e == 0), stop=(e == 1))
            r = rs[cnt % 2]
            k.ts(r.v, ps.v, 1.0 / GLA_DVH, EPS, op0=ALU.mult, op1=ALU.add)
            k.act(r.v, r.v, AF.Sqrt)
            k.recip(r.v, r.v)
            for e in range(2):
                kt = 2 * hd + e
                s_ = sg[e]
                k.act(s_.v, g[:, kt, ts], AF.Silu)
                k.stt(s_.v, o[:, kt, ts], ng[:, kt:kt + 1], s_.v, ALU.mult, ALU.mult)
                k.tt(og[:, kt, ts], s_.v, r.v, ALU.mult)
            cnt += 1
    tk.proj_acc("gla_wo", og, KT, KT, tk.add_to_x)
    k.barrier()
    k.aoff = mark


STEPS["glapost"] = gla_post


def host_gla(host):
    f = np.float32
    inp = host.inp
    u2 = np.zeros((128, 128), f)
    for t2 in range(128):
        for t in range(128):
            if t2 > t and t2 // 64 == t // 64:
                u2[t2, t] = 1.0
    host.common["g_u2"] = u2
    host.common["g_ind"] = (np.arange(128)[:, None] // 64 == np.arange(2)[None, :]).astype(f)
    win = np.asarray(inp["gla_w_in"], f)[0]
    host.common["gla_wqg"] = _acc_tiles(np.concatenate([win[:, 0:512], win[:, 2048:3072]], 1), KT, 12)
    host.common["gla_walo"] = np.ascontiguousarray(win[:, 3072:3088].reshape(KT, 128, 16).transpose(1, 0, 2)).reshape(128, KT * 16)
    kv = win[:, 512:2048]
    host.common["gla_wkv"] = np.ascontiguousarray(kv.reshape(KT, 128, 3, 512).transpose(2, 1, 0, 3)).reshape(3, 128, KT * 512)
    host.common["gla_wa2"] = np.ascontiguousarray(np.asarray(inp["gla_w_a2"], f)[0])
    host.common["gla_ba"] = np.ascontiguousarray(np.asarray(inp["gla_b_a"], f)[0][None, :])
    host.common["gla_ng"] = np.ascontiguousarray(np.asarray(inp["gla_norm"], f)[0].reshape(8, 128).T)
    host.common["gla_wo"] = _acc_tiles(np.asarray(inp["gla_w_o"], f)[0], KT, KT)


HOST_EXTRA.append(host_gla)


def _tokmajor_to_core(outs, name, c0, ncol=128):
    a = np.concatenate([np.asarray(outs[r][name])[:, :, c0:c0 + ncol] for r in range(NCORE)], 0)
    return np.ascontiguousarray(a.transpose(1, 0, 2)).reshape(128, 128 * ncol)


TWO_PI = 6.283185307179586
PI = 3.141592653589793


def s5_core(P, idx, PS):
    k = P.k
    s5_consts(P)
    nm = "s5_%d_" % idx
    d_lamP = P.din(nm + "lamP", [64, 2, 8])
    d_dtP = P.din(nm + "dtP", [64, 8])
    d_bP = P.din(nm + "bP", [64, 2, 8, 16])
    d_cP = P.din(nm + "cP", [64, 2, 8, 16])
    d_lamR = P.din(nm + "lamR", [128, 2, 64])
    d_dtR = P.din(nm + "dtR", [128, 1])
    d_bR = P.din(nm + "bR", [128, 2, 64])
    d_dv = P.din(nm + "dvec", [128, 1])
    d_u = P.din("u", [128, SEQ], BF16)
    d_z = k.dram("z", [128, SEQ], BF16, kind="ExternalOutput")
    jt, bmask, mbd, ident = P.jtab, P.bmask, P.mbd, P.ident
    b0 = k.aoff
    oA = b0
    oB = oA + 8192
    oC = oB + 8192
    oD = oC + 2048
    oE = oD + 4096
    oF = oE + 4224
    assert oF + 6400 <= k.awords, (oF, k.awords)

    class Cur:
        def __init__(self, off):
            self.off = off

        def take(self, name, shape, dtype):
            b = k.take_at(name, shape, dtype, self.off)
            self.off += b.words
            return b

    u = k.take_at("u", [128, SEQ], BF16, oA)
    for i in range(8):
        k.dma(u[:, i * T:(i + 1) * T], d_u[:, i * T:(i + 1) * T], q="sp" if i % 2 == 0 else "act")

    def powtable(cur, np_, lam, ldt_ap, shape3, tag, lo, nl):
        a, b = shape3
        dt = cur.take(tag + "dt", [np_, a, b], F32)
        k.act(dt.v, ldt_ap, AF.Exp)
        lrdt = cur.take(tag + "lrdt", [np_, a, b], F32)
        lidt = cur.take(tag + "lidt", [np_, a, b], F32)
        k.tt(lrdt.v, lam[0], dt.v, ALU.mult)
        k.tt(lidt.v, lam[1], dt.v, ALU.mult)
        mag = cur.take(tag + "mag", [np_, a, nl, b], F32)
        Pre = cur.take(tag + "Pre", [np_, a, nl, b], F32)
        tmp = cur.take(tag + "tmp", [np_, a, nl, b], F32)
        Pim = cur.take(tag + "Pim", [np_, a, nl, b], F32)
        for ai in range(a):
            jv = V(jt, jt.ap0[0:np_, lo:lo + nl].unsqueeze(2).to_broadcast([np_, nl, b]))
            lr_b = V(lrdt, lrdt.ap0[:, ai, :].unsqueeze(1).to_broadcast([np_, nl, b]))
            li_b = V(lidt, lidt.ap0[:, ai, :].unsqueeze(1).to_broadcast([np_, nl, b]))
            k.tt(mag[:, ai, :, :], jv, lr_b, ALU.mult)
            k.tt(Pre[:, ai, :, :], jv, li_b, ALU.mult)
        magf = mag.v.rr("p a l b -> p (a l b)")
        tmpf = tmp.v.rr("p a l b -> p (a l b)")
        pref = Pre.v.rr("p a l b -> p (a l b)")
        pimf = Pim.v.rr("p a l b -> p (a l b)")
        k.act(magf, magf, AF.Exp)
        scr = cur.take(tag + "scr", [np_, a, nl, b], F32)
        scrf = scr.v.rr("p a l b -> p (a l b)")
        tmpi = V(tmp, tmp.ap0.bitcast(I32).rearrange("p a l b -> p (a l b)"))

        def reduce_sin(dst, src):
            k.ts(tmpi, src, 1.0 / TWO_PI, None, op0=ALU.mult)
            k.copy(scrf, tmpi)
            k.stt(scrf, scrf, -TWO_PI, src, ALU.mult, ALU.add)
            k.ts(tmpf, scrf, PI, TWO_PI, op0=ALU.is_gt, op1=ALU.mult)
            k.tt(scrf, scrf, tmpf, ALU.subtract)
            k.ts(tmpf, scrf, -PI, -TWO_PI, op0=ALU.is_lt, op1=ALU.mult)
            k.tt(scrf, scrf, tmpf, ALU.subtract)
            k.act(dst, scrf, AF.Sin)
        reduce_sin(pimf, pref)
        k.tt(pimf, pimf, magf, ALU.mult)
        k.ts(pref, pref, 0.5 * PI, None, op0=ALU.add)
        reduce_sin(pref, pref)
        k.tt(pref, pref, magf, ALU.mult)
        return Pre, Pim

    def bbar(cur, np_, lam, abre, abim, bre, bim, shp, tag, bcast, out_re, out_im):
        a, b = lam[0].ap.shape[1], lam[0].ap.shape[2]
        t = [cur.take(tag + "f%d" % i, [np_, a, b], F32) for i in range(5)]
        lr, li = lam
        k.tt(t[0].v, lr, lr, ALU.mult)
        k.tt(t[1].v, li, li, ALU.mult)
        k.tt(t[0].v, t[0].v, t[1].v, ALU.add)
        k.recip(t[0].v, t[0].v)
        k.ts(t[1].v, abre, -1.0, None, op0=ALU.add)
        k.tt(t[2].v, t[1].v, lr, ALU.mult)
        k.tt(t[3].v, abim, li, ALU.mult)
        k.tt(t[2].v, t[2].v, t[3].v, ALU.add)
        k.tt(t[2].v, t[2].v, t[0].v, ALU.mult)
        k.tt(t[3].v, abim, lr, ALU.mult)
        k.tt(t[4].v, t[1].v, li, ALU.mult)
        k.tt(t[3].v, t[3].v, t[4].v, ALU.subtract)
        k.tt(t[3].v, t[3].v, t[0].v, ALU.mult)
        tm = cur.take(tag + "bbt", shp, F32)
        fre, fim = bcast(t[2]), bcast(t[3])
        k.tt(out_re.v, fre, bre, ALU.mult)
        k.tt(tm.v, fim, bim, ALU.mult)
        k.tt(out_re.v, out_re.v, tm.v, ALU.subtract)
        k.tt(out_im.v, fre, bim, ALU.mult)
        k.tt(tm.v, fim, bre, ALU.mult)
        k.tt(out_im.v, out_im.v, tm.v, ALU.add)

    cB = Cur(oB)
    XT = [cB.take("XT%d" % i, [128, 64, 64], BF16) for i in range(2)]
    c2 = Cur(oD)
    lamR = c2.take("lamR", [128, 2, 64], F32)
    dtR = c2.take("dtR", [128, 1], F32)
    bR = c2.take("bR", [128, 2, 64], F32)
    bbRre = c2.take("bbRre", [128, 1, 64], F32)
    bbRim = c2.take("bbRim", [128, 1, 64], F32)
    k.dma(lamR.v, d_lamR.v)
    k.dma(dtR.v, d_dtR.v)
    k.dma(bR.v, d_bR.v)
    lamRv = (lamR[:, 0:1, :], lamR[:, 1:2, :])
    dtRb = V(dtR, dtR.ap0.unsqueeze(2).to_broadcast([128, 1, 64]))
    c2b = Cur(c2.off)
    P1re, P1im = powtable(c2b, 128, lamRv, dtRb, (1, 64), "R1", 1, 1)
    bbar(c2b, 128, lamRv, P1re[:, :, 0, :], P1im[:, :, 0, :], bR[:, 0:1, :], bR[:, 1:2, :], [128, 1, 64], "R", lambda t: t.v, bbRre, bbRim)
    k.barrier()
    for hf in range(2):
        c2c = Cur(c2.off)
        PreR, PimR = powtable(c2c, 128, lamRv, dtRb, (1, 64), "R%d" % hf, 32 * hf, 32)
        t1 = c2c.take("Rt1", [128, 32, 64], F32)
        t2 = c2c.take("Rt2", [128, 32, 64], F32)
        bre_b = V(bbRre, bbRre.ap0[:, 0, :].unsqueeze(1).to_broadcast([128, 32, 64]))
        bim_b = V(bbRim, bbRim.ap0[:, 0, :].unsqueeze(1).to_broadcast([128, 32, 64]))
        ls = slice(32 * hf, 32 * hf + 32)
        k.tt(t1.v, PreR[:, 0, :, :], bre_b, ALU.mult)
        k.tt(t2.v, PimR[:, 0, :, :], bim_b, ALU.mult)
        k.tt(XT[0][:, ls, :], t1.v, t2.v, ALU.subtract)
        k.tt(t1.v, PreR[:, 0, :, :], bim_b, ALU.mult)
        k.tt(t2.v, PimR[:, 0, :, :], bre_b, ALU.mult)
        k.tt(XT[1][:, ls, :], t1.v, t2.v, ALU.add)
        k.barrier()

    P.checkpoint("tabR", XT[0].v.rr("p l q -> p (l q)"))
    cC = Cur(oC)
    lamP = cC.take("lamP", [64, 2, 8], F32)
    dtP = cC.take("dtP", [64, 8], F32)
    bP = cC.take("bP", [64, 2, 8, 16], F32)
    cP = cC.take("cP", [64, 2, 8, 16], F32)
    dv = cC.take("dv", [128, 1], F32)
    bbPre = cC.take("bbPre", [64, 8, 16], F32)
    bbPim = cC.take("bbPim", [64, 8, 16], F32)
    A64 = cC.take("A64", [64, 3, 8], F32)
    cb = cC.take("cb", [64, 2, 128], BF16)
    assert cC.off <= oD
    k.dma(lamP.v, d_lamP.v)
    k.dma(dtP.v, d_dtP.v)
    k.dma(bP.v.rr("p a b c -> p (a b c)"), d_bP.v.rr("p a b c -> p (a b c)"))
    k.dma(cP.v.rr("p a b c -> p (a b c)"), d_cP.v.rr("p a b c -> p (a b c)"))
    k.dma(dv.v, d_dv.v)
    lamPv = (lamP[:, 0, :].rr("p (g o) -> p g o", o=1), lamP[:, 1, :].rr("p (g o) -> p g o", o=1))
    dtPv = dtP.v.rr("p (g o) -> p g o", o=1)
    c3 = Cur(oE)
    PreP, PimP = powtable(c3, 64, lamPv, dtPv, (8, 1), "P", 0, 65)
    k.copy(A64[:, 0, :], PreP[:, :, 64, 0])
    k.copy(A64[:, 1, :], PimP[:, :, 64, 0])
    k.ts(A64[:, 2, :], PimP[:, :, 64, 0], -1.0, None, op0=ALU.mult)
    bbar(c3, 64, lamPv, PreP[:, :, 1, :], PimP[:, :, 1, :], bP[:, 0, :, :], bP[:, 1, :, :], [64, 8, 16], "P",
         lambda t: V(t, t.ap0[:, :, 0].unsqueeze(2).to_broadcast([64, 8, 16])), bbPre, bbPim)
    k.copy(cb[:, 0, :], cP[:, 0, :, :].rr("p g i -> p (g i)"))
    k.ts(cb[:, 1, :], cP[:, 1, :, :].rr("p g i -> p (g i)"), -1.0, None, op0=ALU.mult)
    PreQ = cC.take("PreQ", [64, 8, 65, 1], F32)
    PimQ = cC.take("PimQ", [64, 8, 65, 1], F32)
    assert cC.off <= oD, (cC.off, oD)
    k.copy(PreQ.v, PreP.v)
    k.copy(PimQ.v, PimP.v)
    k.barrier()
    P.checkpoint("tabP", PreQ.v.rr("p g l o -> p (g l o)"))

    Kb = k.take_at("Kb", [128, 64, 128], BF16, oD)
    c6 = Cur(oE)
    kt0 = c6.take("kt0", [128, 128], F32)
    Xc = [[c6.take("Xc%d_%d" % (i, ri), [64, 16, 8, 16], BF16) for ri in range(2)] for i in range(2)]
    xt1 = c6.take("xt1", [64, 16, 8, 16], F32)
    xt2 = c6.take("xt2", [64, 16, 8, 16], F32)
    for qq in range(4):
        X = Xc[qq % 2]
        lo = 16 * qq

        def pw_b(Pt):
            return V(Pt, Pt.ap0[:, :, lo:lo + 16, 0].rearrange("p g l -> p l g").unsqueeze(3).to_broadcast([64, 16, 8, 16]))

        def gi_b(Bt):
            return V(Bt, Bt.ap0.unsqueeze(1).to_broadcast([64, 16, 8, 16]))
        k.tt(xt1.v, pw_b(PreQ), gi_b(bbPre), ALU.mult)
        k.tt(xt2.v, pw_b(PimQ), gi_b(bbPim), ALU.mult, eng="pool")
        k.tt(X[0].v, xt1.v, xt2.v, ALU.subtract)
        k.tt(xt1.v, pw_b(PreQ), gi_b(bbPim), ALU.mult)
        k.tt(xt2.v, pw_b(PimQ), gi_b(bbPre), ALU.mult, eng="pool")
        k.tt(X[1].v, xt1.v, xt2.v, ALU.add)
        for li in range(16):
            lag = lo + li
            ps = PS[lag % 2]
            k.mm(ps[:, 0:128], X[0][:, li, :, :].rr("p g i -> p (g i)"), cb[:, 0, :], start=True, stop=False)
            k.mm(ps[:, 0:128], X[1][:, li, :, :].rr("p g i -> p (g i)"), cb[:, 1, :], start=False, stop=True)
            if lag == 0:
                k.tt(kt0.v, ps[:, 0:128], mbd.v, ALU.mult)
                k.stt(Kb[:, 0, :], ident.v, dv[:, 0:1], kt0.v, ALU.mult, ALU.add)
            else:
                k.tt(Kb[:, lag, :], ps[:, 0:128], mbd.v, ALU.mult)
    k.barrier()
    P.checkpoint("kb", Kb.v.rr("p l q -> p (l q)")[:, 0:4096])

    NCH = SEQ // 64
    S = k.take_at("S", [64, 2, 8, NCH + 1], F32, oE)
    c4 = Cur(oF)
    wbd = [c4.take("wbd%d" % i, [128, 4, 2, 8, 64], BF16) for i in range(2)]
    Bch = c4.take("Bch", [128, 2, 2, 512], F32)
    st1 = c4.take("st1", [64, 2, 8], F32)
    st2 = c4.take("st2", [64, 2, 8], F32)
    bm_b = V(bmask, bmask.ap0.unsqueeze(2).to_broadcast([128, 8, 64]))
    psb = [[PS[4 + blk * 2 + ri] for ri in range(2)] for blk in range(2)]
    for q in range(16):
        w = wbd[q % 2]
        for li in range(4):
            lag = 4 * q + li
            for ri in range(2):
                k.tt(w[:, li, ri, :, :], V(XT[ri], XT[ri].ap0[:, lag, :].unsqueeze(1).to_broadcast([128, 8, 64])), bm_b, ALU.mult,
                     eng="dve" if ri == 0 else "pool")
        for li in range(4):
            lag = 4 * q + li
            jp = 63 - lag
            for blk in range(2):
                lhs = u.v.rr("p (c j) -> p c j", j=64)[:, blk * 128:(blk + 1) * 128, jp]
                for ri in range(2):
                    k.mm(psb[blk][ri].v, lhs, w[:, li, ri, :, :].rr("p g q -> p (g q)"), start=(lag == 0), stop=(lag == 63))
    for blk in range(2):
        for ri in range(2):
            k.copy(Bch[:, blk, ri, :], psb[blk][ri].v, eng="act" if ri == 0 else "dve")
    k.memset(S[:, :, :, 0:1], 0.0)
    cnt = 0
    for blk in range(2):
        for ri in range(2):
            for g in range(8):
                ps = PS[cnt % 4]
                k.transpose(ps[0:64, 0:128], Bch[:, blk, ri, g * 64:(g + 1) * 64], ident.v)
                k.copy(S[:, ri, g, 1 + blk * 128: 1 + (blk + 1) * 128], ps[0:64, 0:128], eng="act" if cnt % 2 == 0 else "dve")
                cnt += 1
    k.barrier()
    P.checkpoint("states", S.v.rr("p r g c -> p (r g c)")[:, 0:4096])

    ytoep = k.take_at("ytoep", [128, SEQ], BF16, oB)
    cs = Cur(c4.off)
    NB, BM = 16, 16
    sa = cs.take("sa", [64, 2, 8, NB], F32)
    sb2 = cs.take("sb2", [64, 2, 8, NB], F32)
    carry = cs.take("carry", [64, 2, 8, NB], F32)
    Apw = cs.take("Apw", [64, 3, 8], F32)
    apt = cs.take("apt", [64, 3, 8], F32)
    SV = S.v[:, :, :, 1:NCH + 1].rr("p r g (b m) -> p r g b m", m=BM)

    def cmul(dst, src, A, nx):
        ab = V(A, A.ap0[:, 0, :].unsqueeze(1).unsqueeze(3).to_broadcast([64, 2, 8, nx]))
        nim = V(A, A.ap0[:, 2, :].unsqueeze(2).to_broadcast([64, 8, nx]))
        pim = V(A, A.ap0[:, 1, :].unsqueeze(2).to_broadcast([64, 8, nx]))
        k.tt(sa[:, :, :, 0:nx], src, ab, ALU.mult)
        k.tt(sb2[:, 0, :, 0:nx], src[:, 1], nim, ALU.mult)
        k.tt(sb2[:, 1, :, 0:nx], src[:, 0], pim, ALU.mult)
        k.tt(dst, sa[:, :, :, 0:nx], sb2[:, :, :, 0:nx], ALU.add)

    k.copy(Apw.v, A64.v)
    for _ in range(4):
        k.tt(apt[:, 0, :], Apw[:, 0, :], Apw[:, 0, :], ALU.mult)
        k.tt(apt[:, 1, :], Apw[:, 1, :], Apw[:, 1, :], ALU.mult)
        k.tt(apt[:, 2, :], Apw[:, 0, :], Apw[:, 1, :], ALU.mult)
        k.tt(Apw[:, 0, :], apt[:, 0, :], apt[:, 1, :], ALU.subtract)
        k.ts(Apw[:, 1, :], apt[:, 2, :], 2.0, None, op0=ALU.mult)
        k.ts(Apw[:, 2, :], apt[:, 2, :], -2.0, None, op0=ALU.mult)
    for i in range(1, BM):
        cmul(carry.v, SV[:, :, :, :, i - 1], A64, NB)
        k.tt(SV[:, :, :, :, i], SV[:, :, :, :, i], carry.v, ALU.add)
    for blk in range(1, NB):
        cmul(carry[:, :, :, 0:1], SV[:, :, :, blk - 1:blk, BM - 1], Apw, 1)
        k.tt(SV[:, :, :, blk:blk + 1, BM - 1], SV[:, :, :, blk:blk + 1, BM - 1], carry[:, :, :, 0:1], ALU.add)
    k.copy(carry[:, :, :, 0:NB - 1], SV[:, :, :, 0:NB - 1, BM - 1])
    for i in range(BM - 1):
        cmul(carry[:, :, :, 0:NB - 1], carry[:, :, :, 0:NB - 1], A64, NB - 1)
        k.tt(SV[:, :, :, 1:NB, i], SV[:, :, :, 1:NB, i], carry[:, :, :, 0:NB - 1], ALU.add)
    for n in range(SEQ // 512):
        ps = PS[n % 4]
        ts_ = slice(n * 512, (n + 1) * 512)
        pv = V(ps, ps.ap0.rearrange("p (c j) -> p c j", j=64))
        uv = u[:, ts_].rr("p (c j) -> p c j", j=64)
        for lag in range(64):
            k.mm(pv[:, :, lag:64], Kb[:, lag, :], uv[:, :, 0:64 - lag], start=(lag == 0), stop=(lag == 63))
        k.copy(ytoep[:, ts_], ps.v, eng="act")
    k.barrier()
    P.checkpoint("scan", S.v.rr("p r g c -> p (r g c)")[:, 0:4096])

    c5 = Cur(oF)
    Sb = c5.take("Sb", [64, 2, 8, NCH], BF16)
    Yst = c5.take("Yst", [128, 64, 8, 16], BF16)
    k.copy(Sb.v, S[:, :, :, 0:NCH])
    k.barrier()
    c5b = Cur(oE)
    CLg = [[c5b.take("CL%d_%d" % (i, ri), [64, 64, 16], BF16) for ri in range(2)] for i in range(2)]
    ct1 = c5b.take("ct1", [64, 64, 16], F32)
    ct2 = c5b.take("ct2", [64, 64, 16], F32)
    assert c5b.off <= oF, (c5b.off, oF)
    cnt = 0
    for blk in range(2):
        for g in range(8):
            CL = CLg[g % 2]
            pre1 = V(PreQ, PreQ.ap0[:, g, 1:65, 0].unsqueeze(2).to_broadcast([64, 64, 16]))
            pim1 = V(PimQ, PimQ.ap0[:, g, 1:65, 0].unsqueeze(2).to_broadcast([64, 64, 16]))
            cre_b = V(cP, cP.ap0[:, 0, g, :].unsqueeze(1).to_broadcast([64, 64, 16]))
            cim_b = V(cP, cP.ap0[:, 1, g, :].unsqueeze(1).to_broadcast([64, 64, 16]))
            k.tt(ct1.v, pre1, cre_b, ALU.mult)
            k.tt(ct2.v, pim1, cim_b, ALU.mult, eng="pool")
            k.tt(CL[0].v, ct1.v, ct2.v, ALU.subtract)
            k.tt(ct1.v, pim1, cre_b, ALU.mult)
            k.tt(ct2.v, pre1, cim_b, ALU.mult, eng="pool")
            k.tt(ct1.v, ct1.v, ct2.v, ALU.add)
            k.ts(CL[1].v, ct1.v, -1.0, None, op0=ALU.mult)
            for pc in range(2):
                ps = PS[cnt % 2]
                for ri in range(2):
                    k.mm(ps.v, Sb[:, ri, g, blk * 128:(blk + 1) * 128],
                         CL[ri][:, pc * 32:(pc + 1) * 32, :].rr("p j i -> p (j i)"), start=(ri == 0), stop=(ri == 1))
                k.copy(Yst[:, pc * 32:(pc + 1) * 32, g, :], V(ps, ps.ap0.rearrange("p (j i) -> p j i", i=16)), eng="act")
                cnt += 1
        for j4 in range(16):
            pst = PS[2 + (j4 % 2)]
            pstb = V(pst, pst.ap0.bitcast(BF16)[:, 0:512].rearrange("p (a b) -> p a b", a=4))
            for jj in range(4):
                j = 4 * j4 + jj
                k.transpose(pstb[:, jj, :], Yst[:, j, :, :].rr("p g i -> p (g i)"), P.identb.v)
            dst = ytoep.v.rr("p (c j) -> p j c", j=64)[:, 4 * j4:4 * j4 + 4, blk * 128:(blk + 1) * 128]
            k.tt(dst, dst, pstb, ALU.add)
    k.barrier()
    P.checkpoint("ystate", ytoep[:, 0:4096])

    c7 = Cur(oE)
    y2 = [c7.take("y2%d" % i, [128, 512], F32) for i in range(2)]
    GC = 0.7978845608028654
    for n in range(SEQ // 512):
        ts_ = slice(n * 512, (n + 1) * 512)
        y = ytoep[:, ts_]
        w = y2[n % 2]
        k.act(w.v, y, AF.Square)
        k.ts(w.v, w.v, 0.044715, 1.0, op0=ALU.mult, op1=ALU.add)
        k.tt(w.v, w.v, y, ALU.mult, eng="pool")
        k.act(w.v, w.v, AF.Sigmoid, scale=2.0 * GC)
        k.tt(u[:, ts_], y, w.v, ALU.mult)
    k.barrier()
    z = u


    P.checkpoint("toep", z[:, 0:4096])
    for i in range(8):
        P.out_ops.append(k.dma(d_z[:, i * T:(i + 1) * T], z[:, i * T:(i + 1) * T], q="sp" if i % 2 == 0 else "act"))


def s5_consts(P):
    k = P.k
    if hasattr(P, "s5c"):
        return
    P.s5c = True
    idd = P.din("ident", [128, 128])
    P.ident = k.take("ident", [128, 128], F32)
    k.dma(P.ident.v, idd.v)
    P.identb = k.take("identb", [128, 128], BF16)
    k.copy(P.identb.v, P.ident.v)
    d_j = P.din("jtab", [128, 65])
    d_bm = P.din("blockmask", [128, 8])
    d_mbd = P.din("maskbd", [128, 128])
    P.jtab = k.take("jtab", [128, 65], F32)
    P.bmask = k.take("bmask", [128, 8], F32)
    P.mbd = k.take("mbd", [128, 128], F32)
    k.dma(P.jtab.v, d_j.v)
    k.dma(P.bmask.v, d_bm.v)
    k.dma(P.mbd.v, d_mbd.v)


def host_s5(host):
    f = np.float32
    inp = host.inp
    eye = np.eye(128, dtype=f)
    host.common["ident"] = eye
    host.common["jtab"] = np.broadcast_to(np.arange(65, dtype=f)[None, :], (128, 65)).copy()
    host.common["blockmask"] = (np.arange(128)[:, None] // 16 == np.arange(8)[None, :]).astype(f)
    host.common["maskbd"] = (np.arange(128)[:, None] // 16 == np.arange(128)[None, :] // 16).astype(f)
    for c in range(NCORE):
        m = host.percore[c]
        gs = slice(8 * c, 8 * c + 8)
        for idx in range(2):
            nm = "s5_%d_" % idx
            lre = np.asarray(inp["s5_lam_re"], f)[idx, gs]
            lim = np.asarray(inp["s5_lam_im"], f)[idx, gs]
            ldt = np.asarray(inp["s5_log_dt"], f)[idx, gs]
            bre = np.asarray(inp["s5_b_re"], f)[idx, gs]
            bim = np.asarray(inp["s5_b_im"], f)[idx, gs]
            cre = np.asarray(inp["s5_c_re"], f)[idx, gs]
            cim = np.asarray(inp["s5_c_im"], f)[idx, gs]
            m[nm + "lamP"] = np.ascontiguousarray(np.stack([lre.T, lim.T], 1))
            m[nm + "dtP"] = np.ascontiguousarray(np.broadcast_to(ldt[None, :], (64, 8)))
            m[nm + "bP"] = np.ascontiguousarray(np.stack([bre.transpose(1, 0, 2), bim.transpose(1, 0, 2)], 1))
            m[nm + "cP"] = np.ascontiguousarray(np.stack([cre.transpose(2, 0, 1), cim.transpose(2, 0, 1)], 1))
            m[nm + "lamR"] = np.ascontiguousarray(np.stack([np.repeat(lre, 16, 0), np.repeat(lim, 16, 0)], 1))
            m[nm + "dtR"] = np.ascontiguousarray(np.repeat(ldt, 16)[:, None])
            m[nm + "bR"] = np.ascontiguousarray(np.stack([bre.transpose(0, 2, 1).reshape(128, 64),
                                                          bim.transpose(0, 2, 1).reshape(128, 64)], 1))
            m[nm + "dvec"] = np.ascontiguousarray(np.asarray(inp["s5_d"], f)[idx, 128 * c:128 * c + 128][:, None])


HOST_EXTRA.append(host_s5)
STEPS["s5core"] = lambda P, cfg: s5_core(P, cfg["idx"], P.PS)


def _set_x(host, o):
    host.set("xT", [np.asarray(o[c]["outT"]) for c in range(NCORE)])


def _set_gla_inputs(host, o):
    host.set("gT", [np.asarray(o[c]["gT"]) for c in range(NCORE)])
    host.set("g_qT", [np.ascontiguousarray(np.concatenate([np.asarray(o[r]["qT"])[c // 2] for r in range(NCORE)], 1))
                      for c in range(NCORE)])
    host.set("g_k", [_tokmajor_to_core(o, "ktm", (c // 2) * 128) for c in range(NCORE)])
    host.set("g_la", [_tokmajor_to_core(o, "latm", (c // 2) * 128) for c in range(NCORE)])
    host.set("g_v", [_tokmajor_to_core(o, "vtm", (c // 2) * 256 + (c % 2) * 128) for c in range(NCORE)])


def _set_attn_inputs(host, o):
    host.set("a_qT", [np.ascontiguousarray(np.concatenate([np.asarray(o[r]["aqT"])[c] for r in range(NCORE)], 1)) for c in range(NCORE)])
    host.set("a_kT", [np.ascontiguousarray(np.concatenate([np.asarray(o[r]["akT"])[c] for r in range(NCORE)], 1)) for c in range(NCORE)])
    host.set("a_v", [_tokmajor_to_core(o, "avtm", c * 128) for c in range(NCORE)])


def run_s5_layer(host, l, last=False):
    idx = l // 3
    o = launch(host, {"kind": "tok", "steps": ("norm_out:%d,h" % l,)})
    host.set("u", tok_to_chan(o, "h"))
    o = launch(host, {"kind": "s5core", "idx": idx})
    host.set("zf", chan_to_tok(o, "z"))
    steps = ["s5post:%d" % idx, "ffn:%d" % l]
    if last:
        steps.append("final")
    steps.append("store_x:outT")
    _set_x(host, launch(host, {"kind": "tok", "steps": tuple(steps)}))


def run_gla_layer(host, l):
    o = launch(host, {"kind": "tok", "steps": ("glapre",)})
    _set_gla_inputs(host, o)
    o = launch(host, {"kind": "glacore"})
    host.set("goT", chan_to_tok(o, "oT"))
    _set_x(host, launch(host, {"kind": "tok", "steps": ("glapost", "ffn:%d" % l, "store_x:outT")}))


def run_attn_layer(host, l):
    o = launch(host, {"kind": "tok", "steps": ("attnpre",)})
    _set_attn_inputs(host, o)
    o = launch(host, {"kind": "attncore"})
    host.set("aoTt", chan_to_tok(o, "aoT"))
    _set_x(host, launch(host, {"kind": "tok", "steps": ("attnpost", "ffn:%d" % l, "store_x:outT")}))


def gather_out(host):
    outs = [np.asarray(host.percore[c]["xT"], np.float32).reshape(D, T).T for c in range(NCORE)]
    return np.ascontiguousarray(np.concatenate(outs, 0))[None]


def kernel(**inputs):
    host = Host(inputs)
    o = launch(host, {"kind": "tok", "steps": ("norm_out:0,h",)})
    host.set("u", tok_to_chan(o, "h"))
    o = launch(host, {"kind": "s5core", "idx": 0})
    host.set("zf", chan_to_tok(o, "z"))
    o = launch(host, {"kind": "tok", "steps": ("s5post:0", "ffn:0", "glapre", "store_x:outT")})
    _set_x(host, o)
    _set_gla_inputs(host, o)
    o = launch(host, {"kind": "glacore"})
    host.set("goT", chan_to_tok(o, "oT"))
    o = launch(host, {"kind": "tok", "steps": ("glapost", "ffn:1", "attnpre", "store_x:outT")})
    _set_x(host, o)
    _set_attn_inputs(host, o)
    o = launch(host, {"kind": "attncore"})
    host.set("aoTt", chan_to_tok(o, "aoT"))
    o = launch(host, {"kind": "tok", "steps": ("attnpost", "ffn:2", "norm_out:3,h", "store_x:outT")})
    _set_x(host, o)
    host.set("u", tok_to_chan(o, "h"))
    o = launch(host, {"kind": "s5core", "idx": 1})
    host.set("zf", chan_to_tok(o, "z"))
    o = launch(host, {"kind": "tok", "steps": ("s5post:1", "ffn:3", "final", "store_x:outT")})
    _set_x(host, o)
    return gather_out(host)
```

```python
import numpy as np
from concourse.bass_utils import run_bass_kernel_spmd
import numpy as np
from contextlib import ExitStack
import concourse.bass as bass
import concourse.mybir as mybir

F32 = mybir.dt.float32
BF16 = mybir.dt.bfloat16
I32 = mybir.dt.int32
AF = mybir.ActivationFunctionType
ALU = mybir.AluOpType
AX = mybir.AxisListType

COMPUTE = ("pe", "act", "dve", "pool")


class Buf:
    def __init__(self, name, ap0, space):
        self.name = name
        self.ap0 = ap0
        self.space = space
        self.w = []
        self.r = []

    def __getitem__(self, idx):
        return V(self, self.ap0[idx])

    @property
    def v(self):
        return V(self, self.ap0)


class V:
    def __init__(self, buf, ap):
        self.buf = buf
        self.ap = ap

    def __getitem__(self, idx):
        return V(self.buf, self.ap[idx])

    def rr(self, pat, **kw):
        return V(self.buf, self.ap.rearrange(pat, **kw))


class Op:
    __slots__ = ("id", "eng", "fn", "deps", "is_dma", "is_cc", "inc", "semval", "sem", "raw_same", "prev")

    def __init__(self, id, eng, fn, is_dma=False, is_cc=False):
        self.id = id
        self.eng = eng
        self.fn = fn
        self.deps = set()
        self.raw_same = set()
        self.is_dma = is_dma
        self.is_cc = is_cc
        self.inc = False
        self.semval = None
        self.sem = None


class KB:
    def __init__(self):
        self.nc = bass.Bass("TRN2", target_bir_lowering=False)
        self.ops = []
        self.stack = ExitStack()
        self.nbuf = 0
        self.sb_bytes = 0
        self._bar_from = 0

    def dram(self, name, shape, dtype, kind="Internal"):
        if kind == "Internal":
            t = self.nc.dram_tensor(name, list(shape), dtype)
        else:
            t = self.nc.dram_tensor(name, list(shape), dtype, kind=kind)
        return Buf(name, t.ap(), "dram")

    def sb(self, name, shape, dtype):
        t = self.stack.enter_context(self.nc.sbuf_tensor(name, list(shape), dtype))
        n = 1
        for s in shape[1:]:
            n *= s
        self.sb_bytes += n * (4 if dtype in (F32, I32) else 2)
        return Buf(name, t[:], "sbuf")

    def ps(self, name, shape, dtype=F32):
        t = self.stack.enter_context(self.nc.psum_tensor(name, list(shape), dtype))
        return Buf(name, t[:], "psum")

    def rec(self, eng, fn, reads=(), writes=(), is_dma=False, is_cc=False):
        op = Op(len(self.ops), eng, fn, is_dma, is_cc)
        rb = []
        for x in reads:
            if x is None or isinstance(x, (int, float)):
                continue
            b = x.buf if isinstance(x, V) else x
            if b not in rb:
                rb.append(b)
        wb = []
        for x in writes:
            if x is None:
                continue
            b = x.buf if isinstance(x, V) else x
            if b not in wb:
                wb.append(b)
        for b in rb:
            for d in b.w:
                op.deps.add(d)
                op.raw_same.add(d)
            if b.space == "psum":
                for d in b.r:
                    if self.ops[d].eng != eng:
                        op.deps.add(d)
        for b in wb:
            for d in b.w:
                op.deps.add(d)
            for d in b.r:
                op.deps.add(d)
        for b in rb:
            if b not in wb:
                b.r.append(op.id)
        for b in wb:
            b.w = [op.id]
            b.r = []
        op.deps.discard(op.id)
        self.ops.append(op)
        return op

    def barrier(self):
        last = {}
        pend = []
        for op in self.ops[self._bar_from:]:
            if op.fn is None:
                continue
            if op.is_dma or op.is_cc:
                pend.append(op.id)
            else:
                last[op.eng] = op.id
        self._bar_from = len(self.ops)
        deps = set(pend) | set(last.values())
        for e in ("pe", "act", "dve", "pool", "sp"):
            op = Op(len(self.ops), e, None)
            op.deps = set(deps)
            op.raw_same = set(deps)
            self.ops.append(op)
        self._bar_from = len(self.ops) - 5

    def take(self, name, shape, dtype):
        n = 1
        for s in shape[1:]:
            n *= s
        words = n if dtype in (F32, I32) else (n + 1) // 2
        assert self.aoff + words <= self.awords, (name, self.aoff, words, self.awords)
        ap = self.arena.ap0[0:shape[0], self.aoff:self.aoff + words]
        self.aoff += words
        self.amax = max(self.amax, self.aoff)
        if dtype not in (F32,):
            ap = ap.bitcast(dtype)
        ap = ap[:, 0:n]
        if len(shape) == 3:
            ap = ap.rearrange("p (a b) -> p a b", a=shape[1])
        elif len(shape) == 4:
            ap = ap.rearrange("p (a b c) -> p a b c", a=shape[1], b=shape[2])
        elif len(shape) == 5:
            ap = ap.rearrange("p (a b c d) -> p a b c d", a=shape[1], b=shape[2], c=shape[3])
        return Buf(name, ap, "sbuf")

    def take_at(self, name, shape, dtype, off, pbase=0):
        n = 1
        for x in shape[1:]:
            n *= x
        words = n if dtype in (F32, I32) else (n + 1) // 2
        assert off + words <= self.awords, (name, off, words, self.awords)
        self.amax = max(self.amax, off + words)
        ap = self.arena.ap0[pbase:pbase + shape[0], off:off + words]
        if dtype not in (F32,):
            ap = ap.bitcast(dtype)
        ap = ap[:, 0:n]
        if len(shape) == 3:
            ap = ap.rearrange("p (a b) -> p a b", a=shape[1])
        elif len(shape) == 4:
            ap = ap.rearrange("p (a b c) -> p a b c", a=shape[1], b=shape[2])
        elif len(shape) == 5:
            ap = ap.rearrange("p (a b c d) -> p a b c d", a=shape[1], b=shape[2], c=shape[3])
        b = Buf(name, ap, "sbuf")
        b.words = words
        b.off = off
        return b

    def init_arena(self, words):
        self.arena = self.sb("arena", [128, words], F32)
        self.awords = words
        self.aoff = 0
        self.amax = 0

    @staticmethod
    def _a(x):
        return x.ap if isinstance(x, V) else x

    def mm(self, out, lhsT, rhs, start=True, stop=True):
        a = self._a
        return self.rec("pe", lambda e: e.matmul(a(out), a(lhsT), a(rhs), start=start, stop=stop),
                        reads=[lhsT, rhs], writes=[out])

    def transpose(self, out, in_, ident):
        a = self._a
        return self.rec("pe", lambda e: e.transpose(a(out), a(in_), a(ident)), reads=[in_, ident], writes=[out])

    def act(self, out, in_, func, bias=0.0, scale=1.0, accum=None, eng="act"):
        a = self._a
        kw = {}
        if accum is not None:
            kw["accum_out"] = a(accum)
        return self.rec(eng, lambda e: e.activation(a(out), a(in_), func, bias=a(bias), scale=a(scale), **kw),
                        reads=[in_, bias, scale], writes=[out, accum])

    def tt(self, out, in0, in1, op, eng="dve"):
        a = self._a
        return self.rec(eng, lambda e: e.tensor_tensor(a(out), a(in0), a(in1), op), reads=[in0, in1], writes=[out])

    def ts(self, out, in0, s1, s2=None, op0=ALU.mult, op1=None, accum=None, eng="dve"):
        a = self._a
        kw = {}
        if op1 is not None:
            kw["op1"] = op1
        if accum is not None:
            kw["accum_out"] = a(accum)
        return self.rec(eng, lambda e: e.tensor_scalar(a(out), a(in0), a(s1), a(s2) if s2 is not None else None, op0, **kw),
                        reads=[in0, s1, s2], writes=[out, accum])

    def stt(self, out, in0, scalar, in1, op0, op1, eng="dve"):
        a = self._a
        return self.rec(eng, lambda e: e.scalar_tensor_tensor(a(out), a(in0), a(scalar), a(in1), op0, op1),
                        reads=[in0, scalar, in1], writes=[out])

    def copy(self, out, in_, eng="dve"):
        a = self._a
        if eng == "act":
            return self.rec(eng, lambda e: e.copy(a(out), a(in_)), reads=[in_], writes=[out])
        return self.rec(eng, lambda e: e.tensor_copy(a(out), a(in_)), reads=[in_], writes=[out])

    def memset(self, out, val, eng="dve"):
        a = self._a
        return self.rec(eng, lambda e: e.memset(a(out), val), reads=[], writes=[out])

    def recip(self, out, in_):
        a = self._a
        return self.rec("dve", lambda e: e.reciprocal(a(out), a(in_)), reads=[in_], writes=[out])

    def scan(self, out, d0, d1, initial, op0, op1):
        a = self._a
        return self.rec("dve", lambda e: e.tensor_tensor_scan(a(out), a(d0), a(d1), a(initial), op0, op1),
                        reads=[d0, d1, initial], writes=[out])

    def dma(self, out, in_, q="sp", **kw):
        a = self._a
        return self.rec(q, lambda e: e.dma_start(out=a(out), in_=a(in_), **kw), reads=[in_], writes=[out], is_dma=True)

    def cc(self, kind, ins, outs, op=ALU.bypass, groups=None):
        a = self._a
        g = groups or [list(range(8))]
        return self.rec("pool", lambda e: e.collective_compute(kind, op, replica_groups=g,
                                                                  ins=[a(i) for i in ins], outs=[a(o) for o in outs]),
                        reads=list(ins), writes=list(outs), is_cc=True)

    def emit(self, final_wait_ops=()):
        nc = self.nc
        ops = self.ops
        engs = ["pe", "act", "dve", "pool", "sp"]
        for op in ops:
            for d in op.deps:
                dop = ops[d]
                if dop.eng != op.eng:
                    dop.inc = True
                elif dop.is_dma or dop.is_cc:
                    dop.inc = True
                elif op.eng in ("act", "dve", "pool") and not op.is_dma:
                    dop.inc = True
        for d in final_wait_ops:
            d.inc = True
        NDS = {"sp": 24, "pool": 12, "act": 8, "pe": 1, "dve": 1}
        st = self.stack
        csem = {e: st.enter_context(nc.semaphore("c_" + e)) for e in COMPUTE}
        dsem = {e: [st.enter_context(nc.semaphore("d_%s%d" % (e, i))) for i in range(NDS[e])] for e in engs}
        ccsem = st.enter_context(nc.semaphore("ccsem"))
        ccnt = {e: 0 for e in COMPUTE}
        dcnt = {e: [0] * NDS[e] for e in engs}
        drr = {e: 0 for e in engs}
        cccnt = 0
        for op in ops:
            if op.is_dma:
                i = drr[op.eng] % NDS[op.eng]
                drr[op.eng] += 1
                op.sem = ("d", op.eng, i)
                op.prev = dcnt[op.eng][i]
                dcnt[op.eng][i] += 16
                op.semval = dcnt[op.eng][i]
            elif op.is_cc:
                cccnt += 1
                op.sem = ("cc",)
                op.semval = cccnt
            elif op.inc:
                ccnt[op.eng] += 1
                op.sem = ("c", op.eng)
                op.semval = ccnt[op.eng]

        def semh(s):
            if s[0] == "d":
                return dsem[s[1]][s[2]]
            if s[0] == "cc":
                return ccsem
            return csem[s[1]]

        by_eng = {e: [op for op in ops if op.eng == e] for e in engs}
        self.stats = {e: len(by_eng[e]) for e in engs}
        nwaits = {e: 0 for e in engs}

        def run(eng_name, e):
            seen = {}
            for op in by_eng[eng_name]:
                need = {}
                for d in op.deps:
                    dop = ops[d]
                    if dop.sem is None:
                        continue
                    if dop.eng == eng_name and not (dop.is_dma or dop.is_cc):
                        if not (eng_name in ("act", "dve", "pool") and not op.is_dma):
                            continue
                    if need.get(dop.sem, 0) < dop.semval:
                        need[dop.sem] = dop.semval
                if op.is_dma and op.prev > 0:
                    if need.get(op.sem, 0) < op.prev:
                        need[op.sem] = op.prev
                for s, v in need.items():
                    if seen.get(s, 0) >= v:
                        continue
                    e.wait_ge(semh(s), v)
                    nwaits[eng_name] += 1
                    seen[s] = v
                if op.fn is None:
                    continue
                ins = op.fn(e)
                if op.is_dma:
                    ins.then_inc(semh(op.sem), 16)
                elif op.is_cc:
                    ins.then_inc(semh(op.sem), 1)
                elif op.inc:
                    ins.then_inc(semh(op.sem), 1)
            if eng_name == "sp":
                for d in final_wait_ops:
                    e.wait_ge(semh(d.sem), d.semval)

        with nc.Block() as block:
            @block.tensor
            def _(e):
                run("pe", e)

            @block.scalar
            def _(e):
                run("act", e)

            @block.vector
            def _(e):
                run("dve", e)

            @block.gpsimd
            def _(e):
                run("pool", e)

            @block.sync
            def _(e):
                run("sp", e)
        self.stats["waits"] = nwaits
        self.stack.close()
        return nc
D = 1024
SEQ = 16384
NCORE = 8
T = SEQ // NCORE
NT = T // 512
KT = D // 128
FH = 2816
FM = FH // 128
EPS = 1e-6


class StopBuild(Exception):
    pass


class Prog:
    def __init__(self, cfg):
        self.cfg = cfg
        self.k = KB()
        self.ins = {}
        self.outs = []
        self.out_ops = []
        self.dbg_ops = []
        k = self.k
        k.init_arena(52000)
        if cfg.get("kind") == "attncore":
            self.PS2 = [k.ps("pss%d" % i, [128, 1024], F32) for i in range(2)]
            self.PS = [None] * 4 + [k.ps("ps%d" % i, [128, 512], F32) for i in range(4, 8)]
        else:
            self.PS = [k.ps("ps%d" % i, [128, 512], F32) for i in range(8)]

    def din(self, name, shape, dtype=F32):
        b = self.k.dram(name, shape, dtype, kind="ExternalInput")
        self.ins[name] = (tuple(shape), dtype)
        return b

    def dout(self, name, shape, dtype=F32):
        b = self.k.dram(name, shape, dtype, kind="ExternalOutput")
        self.outs.append(name)
        return b

    def checkpoint(self, name, dump=None):
        if self.cfg.get("stop") != name:
            return
        k = self.k
        k.barrier()
        if dump is not None:
            n = dump.ap.shape[1]
            np_ = dump.ap.shape[0]
            dbg = self.dout("dbg", [128, 4096], F32)
            tmp = k.take_at("dbgtmp", [128, 4096], F32, k.awords - 4096)
            k.memset(tmp.v, 0.0)
            k.copy(tmp[0:np_, 0:n], dump)
            self.dbg_ops.append(k.dma(dbg.v, tmp.v))
        raise StopBuild()

    def finish(self):
        self.nc = self.k.emit(final_wait_ops=self.out_ops + self.dbg_ops)
        return self


class Tok:
    def __init__(self, P):
        self.P = P
        k = P.k
        self.k = k
        xin = P.din("xT", [KT, 128, T])
        gains = P.din("gains", [128, 9, KT])
        self.x = k.take("x", [128, KT, T], F32)
        self.gn = k.take("gn", [128, 9, KT], F32)
        self.ones = k.take("ones", [128, 128], BF16)
        k.dma(self.gn.v, gains.v)
        for kt in range(KT):
            k.dma(self.x[:, kt, :], xin[kt], q="sp" if kt % 2 == 0 else "act")
        k.memset(self.ones.v, 1.0)

    def store_x(self, name="outT"):
        out = self.P.dout(name, [KT, 128, T], F32)
        for kt in range(KT):
            self.P.out_ops.append(self.k.dma(out[kt], self.x[:, kt, :], q="sp" if kt % 2 == 0 else "act"))

    def rstd_tile(self, n, tag):
        k, x, PS = self.k, self.x, self.P.PS
        if not hasattr(self, "_nrm_" + tag):
            setattr(self, "_nrm_" + tag, ([k.take(tag + "sq%d" % i, [128, 512], BF16) for i in range(2)],
                                         [k.take(tag + "rstd%d" % i, [128, 512], F32) for i in range(2)]))
        sq, rstd = getattr(self, "_nrm_" + tag)
        ts = slice(n * 512, (n + 1) * 512)
        ps = PS[n % 2]
        for kt in range(KT):
            s = sq[kt % 2]
            k.act(s.v, x[:, kt, ts], AF.Square)
            k.mm(ps.v, self.ones.v, s.v, start=(kt == 0), stop=(kt == KT - 1))
        r = rstd[n % 2]
        k.ts(r.v, ps.v, 1.0 / D, EPS, op0=ALU.mult, op1=ALU.add)
        k.act(r.v, r.v, AF.Sqrt)
        k.recip(r.v, r.v)
        return r

    def rmsnorm(self, which, h, tag):
        k = self.k
        for n in range(NT):
            ts = slice(n * 512, (n + 1) * 512)
            r = self.rstd_tile(n, tag)
            for kt in range(KT):
                k.stt(h[:, kt, ts], self.x[:, kt, ts], self.gn[:, which, kt:kt + 1], r.v, ALU.mult, ALU.mult)

    def norm_out(self, which, name):
        k = self.k
        mark = k.aoff
        h = k.take("h_" + name, [128, KT, T], BF16)
        self.rmsnorm(which, h, "no" + name)
        out = self.P.dout(name, [KT, 128, T], BF16)
        for kt in range(KT):
            self.P.out_ops.append(k.dma(out[kt], h[:, kt, :], q="sp" if kt % 2 == 0 else "act"))
        k.barrier()
        k.aoff = mark

    def final_norm(self):
        k = self.k
        mark = k.aoff
        for n in range(NT):
            ts = slice(n * 512, (n + 1) * 512)
            r = self.rstd_tile(n, "fin")
            for kt in range(KT):
                k.stt(self.x[:, kt, ts], self.x[:, kt, ts], self.gn[:, 8, kt:kt + 1], r.v, ALU.mult, ALU.mult)
        k.barrier()
        k.aoff = mark

    def proj_gated(self, name, src, nk, nmo, combine, w2=None):
        k, PS = self.k, self.P.PS
        wd = self.P.din(name, [nmo, 128, nk * 256])
        if w2 is None:
            w2 = [k.take(name + "w%d" % i, [128, nk, 2, 128], BF16) for i in range(2)]
        cnt = 0
        for mo in range(nmo):
            w = w2[mo % 2]
            k.dma(w.v.rr("p a b c -> p (a b c)"), wd[mo], q="pool")
            for n in range(NT):
                ts = slice(n * 512, (n + 1) * 512)
                pa = PS[2 + 2 * (cnt % 2)]
                pb = PS[3 + 2 * (cnt % 2)]
                for kt in range(nk):
                    k.mm(pa.v, w[:, kt, 0, :], src[:, kt, ts], start=(kt == 0), stop=(kt == nk - 1))
                for kt in range(nk):
                    k.mm(pb.v, w[:, kt, 1, :], src[:, kt, ts], start=(kt == 0), stop=(kt == nk - 1))
                combine(mo, n, ts, pa, pb, cnt)
                cnt += 1

    def proj_acc(self, name, src, nk, nmo, sink, w2=None):
        k, PS = self.k, self.P.PS
        wd = self.P.din(name, [nmo, 128, nk * 128])
        if w2 is None:
            w2 = [k.take(name + "w%d" % i, [128, nk, 128], BF16) for i in range(2)]
        cnt = 0
        for mo in range(nmo):
            w = w2[mo % 2]
            k.dma(w.v.rr("p a b -> p (a b)"), wd[mo], q="pool")
            for n in range(NT):
                ts = slice(n * 512, (n + 1) * 512)
                po = PS[6 + (cnt % 2)]
                for kt in range(nk):
                    k.mm(po.v, w[:, kt, :], src[:, kt, ts], start=(kt == 0), stop=(kt == nk - 1))
                sink(mo, n, ts, po, cnt)
                cnt += 1

    def add_to_x(self, mo, n, ts, po, cnt):
        self.k.tt(self.x[:, mo, ts], self.x[:, mo, ts], po.v, ALU.add)

    def ffn(self, l):
        k = self.k
        mark = k.aoff
        HG = FM // 2
        h = k.take("h", [128, KT, T], BF16)
        a = k.take("a", [128, HG, T], BF16)
        sg = [k.take("sg%d" % i, [128, 512], F32) for i in range(2)]
        self.rmsnorm(4 + l, h, "ffn%d" % l)
        wg2 = [k.take("wg2_%d" % i, [128, KT, 2, 128], BF16) for i in range(2)]
        wd2 = [k.take("wd2_%d" % i, [128, HG, 128], BF16) for i in range(2)]
        for grp in range(2):
            def comb(mi, n, ts, pg, pu, cnt):
                s = sg[cnt % 2]
                k.act(s.v, pg.v, AF.Silu)
                k.tt(a[:, mi, ts], s.v, pu.v, ALU.mult)
            self.proj_gated("wgu%d_%d" % (l, grp), h, KT, HG, comb, wg2)
            self.proj_acc("wdn%d_%d" % (l, grp), a, HG, KT, self.add_to_x, wd2)
        k.barrier()
        k.aoff = mark

    def s5post(self, idx):
        k = self.k
        mark = k.aoff
        zin = self.P.din("zf", [KT, 128, T], BF16)
        zf = k.take("zf", [128, KT, T], BF16)
        for kt in range(KT):
            k.dma(zf[:, kt, :], zin[kt], q="sp" if kt % 2 == 0 else "act")
        sg = [k.take("sgl%d" % i, [128, 512], F32) for i in range(2)]

        def comb(mo, n, ts, pa, pb, cnt):
            s = sg[cnt % 2]
            k.act(s.v, pb.v, AF.Sigmoid)
            k.tt(s.v, s.v, pa.v, ALU.mult)
            k.tt(self.x[:, mo, ts], self.x[:, mo, ts], s.v, ALU.add)
        self.proj_gated("s5_%d_wglu" % idx, zf, KT, KT, comb)
        k.barrier()
        k.aoff = mark


STEPS = {}


def build(cfg):
    P = Prog(cfg)
    try:
        if cfg["kind"] == "tok":
            tk = Tok(P)
            for st in cfg["steps"]:
                nm, _, arg = st.partition(":")
                if nm == "norm_out":
                    which, name = arg.split(",")
                    tk.norm_out(int(which), name)
                elif nm == "ffn":
                    tk.ffn(int(arg))
                elif nm == "s5post":
                    tk.s5post(int(arg))
                elif nm == "final":
                    tk.final_norm()
                elif nm == "store_x":
                    tk.store_x(arg or "outT")
                else:
                    STEPS[nm](tk, arg)
        else:
            STEPS[cfg["kind"]](P, cfg)
    except StopBuild:
        pass
    return P.finish()


def _pair_tiles(w, nk, nmo):
    w = w.reshape(nk, 128, 2, nmo, 128).transpose(3, 1, 0, 2, 4)
    return np.ascontiguousarray(w).reshape(nmo, 128, nk * 256)


def _acc_tiles(w, nk, nmo):
    w = w.reshape(nk, 128, nmo, 128).transpose(2, 1, 0, 3)
    return np.ascontiguousarray(w).reshape(nmo, 128, nk * 128)


class Host:
    def __init__(self, inp):
        f = np.float32
        self.inp = inp
        g = np.stack([np.asarray(inp["norm_mix"], f)[i] for i in range(4)]
                     + [np.asarray(inp["norm_ffn"], f)[i] for i in range(4)]
                     + [np.asarray(inp["norm_final"], f)], 0)
        self.common = {"gains": np.ascontiguousarray(g.reshape(9, KT, 128).transpose(2, 0, 1))}
        self.percore = [dict() for _ in range(NCORE)]
        X = np.asarray(inp["x"], f)[0]
        for c in range(NCORE):
            self.percore[c]["xT"] = np.ascontiguousarray(X[c * T:(c + 1) * T].T).reshape(KT, 128, T)
        for fn in HOST_EXTRA:
            fn(self)

    def weight(self, name):
        f = np.float32
        inp = self.inp
        if name.startswith("wgu"):
            l, grp = int(name[3]), int(name[5])
            w = np.asarray(inp["ffn_w_gate_up"], f)[l]
            HG = FM // 2
            cols = np.concatenate([np.arange(grp * HG * 128, (grp + 1) * HG * 128),
                                   FH + np.arange(grp * HG * 128, (grp + 1) * HG * 128)])
            return _pair_tiles(w[:, cols], KT, HG)
        if name.startswith("wdn"):
            l, grp = int(name[3]), int(name[5])
            w = np.asarray(inp["ffn_w_down"], f)[l]
            HG = FM // 2
            return _acc_tiles(w[grp * HG * 128:(grp + 1) * HG * 128], HG, KT)
        if name.startswith("s5_") and name.endswith("wglu"):
            idx = int(name[3])
            return _pair_tiles(np.asarray(inp["s5_w_glu"], f)[idx], KT, KT)
        for pre, fn in WEIGHT_EXTRA.items():
            if name.startswith(pre):
                return fn(self, name)
        raise KeyError(name)

    def set(self, name, per_core_list):
        for c in range(NCORE):
            self.percore[c][name] = per_core_list[c]

    def get(self, name, c):
        if name in self.percore[c]:
            return self.percore[c][name]
        if name not in self.common:
            self.common[name] = self.weight(name)
        return self.common[name]


_PROGS = {}


def launch(host, cfg):
    key = repr(sorted(cfg.items(), key=str))
    if key not in _PROGS:
        _PROGS[key] = build(cfg)
    P = _PROGS[key]
    maps = [{n: host.get(n, c) for n in P.ins} for c in range(NCORE)]
    res = run_bass_kernel_spmd(P.nc, maps, core_ids=list(range(NCORE)))
    return [res.results[c] for c in range(NCORE)]


def tok_to_chan(outs, name):
    return [np.ascontiguousarray(np.concatenate([np.asarray(outs[r][name])[c] for r in range(NCORE)], axis=1))
            for c in range(NCORE)]


def chan_to_tok(outs, name):
    return [np.ascontiguousarray(np.stack([np.asarray(outs[c][name])[:, r * T:(r + 1) * T] for c in range(NCORE)], 0))
            for r in range(NCORE)]


HOST_EXTRA = []
WEIGHT_EXTRA = {}


import math as _math
ATT_LAYER = 2
LAMBDA_INIT = 0.8 - 0.6 * _math.exp(-0.3 * ATT_LAYER)
ROPE_THETA_ = 500000.0


def _sin_reduced(k, dst, src, tmpi, tmpf, scr):
    k.ts(tmpi, src, 1.0 / TWO_PI, None, op0=ALU.mult)
    k.copy(scr, tmpi)
    k.stt(scr, scr, -TWO_PI, src, ALU.mult, ALU.add)
    k.ts(tmpf, scr, PI, TWO_PI, op0=ALU.is_gt, op1=ALU.mult)
    k.tt(scr, scr, tmpf, ALU.subtract)
    k.ts(tmpf, scr, -PI, -TWO_PI, op0=ALU.is_lt, op1=ALU.mult)
    k.tt(scr, scr, tmpf, ALU.subtract)
    k.act(dst, scr, AF.Sin)


def attn_pre(tk, arg):
    P, k, PS = tk.P, tk.k, tk.P.PS
    mark = k.aoff
    h = k.take("h", [128, KT, T], BF16)
    tk.rmsnorm(2, h, "att")
    o_q = P.dout("aqT", [8, 128, T], BF16)
    o_k = P.dout("akT", [8, 128, T], BF16)
    o_v = P.dout("avtm", [16, 128, 1024], BF16)
    d_pos = P.din("pos_rep", [128, T], I32)
    d_invf = P.din("rope_invf", [128, 1])
    d_perm = P.din("rope_perm", [128, 128])
    invf = k.take("invf", [128, 1], F32)
    perm = k.take("perm", [128, 128], BF16)
    k.dma(invf.v, d_invf.v)
    k.dma(perm.v, d_perm.v, q="pool")
    COS = k.take("COS", [128, T], F32)
    SIN = k.take("SIN", [128, T], F32)
    COSq = k.take("COSq", [128, T], F32)
    SINq = k.take("SINq", [128, T], F32)
    m2 = k.aoff
    posi = k.take("posi", [128, T], I32)
    k.dma(posi.v, d_pos.v)
    ang = k.take("ang", [128, T], F32)
    tmpf = k.take("rtmp", [128, T], F32)
    scr = k.take("rscr", [128, T], F32)
    tmpi = V(tmpf, tmpf.ap0.bitcast(I32))
    k.copy(ang.v, posi.v)
    k.ts(ang.v, ang.v, invf[:, 0:1], None, op0=ALU.mult)
    _sin_reduced(k, SIN.v, ang.v, tmpi, tmpf.v, scr.v)
    k.ts(ang.v, ang.v, 0.5 * PI, None, op0=ALU.add)
    _sin_reduced(k, COS.v, ang.v, tmpi, tmpf.v, scr.v)
    k.ts(COSq.v, COS.v, 0.125, None, op0=ALU.mult)
    k.ts(SINq.v, SIN.v, 0.125, None, op0=ALU.mult)
    k.barrier()
    k.aoff = m2
    P.checkpoint("a_tab", COS.v)
    stq = [k.take("astq%d" % i, [128, T], BF16) for i in range(2)]
    qb = [k.take("aqb%d" % i, [128, 512], BF16) for i in range(2)]
    t1 = [k.take("at1%d" % i, [128, 512], F32) for i in range(2)]
    t2 = [k.take("at2%d" % i, [128, 512], F32) for i in range(2)]

    def sink_qk(mo, n, ts, po, cnt):
        st = stq[mo % 2]
        b = qb[cnt % 2]
        k.copy(b.v, po.v, eng="act")
        pp = PS[cnt % 2]
        k.mm(pp.v, perm.v, b.v)
        c_, s_ = (COSq, SINq) if mo < 8 else (COS, SIN)
        a1, a2 = t1[cnt % 2], t2[cnt % 2]
        k.tt(a1.v, b.v, c_[:, ts], ALU.mult)
        k.tt(a2.v, pp.v, s_[:, ts], ALU.mult)
        k.tt(st[:, ts], a1.v, a2.v, ALU.add, eng="pool")
        if n == NT - 1:
            dst = o_q[mo] if mo < 8 else o_k[mo - 8]
            P.out_ops.append(k.dma(dst, st.v, q="sp"))
    tk.proj_acc("att_wqk", h, KT, 16, sink_qk)
    P.checkpoint("a_qk", stq[1].v)
    d_wv = P.din("att_wv", [2, 128, KT * 512])
    wv = [k.take("awv%d" % i, [128, KT, 512], BF16) for i in range(2)]
    tst = [k.take("atst%d" % i, [128, 512], BF16) for i in range(4)]
    cnt = 0
    for ci in range(2):
        w = wv[ci]
        k.dma(w.v.rr("p a b -> p (a b)"), d_wv[ci], q="pool")
        for tt in range(16):
            ps = PS[2 + cnt % 2]
            for kt in range(KT):
                k.mm(ps.v, h[:, kt, tt * 128:(tt + 1) * 128], w[:, kt, :], start=(kt == 0), stop=(kt == KT - 1))
            st = tst[cnt % 4]
            k.copy(st.v, ps.v, eng="act" if cnt % 2 == 0 else "dve")
            P.out_ops.append(k.dma(o_v[tt][:, ci * 512:(ci + 1) * 512], st.v, q="sp" if cnt % 2 == 0 else "act"))
            cnt += 1
    k.barrier()
    k.aoff = mark


STEPS["attnpre"] = attn_pre


def attn_core(P, cfg):
    k, PS = P.k, P.PS
    d_q = P.din("a_qT", [128, SEQ], BF16)
    d_k = P.din("a_kT", [128, SEQ], BF16)
    d_v = P.din("a_v", [128, 128 * 128], BF16)
    d_lam = P.din("a_lamv", [128, 4, 64])
    d_sg = P.din("a_subg", [128, 1])
    d_pq = P.din("a_posq", [128, 512], I32)
    d_pk = P.din("a_posk", [128, 4], I32)
    d_o = P.dout("aoT", [128, SEQ], BF16)
    Q = k.take("Q", [128, SEQ], BF16)
    K_ = k.take("K", [128, SEQ], BF16)
    Vv = k.take("V", [128, 128, 128], BF16)
    oT = k.take("oT", [128, 2, 512], BF16)
    ones = k.take("ones", [128, 128], BF16)
    k.memset(ones.v, 1.0)
    for i in range(4):
        sl = slice(i * 4096, (i + 1) * 4096)
        k.dma(Q[:, sl], d_q[:, sl], q="sp")
        k.dma(K_[:, sl], d_k[:, sl], q="act")
        k.dma(Vv.v.rr("p a b -> p (a b)")[:, sl], d_v[:, sl], q="sp")
    lv = k.take("lv", [128, 4, 64], F32)
    sgl = k.take("sgl", [128, 1], F32)
    k.dma(lv.v, d_lam.v)
    k.dma(sgl.v, d_sg.v)
    lp = k.take("lp", [128, 2, 64], F32)
    ls = k.take("ls", [128, 2], F32)
    lam = k.take("lam", [128, 1], F32)
    k.tt(lp[:, 0, :], lv[:, 0, :], lv[:, 1, :], ALU.mult)
    k.tt(lp[:, 1, :], lv[:, 2, :], lv[:, 3, :], ALU.mult)
    k.ts(lp[:, 0, :], lp[:, 0, :], 1.0, 0.0, op0=ALU.mult, op1=ALU.add, accum=ls[:, 0:1])
    k.ts(lp[:, 1, :], lp[:, 1, :], 1.0, 0.0, op0=ALU.mult, op1=ALU.add, accum=ls[:, 1:2])
    k.act(ls.v, ls.v, AF.Exp)
    k.tt(lam.v, ls[:, 0:1], ls[:, 1:2], ALU.subtract)
    k.ts(lam.v, lam.v, LAMBDA_INIT, None, op0=ALU.add)
    k.ts(sgl.v, sgl.v, 1.0 - LAMBDA_INIT, None, op0=ALU.mult)
    pq = k.take("pq", [128, 512], I32)
    pk = k.take("pk", [128, 4], I32)
    k.dma(pq.v, d_pq.v)
    k.dma(pk.v, d_pk.v)
    k.ts(pq.v, pq.v, 6, None, op0=ALU.arith_shift_right)
    k.ts(pk.v, pk.v, 6, None, op0=ALU.arith_shift_right)
    cq = k.take("cq", [128, 512], F32)
    ck = k.take("ck", [128, 4], F32)
    k.copy(cq.v, pq.v)
    k.copy(ck.v, pk.v)
    M = [k.take("M%d" % t, [128, 512], BF16) for t in range(4)]
    for t in range(4):
        k.ts(M[t].v, cq.v, ck[:, t:t + 1], None, op0=ALU.is_ge)
    E = [k.take("E%d" % i, [128, 2, 512], BF16) for i in range(3)]
    M2 = [k.take("M2_%d" % t, [128, 2, 512], BF16) for t in range(4)]
    for t in range(4):
        k.copy(M2[t][:, 0, :], M[t].v)
        k.copy(M2[t][:, 1, :], M[t].v)
    rinv = [[k.take("rinv%d_%d" % (i, s_), [128, 512], F32) for s_ in range(2)] for i in range(2)]
    Os = [[k.take("Os%d_%d" % (i, s_), [128, 512], F32) for s_ in range(2)] for i in range(2)]
    ofs = [k.take("of%d" % i, [128, 512], F32) for i in range(2)]
    sqb = [k.take("sqb%d" % i, [128, 512], BF16) for i in range(2)]
    pending = {}
    Eacc = [k.take("Eacc%d" % i, [128, 512], F32) for i in range(2)]
    ones32 = k.take("ones32", [128, 128], F32)
    k.memset(ones32.v, 1.0)
    O = [PS[4], PS[5]]
    R = [PS[6], PS[7]]
    steps = [(qi, kj) for qi in range(SEQ // 512) for kj in range(4 * qi + 4)]

    def emit_qk(si):
        qi, kj = steps[si]
        qs = slice(qi * 512, (qi + 1) * 512)
        ks = slice(kj * 128, (kj + 1) * 128)
        for s in range(2):
            rows = slice(64 * s, 64 * s + 64)
            k.mm(P.PS2[si % 2][:, s * 512:(s + 1) * 512], K_[rows, ks], Q[rows, qs])

    emit_qk(0)
    for si, (qi, kj) in enumerate(steps):
        qs = slice(qi * 512, (qi + 1) * 512)
        nk = 4 * qi + 4
        if si + 1 < len(steps):
            emit_qk(si + 1)
        Ex = E[si % 3]
        k.act(Ex.v.rr("p a b -> p (a b)"), P.PS2[si % 2].v, AF.Exp)
        if kj >= 4 * qi:
            k.tt(Ex.v, Ex.v, M2[kj - 4 * qi].v, ALU.mult)
        for s in range(2):
            k.mm(O[s].v, Vv[:, kj, :], Ex[:, s, :], start=(kj == 0), stop=(kj == nk - 1))
            if s == 0:
                if kj == 0:
                    k.copy(Eacc[0].v, Ex[:, 0, :])
                else:
                    k.tt(Eacc[0].v, Eacc[0].v, Ex[:, 0, :], ALU.add)
            else:
                k.mm(R[1].v, ones.v, Ex[:, 1, :], start=(kj == 0), stop=(kj == nk - 1))
        for fn in pending.pop(si, []):
            fn()
        if kj != nk - 1:
            continue
        par = qi % 2
        Osb = [Os[par][0], Os[par][1]]
        rv = [rinv[par][0], rinv[par][1]]
        k.mm(R[0].v, ones32.v, Eacc[0].v)
        k.copy(Osb[0].v, O[0].v, eng="act")
        k.copy(Osb[1].v, O[1].v, eng="act")
        k.act(rv[1].v, R[1].v, AF.Ln)
        k.act(rv[0].v, R[0].v, AF.Ln)
        k.act(rv[1].v, rv[1].v, AF.Exp, scale=-1.0)
        k.act(rv[0].v, rv[0].v, AF.Exp, scale=-1.0)

        def stage_b(qi=qi, par=par, Osb=Osb, rv=rv):
            of = ofs[par]
            k.tt(of.v, Osb[0].v, rv[0].v, ALU.mult)
            k.stt(Osb[1].v, Osb[1].v, lam[:, 0:1], rv[1].v, ALU.mult, ALU.mult)
            k.tt(of.v, of.v, Osb[1].v, ALU.subtract)
            k.act(sqb[par].v, of.v, AF.Square)
            k.mm(R[0].v, ones.v, sqb[par].v)
            k.ts(rv[0].v, R[0].v, 1.0 / 128.0, 1e-5, op0=ALU.mult, op1=ALU.add)

        def stage_c(qi=qi, par=par, rv=rv):
            of = ofs[par]
            qs_ = slice(qi * 512, (qi + 1) * 512)
            k.act(rv[0].v, rv[0].v, AF.Ln)
            k.act(rv[0].v, rv[0].v, AF.Exp, scale=-0.5)
            ob = oT[:, par, :]
            k.stt(ob, of.v, sgl[:, 0:1], rv[0].v, ALU.mult, ALU.mult)
            P.out_ops.append(k.dma(d_o[:, qs_], ob, q="sp" if par == 0 else "act"))
        if si + 4 < len(steps):
            pending.setdefault(si + 2, []).append(stage_b)
            pending.setdefault(si + 4, []).append(stage_c)
        else:
            stage_b()
            stage_c()


STEPS["attncore"] = attn_core


def attn_post(tk, arg):
    P, k = tk.P, tk.k
    mark = k.aoff
    d_o = P.din("aoTt", [8, 128, T], BF16)
    o = k.take("ao", [128, 8, T], BF16)
    for kt in range(8):
        k.dma(o[:, kt, :], d_o[kt], q="sp" if kt % 2 == 0 else "act")
    tk.proj_acc("att_wo", o, KT, KT, tk.add_to_x)
    k.barrier()
    k.aoff = mark


STEPS["attnpost"] = attn_post


def host_attn(host):
    f = np.float32
    inp = host.inp
    w = np.asarray(inp["diff_w_qkv"], f)[0]
    host.common["att_wqk"] = _acc_tiles(w[:, 0:2048], KT, 16)
    host.common["att_wv"] = np.ascontiguousarray(w[:, 2048:3072].reshape(KT, 128, 2, 512).transpose(2, 1, 0, 3)).reshape(2, 128, KT * 512)
    host.common["att_wo"] = _acc_tiles(np.asarray(inp["diff_w_o"], f)[0], KT, KT)
    d = np.arange(128) % 64
    half = 8
    invf = np.where(d < 16, ROPE_THETA_ ** (-(d % half).astype(np.float64) / half), 0.0).astype(f)
    host.common["rope_invf"] = np.ascontiguousarray(invf[:, None])
    perm = np.zeros((128, 128), f)
    for p in range(128):
        dd = p % 64
        if dd < 8:
            perm[p + 8, p] = -1.0
        elif dd < 16:
            perm[p - 8, p] = 1.0
    host.common["rope_perm"] = perm
    pos = np.asarray(inp["positions"])[0].astype(np.int32)
    for c in range(NCORE):
        host.percore[c]["pos_rep"] = np.ascontiguousarray(np.broadcast_to(pos[None, c * T:(c + 1) * T], (128, T)))
        lamv = np.stack([np.asarray(inp[n], f)[0] for n in ("diff_lam_q1", "diff_lam_k1", "diff_lam_q2", "diff_lam_k2")], 0)
        host.percore[c]["a_lamv"] = np.ascontiguousarray(np.broadcast_to(lamv[None], (128, 4, 64)))
        host.percore[c]["a_subg"] = np.ascontiguousarray(np.asarray(inp["diff_subln"], f)[0][:, None])
        host.percore[c]["a_posq"] = np.ascontiguousarray(np.broadcast_to(pos[None, 0:512], (128, 512)))
        host.percore[c]["a_posk"] = np.ascontiguousarray(pos[0:512].reshape(4, 128).T)


HOST_EXTRA.append(host_attn)


GLA_H = 4
GLA_DKH = 128
GLA_DVH = 256


def gla_pre(tk, arg):
    P, k, PS = tk.P, tk.k, tk.P.PS
    mark = k.aoff
    h = k.take("h", [128, KT, T], BF16)
    tk.rmsnorm(1, h, "gla")
    o_q = P.dout("qT", [4, 128, T], BF16)
    o_g = P.dout("gT", [8, 128, T], BF16)
    o_k = P.dout("ktm", [16, 128, 512], BF16)
    o_v = P.dout("vtm", [16, 128, 1024], BF16)
    o_la = P.dout("latm", [16, 128, 512], BF16)
    stq = [k.take("stq%d" % i, [128, T], BF16) for i in range(2)]

    def sink_qg(mo, n, ts, po, cnt):
        st = stq[mo % 2]
        k.copy(st[:, ts], po.v, eng="act" if cnt % 2 == 0 else "dve")
        if n == NT - 1:
            dst = o_q[mo] if mo < 4 else o_g[mo - 4]
            P.out_ops.append(k.dma(dst, st.v, q="sp"))
    tk.proj_acc("gla_wqg", h, KT, 12, sink_qg)

    d_wa = P.din("gla_walo", [128, KT * 16])
    wa = k.take("wa", [128, KT, 16], BF16)
    k.dma(wa.v.rr("p a b -> p (a b)"), d_wa.v, q="pool")
    alo = k.take("alo", [16, T], BF16)
    for n in range(NT):
        ts = slice(n * 512, (n + 1) * 512)
        ps = PS[n % 2]
        for kt in range(KT):
            k.mm(ps[0:16, :], wa[:, kt, :], h[:, kt, ts], start=(kt == 0), stop=(kt == KT - 1))
        k.copy(alo[:, ts], ps[0:16, :])

    d_wkv = P.din("gla_wkv", [3, 128, KT * 512])
    d_wa2 = P.din("gla_wa2", [16, 512])
    d_ba = P.din("gla_ba", [1, 512])
    wa2 = k.take("wa2", [16, 512], BF16)
    ba = k.take("ba", [1, 512], BF16)
    k.dma(wa2.v, d_wa2.v, q="pool")
    k.dma(ba.v, d_ba.v, q="pool")
    wkv = [k.take("wkv%d" % i, [128, KT, 512], BF16) for i in range(2)]
    tst = [k.take("tst%d" % i, [128, 512], BF16) for i in range(4)]
    cnt = 0
    for ci in range(3):
        w = wkv[ci % 2]
        k.dma(w.v.rr("p a b -> p (a b)"), d_wkv[ci], q="pool")
        for tt in range(16):
            ps = PS[2 + cnt % 2]
            for kt in range(KT):
                k.mm(ps.v, h[:, kt, tt * 128:(tt + 1) * 128], w[:, kt, :], start=(kt == 0), stop=(kt == KT - 1))
            st = tst[cnt % 4]
            k.copy(st.v, ps.v, eng="act" if cnt % 2 == 0 else "dve")
            dst = o_k[tt] if ci == 0 else o_v[tt][:, (ci - 1) * 512:ci * 512]
            P.out_ops.append(k.dma(dst, st.v, q="sp" if cnt % 2 == 0 else "act"))
            cnt += 1
    lt = [k.take("lt%d" % i, [128, 512], F32) for i in range(2)]
    for tt in range(16):
        ps = PS[4 + tt % 2]
        k.mm(ps.v, alo[:, tt * 128:(tt + 1) * 128], wa2.v, start=True, stop=False)
        k.mm(ps.v, tk.ones[0:1, 0:128], ba.v, start=False, stop=True)
        t_ = lt[tt % 2]
        k.act(t_.v, ps.v, AF.Exp, scale=-1.0)
        k.ts(t_.v, t_.v, 1.0, None, op0=ALU.add)
        k.act(t_.v, t_.v, AF.Ln)
        st = tst[tt % 4]
        k.ts(st.v, t_.v, -1.0 / 16.0, None, op0=ALU.mult)
        P.out_ops.append(k.dma(o_la[tt], st.v, q="sp" if tt % 2 == 0 else "act"))
    k.barrier()
    k.aoff = mark


STEPS["glapre"] = gla_pre


def gla_core(P, cfg):
    k, PS = P.k, P.PS
    d_q = P.din("g_qT", [128, SEQ], BF16)
    d_k = P.din("g_k", [128, 128 * 128], BF16)
    d_la = P.din("g_la", [128, 128 * 128], BF16)
    d_v = P.din("g_v", [128, 128 * 128], BF16)
    d_u2 = P.din("g_u2", [128, 128])
    d_ind = P.din("g_ind", [128, 2])
    d_o = P.dout("oT", [128, SEQ], BF16)
    q = k.take("q", [128, SEQ], BF16)
    kk = k.take("kk", [128, 128, 128], BF16)
    la = k.take("la", [128, 128, 128], BF16)
    v = k.take("v", [128, 128, 128], BF16)
    oT = k.take("oT", [128, SEQ], BF16)
    u2 = k.take("u2", [128, 128], BF16)
    ind = k.take("ind", [128, 2], BF16)
    state = k.take("state", [128, 128], F32)
    stb = [k.take("stb%d" % i, [128, 128], BF16) for i in range(2)]
    er = [k.take("er%d" % i, [128, 128], F32) for i in range(2)]
    kd = [k.take("kd%d" % i, [128, 128], BF16) for i in range(2)]
    dec = [k.take("dec%d" % i, [128, 2], F32) for i in range(2)]
    k.dma(u2.v, d_u2.v, q="pool")
    k.dma(ind.v, d_ind.v, q="pool")
    for i in range(4):
        sl = slice(i * 4096, (i + 1) * 4096)
        k.dma(q[:, sl], d_q[:, sl], q="sp")
        k.dma(kk.v.rr("p a b -> p (a b)")[:, sl], d_k[:, sl], q="act")
        k.dma(la.v.rr("p a b -> p (a b)")[:, sl], d_la[:, sl], q="sp")
        k.dma(v.v.rr("p a b -> p (a b)")[:, sl], d_v[:, sl], q="act")
    k.memset(state.v, 0.0)
    SC = float(GLA_DKH) ** -0.5
    def tile_prep(j):
        prv = PS[j % 2]
        ptt = PS[2 + j % 2]
        k.mm(prv[:, 0:128], u2.v, la[:, j, :])
        k.mm(ptt[:, 0:2], la[:, j, :], ind.v)
        e = er[j % 2]
        k.act(e.v, prv[:, 0:128], AF.Exp)
        k.tt(kd[j % 2].v, e.v, kk[:, j, :], ALU.mult)
        k.act(dec[j % 2].v, ptt[:, 0:2], AF.Exp)

    def upd(c):
        j, ch = c // 2, c % 2
        rows = slice(64 * ch, 64 * ch + 64)
        k.mm(PS[4 + c % 2][:, 0:128], kd[j % 2][rows, :], v[rows, j, :])

    tile_prep(0)
    upd(0)
    NCHK = 256
    for c in range(NCHK):
        j, ch = c // 2, c % 2
        if ch == 0 and j + 1 < 128:
            tile_prep(j + 1)
        if c + 1 < NCHK:
            upd(c + 1)
        po = PS[6 + (c // 8) % 2]
        k.stt(state.v, state.v, dec[j % 2][:, ch:ch + 1], PS[4 + c % 2][:, 0:128], ALU.mult, ALU.add)
        sb_ = stb[c % 2]
        k.copy(sb_.v, state.v, eng="act")
        off = (c % 8) * 64
        k.mm(po[:, off:off + 64], sb_.v, q[:, c * 64:(c + 1) * 64])
        if c % 8 == 7:
            n = c // 8
            k.ts(oT[:, n * 512:(n + 1) * 512], po.v, SC, None, op0=ALU.mult)
    for i in range(4):
        sl = slice(i * 4096, (i + 1) * 4096)
        P.out_ops.append(k.dma(d_o[:, sl], oT[:, sl], q="sp" if i % 2 == 0 else "act"))


STEPS["glacore"] = gla_core


def gla_post(tk, arg):
    P, k, PS = tk.P, tk.k, tk.P.PS
    mark = k.aoff
    d_o = P.din("goT", [8, 128, T], BF16)
    d_g = P.din("gT", [8, 128, T], BF16)
    d_ng = P.din("gla_ng", [128, 8])
    o = k.take("o", [128, 8, T], BF16)
    g = k.take("g", [128, 8, T], BF16)
    og = k.take("og", [128, 8, T], BF16)
    ng = k.take("ng", [128, 8], F32)
    k.dma(ng.v, d_ng.v)
    for kt in range(8):
        k.dma(o[:, kt, :], d_o[kt], q="sp")
        k.dma(g[:, kt, :], d_g[kt], q="act")
    sq = [k.take("gsq%d" % i, [128, 512], BF16) for i in range(2)]
    rs = [k.take("grs%d" % i, [128, 512], F32) for i in range(2)]
    sg = [k.take("gsg%d" % i, [128, 512], F32) for i in range(2)]
    cnt = 0
    for hd in range(4):
        for n in range(NT):
            ts = slice(n * 512, (n + 1) * 512)
            ps = PS[cnt % 2]
            for e in range(2):
                s = sq[e]
                k.act(s.v, o[:, 2 * hd + e, ts], AF.Square)
                k.mm(ps.v, tk.ones.v, s.v, start=(e == 0), stop=(e == 1))
            r = rs[cnt % 2]
            k.ts(r.v, ps.v, 1.0 / GLA_DVH, EPS, op0=ALU.mult, op1=ALU.add)
            k.act(r.v, r.v, AF.Sqrt)
            k.recip(r.v, r.v)
            for e in range(2):
                kt = 2 * hd + e
                s_ = sg[e]
                k.act(s_.v, g[:, kt, ts], AF.Silu)
                k.stt(s_.v, o[:, kt, ts], ng[:, kt:kt + 1], s_.v, ALU.mult, ALU.mult)
                k.tt(og[:, kt, ts], s_.v, r.v, ALU.mult)
            cnt += 1
    tk.proj_acc("gla_wo", og, KT, KT, tk.add_to_x)
    k.barrier()
    k.aoff = mark


STEPS["glapost"] = gla_post


def host_gla(host):
    f = np.float32
    inp = host.inp
    u2 = np.zeros((128, 128), f)
    for t2 in range(128):
        for t in range(128):
            if t2 > t and t2 // 64 == t // 64:
                u2[t2, t] = 1.0
    host.common["g_u2"] = u2
    host.common["g_ind"] = (np.arange(128)[:, None] // 64 == np.arange(2)[None, :]).astype(f)
    win = np.asarray(inp["gla_w_in"], f)[0]
    host.common["gla_wqg"] = _acc_tiles(np.concatenate([win[:, 0:512], win[:, 2048:3072]], 1), KT, 12)
    host.common["gla_walo"] = np.ascontiguousarray(win[:, 3072:3088].reshape(KT, 128, 16).transpose(1, 0, 2)).reshape(128, KT * 16)
    kv = win[:, 512:2048]
    host.common["gla_wkv"] = np.ascontiguousarray(kv.reshape(KT, 128, 3, 512).transpose(2, 1, 0, 3)).reshape(3, 128, KT * 512)
    host.common["gla_wa2"] = np.ascontiguousarray(np.asarray(inp["gla_w_a2"], f)[0])
    host.common["gla_ba"] = np.ascontiguousarray(np.asarray(inp["gla_b_a"], f)[0][None, :])
    host.common["gla_ng"] = np.ascontiguousarray(np.asarray(inp["gla_norm"], f)[0].reshape(8, 128).T)
    host.common["gla_wo"] = _acc_tiles(np.asarray(inp["gla_w_o"], f)[0], KT, KT)


HOST_EXTRA.append(host_gla)


def _tokmajor_to_core(outs, name, c0, ncol=128):
    a = np.concatenate([np.asarray(outs[r][name])[:, :, c0:c0 + ncol] for r in range(NCORE)], 0)
    return np.ascontiguousarray(a.transpose(1, 0, 2)).reshape(128, 128 * ncol)


TWO_PI = 6.283185307179586
PI = 3.141592653589793


def s5_core(P, idx, PS):
    k = P.k
    s5_consts(P)
    nm = "s5_%d_" % idx
    d_lamP = P.din(nm + "lamP", [64, 2, 8])
    d_dtP = P.din(nm + "dtP", [64, 8])
    d_bP = P.din(nm + "bP", [64, 2, 8, 16])
    d_cP = P.din(nm + "cP", [64, 2, 8, 16])
    d_lamR = P.din(nm + "lamR", [128, 2, 64])
    d_dtR = P.din(nm + "dtR", [128, 1])
    d_bR = P.din(nm + "bR", [128, 2, 64])
    d_dv = P.din(nm + "dvec", [128, 1])
    d_u = P.din("u", [128, SEQ], BF16)
    d_z = k.dram("z", [128, SEQ], BF16, kind="ExternalOutput")
    jt, bmask, mbd, ident = P.jtab, P.bmask, P.mbd, P.ident
    b0 = k.aoff
    oA = b0
    oB = oA + 8192
    oC = oB + 8192
    oD = oC + 2048
    oE = oD + 4096
    oF = oE + 4224
    assert oF + 6400 <= k.awords, (oF, k.awords)

    class Cur:
        def __init__(self, off):
            self.off = off

        def take(self, name, shape, dtype):
            b = k.take_at(name, shape, dtype, self.off)
            self.off += b.words
            return b

    u = k.take_at("u", [128, SEQ], BF16, oA)
    for i in range(8):
        k.dma(u[:, i * T:(i + 1) * T], d_u[:, i * T:(i + 1) * T], q="sp" if i % 2 == 0 else "act")

    def powtable(cur, np_, lam, ldt_ap, shape3, tag, lo, nl):
        a, b = shape3
        dt = cur.take(tag + "dt", [np_, a, b], F32)
        k.act(dt.v, ldt_ap, AF.Exp)
        lrdt = cur.take(tag + "lrdt", [np_, a, b], F32)
        lidt = cur.take(tag + "lidt", [np_, a, b], F32)
        k.tt(lrdt.v, lam[0], dt.v, ALU.mult)
        k.tt(lidt.v, lam[1], dt.v, ALU.mult)
        mag = cur.take(tag + "mag", [np_, a, nl, b], F32)
        Pre = cur.take(tag + "Pre", [np_, a, nl, b], F32)
        tmp = cur.take(tag + "tmp", [np_, a, nl, b], F32)
        Pim = cur.take(tag + "Pim", [np_, a, nl, b], F32)
        for ai in range(a):
            jv = V(jt, jt.ap0[0:np_, lo:lo + nl].unsqueeze(2).to_broadcast([np_, nl, b]))
            lr_b = V(lrdt, lrdt.ap0[:, ai, :].unsqueeze(1).to_broadcast([np_, nl, b]))
            li_b = V(lidt, lidt.ap0[:, ai, :].unsqueeze(1).to_broadcast([np_, nl, b]))
            k.tt(mag[:, ai, :, :], jv, lr_b, ALU.mult)
            k.tt(Pre[:, ai, :, :], jv, li_b, ALU.mult)
        magf = mag.v.rr("p a l b -> p (a l b)")
        tmpf = tmp.v.rr("p a l b -> p (a l b)")
        pref = Pre.v.rr("p a l b -> p (a l b)")
        pimf = Pim.v.rr("p a l b -> p (a l b)")
        k.act(magf, magf, AF.Exp)
        scr = cur.take(tag + "scr", [np_, a, nl, b], F32)
        scrf = scr.v.rr("p a l b -> p (a l b)")
        tmpi = V(tmp, tmp.ap0.bitcast(I32).rearrange("p a l b -> p (a l b)"))

        def reduce_sin(dst, src):
            k.ts(tmpi, src, 1.0 / TWO_PI, None, op0=ALU.mult)
            k.copy(scrf, tmpi)
            k.stt(scrf, scrf, -TWO_PI, src, ALU.mult, ALU.add)
            k.ts(tmpf, scrf, PI, TWO_PI, op0=ALU.is_gt, op1=ALU.mult)
            k.tt(scrf, scrf, tmpf, ALU.subtract)
            k.ts(tmpf, scrf, -PI, -TWO_PI, op0=ALU.is_lt, op1=ALU.mult)
            k.tt(scrf, scrf, tmpf, ALU.subtract)
            k.act(dst, scrf, AF.Sin)
        reduce_sin(pimf, pref)
        k.tt(pimf, pimf, magf, ALU.mult)
        k.ts(pref, pref, 0.5 * PI, None, op0=ALU.add)
        reduce_sin(pref, pref)
        k.tt(pref, pref, magf, ALU.mult)
        return Pre, Pim

    def bbar(cur, np_, lam, abre, abim, bre, bim, shp, tag, bcast, out_re, out_im):
        a, b = lam[0].ap.shape[1], lam[0].ap.shape[2]
        t = [cur.take(tag + "f%d" % i, [np_, a, b], F32) for i in range(5)]
        lr, li = lam
        k.tt(t[0].v, lr, lr, ALU.mult)
        k.tt(t[1].v, li, li, ALU.mult)
        k.tt(t[0].v, t[0].v, t[1].v, ALU.add)
        k.recip(t[0].v, t[0].v)
        k.ts(t[1].v, abre, -1.0, None, op0=ALU.add)
        k.tt(t[2].v, t[1].v, lr, ALU.mult)
        k.tt(t[3].v, abim, li, ALU.mult)
        k.tt(t[2].v, t[2].v, t[3].v, ALU.add)
        k.tt(t[2].v, t[2].v, t[0].v, ALU.mult)
        k.tt(t[3].v, abim, lr, ALU.mult)
        k.tt(t[4].v, t[1].v, li, ALU.mult)
        k.tt(t[3].v, t[3].v, t[4].v, ALU.subtract)
        k.tt(t[3].v, t[3].v, t[0].v, ALU.mult)
        tm = cur.take(tag + "bbt", shp, F32)
        fre, fim = bcast(t[2]), bcast(t[3])
        k.tt(out_re.v, fre, bre, ALU.mult)
        k.tt(tm.v, fim, bim, ALU.mult)
        k.tt(out_re.v, out_re.v, tm.v, ALU.subtract)
        k.tt(out_im.v, fre, bim, ALU.mult)
        k.tt(tm.v, fim, bre, ALU.mult)
        k.tt(out_im.v, out_im.v, tm.v, ALU.add)

    cB = Cur(oB)
    XT = [cB.take("XT%d" % i, [128, 64, 64], BF16) for i in range(2)]
    c2 = Cur(oD)
    lamR = c2.take("lamR", [128, 2, 64], F32)
    dtR = c2.take("dtR", [128, 1], F32)
    bR = c2.take("bR", [128, 2, 64], F32)
    bbRre = c2.take("bbRre", [128, 1, 64], F32)
    bbRim = c2.take("bbRim", [128, 1, 64], F32)
    k.dma(lamR.v, d_lamR.v)
    k.dma(dtR.v, d_dtR.v)
    k.dma(bR.v, d_bR.v)
    lamRv = (lamR[:, 0:1, :], lamR[:, 1:2, :])
    dtRb = V(dtR, dtR.ap0.unsqueeze(2).to_broadcast([128, 1, 64]))
    c2b = Cur(c2.off)
    P1re, P1im = powtable(c2b, 128, lamRv, dtRb, (1, 64), "R1", 1, 1)
    bbar(c2b, 128, lamRv, P1re[:, :, 0, :], P1im[:, :, 0, :], bR[:, 0:1, :], bR[:, 1:2, :], [128, 1, 64], "R", lambda t: t.v, bbRre, bbRim)
    k.barrier()
    for hf in range(2):
        c2c = Cur(c2.off)
        PreR, PimR = powtable(c2c, 128, lamRv, dtRb, (1, 64), "R%d" % hf, 32 * hf, 32)
        t1 = c2c.take("Rt1", [128, 32, 64], F32)
        t2 = c2c.take("Rt2", [128, 32, 64], F32)
        bre_b = V(bbRre, bbRre.ap0[:, 0, :].unsqueeze(1).to_broadcast([128, 32, 64]))
        bim_b = V(bbRim, bbRim.ap0[:, 0, :].unsqueeze(1).to_broadcast([128, 32, 64]))
        ls = slice(32 * hf, 32 * hf + 32)
        k.tt(t1.v, PreR[:, 0, :, :], bre_b, ALU.mult)
        k.tt(t2.v, PimR[:, 0, :, :], bim_b, ALU.mult)
        k.tt(XT[0][:, ls, :], t1.v, t2.v, ALU.subtract)
        k.tt(t1.v, PreR[:, 0, :, :], bim_b, ALU.mult)
        k.tt(t2.v, PimR[:, 0, :, :], bre_b, ALU.mult)
        k.tt(XT[1][:, ls, :], t1.v, t2.v, ALU.add)
        k.barrier()

    P.checkpoint("tabR", XT[0].v.rr("p l q -> p (l q)"))
    cC = Cur(oC)
    lamP = cC.take("lamP", [64, 2, 8], F32)
    dtP = cC.take("dtP", [64, 8], F32)
    bP = cC.take("bP", [64, 2, 8, 16], F32)
    cP = cC.take("cP", [64, 2, 8, 16], F32)
    dv = cC.take("dv", [128, 1], F32)
    bbPre = cC.take("bbPre", [64, 8, 16], F32)
    bbPim = cC.take("bbPim", [64, 8, 16], F32)
    A64 = cC.take("A64", [64, 3, 8], F32)
    cb = cC.take("cb", [64, 2, 128], BF16)
    assert cC.off <= oD
    k.dma(lamP.v, d_lamP.v)
    k.dma(dtP.v, d_dtP.v)
    k.dma(bP.v.rr("p a b c -> p (a b c)"), d_bP.v.rr("p a b c -> p (a b c)"))
    k.dma(cP.v.rr("p a b c -> p (a b c)"), d_cP.v.rr("p a b c -> p (a b c)"))
    k.dma(dv.v, d_dv.v)
    lamPv = (lamP[:, 0, :].rr("p (g o) -> p g o", o=1), lamP[:, 1, :].rr("p (g o) -> p g o", o=1))
    dtPv = dtP.v.rr("p (g o) -> p g o", o=1)
    c3 = Cur(oE)
    PreP, PimP = powtable(c3, 64, lamPv, dtPv, (8, 1), "P", 0, 65)
    k.copy(A64[:, 0, :], PreP[:, :, 64, 0])
    k.copy(A64[:, 1, :], PimP[:, :, 64, 0])
    k.ts(A64[:, 2, :], PimP[:, :, 64, 0], -1.0, None, op0=ALU.mult)
    bbar(c3, 64, lamPv, PreP[:, :, 1, :], PimP[:, :, 1, :], bP[:, 0, :, :], bP[:, 1, :, :], [64, 8, 16], "P",
         lambda t: V(t, t.ap0[:, :, 0].unsqueeze(2).to_broadcast([64, 8, 16])), bbPre, bbPim)
    k.copy(cb[:, 0, :], cP[:, 0, :, :].rr("p g i -> p (g i)"))
    k.ts(cb[:, 1, :], cP[:, 1, :, :].rr("p g i -> p (g i)"), -1.0, None, op0=ALU.mult)
    PreQ = cC.take("PreQ", [64, 8, 65, 1], F32)
    PimQ = cC.take("PimQ", [64, 8, 65, 1], F32)
    assert cC.off <= oD, (cC.off, oD)
    k.copy(PreQ.v, PreP.v)
    k.copy(PimQ.v, PimP.v)
    k.barrier()
    P.checkpoint("tabP", PreQ.v.rr("p g l o -> p (g l o)"))

    Kb = k.take_at("Kb", [128, 64, 128], BF16, oD)
    c6 = Cur(oE)
    kt0 = c6.take("kt0", [128, 128], F32)
    Xc = [[c6.take("Xc%d_%d" % (i, ri), [64, 16, 8, 16], BF16) for ri in range(2)] for i in range(2)]
    xt1 = c6.take("xt1", [64, 16, 8, 16], F32)
    xt2 = c6.take("xt2", [64, 16, 8, 16], F32)
    for qq in range(4):
        X = Xc[qq % 2]
        lo = 16 * qq

        def pw_b(Pt):
            return V(Pt, Pt.ap0[:, :, lo:lo + 16, 0].rearrange("p g l -> p l g").unsqueeze(3).to_broadcast([64, 16, 8, 16]))

        def gi_b(Bt):
            return V(Bt, Bt.ap0.unsqueeze(1).to_broadcast([64, 16, 8, 16]))
        k.tt(xt1.v, pw_b(PreQ), gi_b(bbPre), ALU.mult)
        k.tt(xt2.v, pw_b(PimQ), gi_b(bbPim), ALU.mult, eng="pool")
        k.tt(X[0].v, xt1.v, xt2.v, ALU.subtract)
        k.tt(xt1.v, pw_b(PreQ), gi_b(bbPim), ALU.mult)
        k.tt(xt2.v, pw_b(PimQ), gi_b(bbPre), ALU.mult, eng="pool")
        k.tt(X[1].v, xt1.v, xt2.v, ALU.add)
        for li in range(16):
            lag = lo + li
            ps = PS[lag % 2]
            k.mm(ps[:, 0:128], X[0][:, li, :, :].rr("p g i -> p (g i)"), cb[:, 0, :], start=True, stop=False)
            k.mm(ps[:, 0:128], X[1][:, li, :, :].rr("p g i -> p (g i)"), cb[:, 1, :], start=False, stop=True)
            if lag == 0:
                k.tt(kt0.v, ps[:, 0:128], mbd.v, ALU.mult)
                k.stt(Kb[:, 0, :], ident.v, dv[:, 0:1], kt0.v, ALU.mult, ALU.add)
            else:
                k.tt(Kb[:, lag, :], ps[:, 0:128], mbd.v, ALU.mult)
    k.barrier()
    P.checkpoint("kb", Kb.v.rr("p l q -> p (l q)")[:, 0:4096])

    NCH = SEQ // 64
    S = k.take_at("S", [64, 2, 8, NCH + 1], F32, oE)
    c4 = Cur(oF)
    wbd = [c4.take("wbd%d" % i, [128, 4, 2, 8, 64], BF16) for i in range(2)]
    Bch = c4.take("Bch", [128, 2, 2, 512], F32)
    st1 = c4.take("st1", [64, 2, 8], F32)
    st2 = c4.take("st2", [64, 2, 8], F32)
    bm_b = V(bmask, bmask.ap0.unsqueeze(2).to_broadcast([128, 8, 64]))
    psb = [[PS[4 + blk * 2 + ri] for ri in range(2)] for blk in range(2)]
    for q in range(16):
        w = wbd[q % 2]
        for li in range(4):
            lag = 4 * q + li
            for ri in range(2):
                k.tt(w[:, li, ri, :, :], V(XT[ri], XT[ri].ap0[:, lag, :].unsqueeze(1).to_broadcast([128, 8, 64])), bm_b, ALU.mult,
                     eng="dve" if ri == 0 else "pool")
        for li in range(4):
            lag = 4 * q + li
            jp = 63 - lag
            for blk in range(2):
                lhs = u.v.rr("p (c j) -> p c j", j=64)[:, blk * 128:(blk + 1) * 128, jp]
                for ri in range(2):
                    k.mm(psb[blk][ri].v, lhs, w[:, li, ri, :, :].rr("p g q -> p (g q)"), start=(lag == 0), stop=(lag == 63))
    for blk in range(2):
        for ri in range(2):
            k.copy(Bch[:, blk, ri, :], psb[blk][ri].v, eng="act" if ri == 0 else "dve")
    k.memset(S[:, :, :, 0:1], 0.0)
    cnt = 0
    for blk in range(2):
        for ri in range(2):
            for g in range(8):
                ps = PS[cnt % 4]
                k.transpose(ps[0:64, 0:128], Bch[:, blk, ri, g * 64:(g + 1) * 64], ident.v)
                k.copy(S[:, ri, g, 1 + blk * 128: 1 + (blk + 1) * 128], ps[0:64, 0:128], eng="act" if cnt % 2 == 0 else "dve")
                cnt += 1
    k.barrier()
    P.checkpoint("states", S.v.rr("p r g c -> p (r g c)")[:, 0:4096])

    ytoep = k.take_at("ytoep", [128, SEQ], BF16, oB)
    cs = Cur(c4.off)
    NB, BM = 16, 16
    sa = cs.take("sa", [64, 2, 8, NB], F32)
    sb2 = cs.take("sb2", [64, 2, 8, NB], F32)
    carry = cs.take("carry", [64, 2, 8, NB], F32)
    Apw = cs.take("Apw", [64, 3, 8], F32)
    apt = cs.take("apt", [64, 3, 8], F32)
    SV = S.v[:, :, :, 1:NCH + 1].rr("p r g (b m) -> p r g b m", m=BM)

    def cmul(dst, src, A, nx):
        ab = V(A, A.ap0[:, 0, :].unsqueeze(1).unsqueeze(3).to_broadcast([64, 2, 8, nx]))
        nim = V(A, A.ap0[:, 2, :].unsqueeze(2).to_broadcast([64, 8, nx]))
        pim = V(A, A.ap0[:, 1, :].unsqueeze(2).to_broadcast([64, 8, nx]))
        k.tt(sa[:, :, :, 0:nx], src, ab, ALU.mult)
        k.tt(sb2[:, 0, :, 0:nx], src[:, 1], nim, ALU.mult)
        k.tt(sb2[:, 1, :, 0:nx], src[:, 0], pim, ALU.mult)
        k.tt(dst, sa[:, :, :, 0:nx], sb2[:, :, :, 0:nx], ALU.add)

    k.copy(Apw.v, A64.v)
    for _ in range(4):
        k.tt(apt[:, 0, :], Apw[:, 0, :], Apw[:, 0, :], ALU.mult)
        k.tt(apt[:, 1, :], Apw[:, 1, :], Apw[:, 1, :], ALU.mult)
        k.tt(apt[:, 2, :], Apw[:, 0, :], Apw[:, 1, :], ALU.mult)
        k.tt(Apw[:, 0, :], apt[:, 0, :], apt[:, 1, :], ALU.subtract)
        k.ts(Apw[:, 1, :], apt[:, 2, :], 2.0, None, op0=ALU.mult)
        k.ts(Apw[:, 2, :], apt[:, 2, :], -2.0, None, op0=ALU.mult)
    for i in range(1, BM):
        cmul(carry.v, SV[:, :, :, :, i - 1], A64, NB)
        k.tt(SV[:, :, :, :, i], SV[:, :, :, :, i], carry.v, ALU.add)
    for blk in range(1, NB):
        cmul(carry[:, :, :, 0:1], SV[:, :, :, blk - 1:blk, BM - 1], Apw, 1)
        k.tt(SV[:, :, :, blk:blk + 1, BM - 1], SV[:, :, :, blk:blk + 1, BM - 1], carry[:, :, :, 0:1], ALU.add)
    k.copy(carry[:, :, :, 0:NB - 1], SV[:, :, :, 0:NB - 1, BM - 1])
    for i in range(BM - 1):
        cmul(carry[:, :, :, 0:NB - 1], carry[:, :, :, 0:NB - 1], A64, NB - 1)
        k.tt(SV[:, :, :, 1:NB, i], SV[:, :, :, 1:NB, i], carry[:, :, :, 0:NB - 1], ALU.add)
    Sb = k.take_at("Sb", [64, 2, 8, NCH], BF16, Bch.off)
    k.copy(Sb.v, S[:, :, :, 0:NCH])
    cw = Cur(wbd[0].off)
    CLg = [[cw.take("CL%d_%d" % (i, ri), [64, 64, 16], BF16) for ri in range(2)] for i in range(2)]
    ct1 = cw.take("ct1", [64, 64, 16], F32)
    ct2 = cw.take("ct2", [64, 64, 16], F32)
    assert cw.off <= wbd[0].off + 4096
    Yst = k.take_at("Yst", [128, 64, 8, 16], BF16, oE)
    ytb = [k.take_at("ytoep%d" % blk, [128, SEQ // 2], BF16, oB + blk * 4096) for blk in range(2)]
    ycnt = [0]

    def ystate_unit(blk, g):
        CL = CLg[g % 2]
        pre1 = V(PreQ, PreQ.ap0[:, g, 1:65, 0].unsqueeze(2).to_broadcast([64, 64, 16]))
        pim1 = V(PimQ, PimQ.ap0[:, g, 1:65, 0].unsqueeze(2).to_broadcast([64, 64, 16]))
        cre_b = V(cP, cP.ap0[:, 0, g, :].unsqueeze(1).to_broadcast([64, 64, 16]))
        cim_b = V(cP, cP.ap0[:, 1, g, :].unsqueeze(1).to_broadcast([64, 64, 16]))
        k.tt(ct1.v, pre1, cre_b, ALU.mult)
        k.tt(ct2.v, pim1, cim_b, ALU.mult, eng="pool")
        k.tt(CL[0].v, ct1.v, ct2.v, ALU.subtract)
        k.tt(ct1.v, pim1, cre_b, ALU.mult)
        k.tt(ct2.v, pre1, cim_b, ALU.mult, eng="pool")
        k.tt(ct1.v, ct1.v, ct2.v, ALU.add)
        k.ts(CL[1].v, ct1.v, -1.0, None, op0=ALU.mult)
        for pc in range(2):
            ps = PS[4 + ycnt[0] % 2]
            for ri in range(2):
                k.mm(ps.v, Sb[:, ri, g, blk * 128:(blk + 1) * 128],
                     CL[ri][:, pc * 32:(pc + 1) * 32, :].rr("p j i -> p (j i)"), start=(ri == 0), stop=(ri == 1))
            k.copy(Yst[:, pc * 32:(pc + 1) * 32, g, :], V(ps, ps.ap0.rearrange("p (j i) -> p j i", i=16)), eng="act")
            ycnt[0] += 1

    def ystate_fold(blk):
        for j4 in range(16):
            pst = PS[6 + (j4 % 2)]
            pstb = V(pst, pst.ap0.bitcast(BF16)[:, 0:512].rearrange("p (a b) -> p a b", a=4))
            for jj in range(4):
                j = 4 * j4 + jj
                k.transpose(pstb[:, jj, :], Yst[:, j, :, :].rr("p g i -> p (g i)"), P.identb.v)
            dst = ytb[blk].v.rr("p (c j) -> p j c", j=64)[:, 4 * j4:4 * j4 + 4, :]
            k.tt(dst, dst, pstb, ALU.add)

    for n in range(SEQ // 512):
        ps = PS[n % 4]
        ts_ = slice(n * 512, (n + 1) * 512)
        pv = V(ps, ps.ap0.rearrange("p (c j) -> p c j", j=64))
        uv = u[:, ts_].rr("p (c j) -> p c j", j=64)
        for lag in range(64):
            k.mm(pv[:, :, lag:64], Kb[:, lag, :], uv[:, :, 0:64 - lag], start=(lag == 0), stop=(lag == 63))
        blk = n // 16
        k.copy(ytb[blk][:, (n % 16) * 512:(n % 16 + 1) * 512], ps.v, eng="act")
        if n % 2 == 1:
            ystate_unit(blk, (n // 2) % 8)
        if n % 16 == 15:
            ystate_fold(blk)
    k.barrier()
    ytoep = k.take_at("ytoep", [128, SEQ], BF16, oB)
    P.checkpoint("ystate", ytoep[:, 0:4096])

    c7 = Cur(oE)
    y2 = [c7.take("y2%d" % i, [128, 512], F32) for i in range(2)]
    GC = 0.7978845608028654
    for n in range(SEQ // 512):
        ts_ = slice(n * 512, (n + 1) * 512)
        y = ytoep[:, ts_]
        w = y2[n % 2]
        k.act(w.v, y, AF.Square)
        k.ts(w.v, w.v, 0.044715, 1.0, op0=ALU.mult, op1=ALU.add)
        k.tt(w.v, w.v, y, ALU.mult, eng="pool")
        k.act(w.v, w.v, AF.Sigmoid, scale=2.0 * GC)
        k.tt(u[:, ts_], y, w.v, ALU.mult)
    k.barrier()
    z = u


    P.checkpoint("toep", z[:, 0:4096])
    for i in range(8):
        P.out_ops.append(k.dma(d_z[:, i * T:(i + 1) * T], z[:, i * T:(i + 1) * T], q="sp" if i % 2 == 0 else "act"))


def s5_consts(P):
    k = P.k
    if hasattr(P, "s5c"):
        return
    P.s5c = True
    idd = P.din("ident", [128, 128])
    P.ident = k.take("ident", [128, 128], F32)
    k.dma(P.ident.v, idd.v)
    P.identb = k.take("identb", [128, 128], BF16)
    k.copy(P.identb.v, P.ident.v)
    d_j = P.din("jtab", [128, 65])
    d_bm = P.din("blockmask", [128, 8])
    d_mbd = P.din("maskbd", [128, 128])
    P.jtab = k.take("jtab", [128, 65], F32)
    P.bmask = k.take("bmask", [128, 8], F32)
    P.mbd = k.take("mbd", [128, 128], F32)
    k.dma(P.jtab.v, d_j.v)
    k.dma(P.bmask.v, d_bm.v)
    k.dma(P.mbd.v, d_mbd.v)


def host_s5(host):
    f = np.float32
    inp = host.inp
    eye = np.eye(128, dtype=f)
    host.common["ident"] = eye
    host.common["jtab"] = np.broadcast_to(np.arange(65, dtype=f)[None, :], (128, 65)).copy()
    host.common["blockmask"] = (np.arange(128)[:, None] // 16 == np.arange(8)[None, :]).astype(f)
    host.common["maskbd"] = (np.arange(128)[:, None] // 16 == np.arange(128)[None, :] // 16).astype(f)
    for c in range(NCORE):
        m = host.percore[c]
        gs = slice(8 * c, 8 * c + 8)
        for idx in range(2):
            nm = "s5_%d_" % idx
            lre = np.asarray(inp["s5_lam_re"], f)[idx, gs]
            lim = np.asarray(inp["s5_lam_im"], f)[idx, gs]
            ldt = np.asarray(inp["s5_log_dt"], f)[idx, gs]
            bre = np.asarray(inp["s5_b_re"], f)[idx, gs]
            bim = np.asarray(inp["s5_b_im"], f)[idx, gs]
            cre = np.asarray(inp["s5_c_re"], f)[idx, gs]
            cim = np.asarray(inp["s5_c_im"], f)[idx, gs]
            m[nm + "lamP"] = np.ascontiguousarray(np.stack([lre.T, lim.T], 1))
            m[nm + "dtP"] = np.ascontiguousarray(np.broadcast_to(ldt[None, :], (64, 8)))
            m[nm + "bP"] = np.ascontiguousarray(np.stack([bre.transpose(1, 0, 2), bim.transpose(1, 0, 2)], 1))
            m[nm + "cP"] = np.ascontiguousarray(np.stack([cre.transpose(2, 0, 1), cim.transpose(2, 0, 1)], 1))
            m[nm + "lamR"] = np.ascontiguousarray(np.stack([np.repeat(lre, 16, 0), np.repeat(lim, 16, 0)], 1))
            m[nm + "dtR"] = np.ascontiguousarray(np.repeat(ldt, 16)[:, None])
            m[nm + "bR"] = np.ascontiguousarray(np.stack([bre.transpose(0, 2, 1).reshape(128, 64),
                                                          bim.transpose(0, 2, 1).reshape(128, 64)], 1))
            m[nm + "dvec"] = np.ascontiguousarray(np.asarray(inp["s5_d"], f)[idx, 128 * c:128 * c + 128][:, None])


HOST_EXTRA.append(host_s5)
STEPS["s5core"] = lambda P, cfg: s5_core(P, cfg["idx"], P.PS)


def _set_x(host, o):
    host.set("xT", [np.asarray(o[c]["outT"]) for c in range(NCORE)])


def _set_gla_inputs(host, o):
    host.set("gT", [np.asarray(o[c]["gT"]) for c in range(NCORE)])
    host.set("g_qT", [np.ascontiguousarray(np.concatenate([np.asarray(o[r]["qT"])[c // 2] for r in range(NCORE)], 1))
                      for c in range(NCORE)])
    host.set("g_k", [_tokmajor_to_core(o, "ktm", (c // 2) * 128) for c in range(NCORE)])
    host.set("g_la", [_tokmajor_to_core(o, "latm", (c // 2) * 128) for c in range(NCORE)])
    host.set("g_v", [_tokmajor_to_core(o, "vtm", (c // 2) * 256 + (c % 2) * 128) for c in range(NCORE)])


def _set_attn_inputs(host, o):
    host.set("a_qT", [np.ascontiguousarray(np.concatenate([np.asarray(o[r]["aqT"])[c] for r in range(NCORE)], 1)) for c in range(NCORE)])
    host.set("a_kT", [np.ascontiguousarray(np.concatenate([np.asarray(o[r]["akT"])[c] for r in range(NCORE)], 1)) for c in range(NCORE)])
    host.set("a_v", [_tokmajor_to_core(o, "avtm", c * 128) for c in range(NCORE)])


def run_s5_layer(host, l, last=False):
    idx = l // 3
    o = launch(host, {"kind": "tok", "steps": ("norm_out:%d,h" % l,)})
    host.set("u", tok_to_chan(o, "h"))
    o = launch(host, {"kind": "s5core", "idx": idx})
    host.set("zf", chan_to_tok(o, "z"))
    steps = ["s5post:%d" % idx, "ffn:%d" % l]
    if last:
        steps.append("final")
    steps.append("store_x:outT")
    _set_x(host, launch(host, {"kind": "tok", "steps": tuple(steps)}))


def run_gla_layer(host, l):
    o = launch(host, {"kind": "tok", "steps": ("glapre",)})
    _set_gla_inputs(host, o)
    o = launch(host, {"kind": "glacore"})
    host.set("goT", chan_to_tok(o, "oT"))
    _set_x(host, launch(host, {"kind": "tok", "steps": ("glapost", "ffn:%d" % l, "store_x:outT")}))


def run_attn_layer(host, l):
    o = launch(host, {"kind": "tok", "steps": ("attnpre",)})
    _set_attn_inputs(host, o)
    o = launch(host, {"kind": "attncore"})
    host.set("aoTt", chan_to_tok(o, "aoT"))
    _set_x(host, launch(host, {"kind": "tok", "steps": ("attnpost", "ffn:%d" % l, "store_x:outT")}))


def gather_out(host):
    outs = [np.asarray(host.percore[c]["xT"], np.float32).reshape(D, T).T for c in range(NCORE)]
    return np.ascontiguousarray(np.concatenate(outs, 0))[None]


def kernel(**inputs):
    host = Host(inputs)
    o = launch(host, {"kind": "tok", "steps": ("norm_out:0,h",)})
    host.set("u", tok_to_chan(o, "h"))
    o = launch(host, {"kind": "s5core", "idx": 0})
    host.set("zf", chan_to_tok(o, "z"))
    o = launch(host, {"kind": "tok", "steps": ("s5post:0", "ffn:0", "glapre", "store_x:outT")})
    _set_x(host, o)
    _set_gla_inputs(host, o)
    o = launch(host, {"kind": "glacore"})
    host.set("goT", chan_to_tok(o, "oT"))
    o = launch(host, {"kind": "tok", "steps": ("glapost", "ffn:1", "attnpre", "store_x:outT")})
    _set_x(host, o)
    _set_attn_inputs(host, o)
    o = launch(host, {"kind": "attncore"})
    host.set("aoTt", chan_to_tok(o, "aoT"))
    o = launch(host, {"kind": "tok", "steps": ("attnpost", "ffn:2", "norm_out:3,h", "store_x:outT")})
    _set_x(host, o)
    host.set("u", tok_to_chan(o, "h"))
    o = launch(host, {"kind": "s5core", "idx": 1})
    host.set("zf", chan_to_tok(o, "z"))
    o = launch(host, {"kind": "tok", "steps": ("s5post:1", "ffn:3", "final", "store_x:outT")})
    _set_x(host, o)
    return gather_out(host)
```

```python
import numpy as np
from concourse.bass_utils import run_bass_kernel_spmd
import numpy as np
from contextlib import ExitStack
import concourse.bass as bass
import concourse.mybir as mybir

F32 = mybir.dt.float32
BF16 = mybir.dt.bfloat16
I32 = mybir.dt.int32
AF = mybir.ActivationFunctionType
ALU = mybir.AluOpType
AX = mybir.AxisListType

COMPUTE = ("pe", "act", "dve", "pool")


class Buf:
    def __init__(self, name, ap0, space):
        self.name = name
        self.ap0 = ap0
        self.space = space
        self.w = []
        self.r = []

    def __getitem__(self, idx):
        return V(self, self.ap0[idx])

    @property
    def v(self):
        return V(self, self.ap0)


class V:
    def __init__(self, buf, ap):
        self.buf = buf
        self.ap = ap

    def __getitem__(self, idx):
        return V(self.buf, self.ap[idx])

    def rr(self, pat, **kw):
        return V(self.buf, self.ap.rearrange(pat, **kw))


class Op:
    __slots__ = ("id", "eng", "fn", "deps", "is_dma", "is_cc", "inc", "semval", "sem", "raw_same", "prev")

    def __init__(self, id, eng, fn, is_dma=False, is_cc=False):
        self.id = id
        self.eng = eng
        self.fn = fn
        self.deps = set()
        self.raw_same = set()
        self.is_dma = is_dma
        self.is_cc = is_cc
        self.inc = False
        self.semval = None
        self.sem = None


class KB:
    def __init__(self):
        self.nc = bass.Bass("TRN2", target_bir_lowering=False)
        self.ops = []
        self.stack = ExitStack()
        self.nbuf = 0
        self.sb_bytes = 0
        self._bar_from = 0

    def dram(self, name, shape, dtype, kind="Internal"):
        if kind == "Internal":
            t = self.nc.dram_tensor(name, list(shape), dtype)
        else:
            t = self.nc.dram_tensor(name, list(shape), dtype, kind=kind)
        return Buf(name, t.ap(), "dram")

    def sb(self, name, shape, dtype):
        t = self.stack.enter_context(self.nc.sbuf_tensor(name, list(shape), dtype))
        n = 1
        for s in shape[1:]:
            n *= s
        self.sb_bytes += n * (4 if dtype in (F32, I32) else 2)
        return Buf(name, t[:], "sbuf")

    def ps(self, name, shape, dtype=F32):
        t = self.stack.enter_context(self.nc.psum_tensor(name, list(shape), dtype))
        return Buf(name, t[:], "psum")

    def rec(self, eng, fn, reads=(), writes=(), is_dma=False, is_cc=False):
        op = Op(len(self.ops), eng, fn, is_dma, is_cc)
        rb = []
        for x in reads:
            if x is None or isinstance(x, (int, float)):
                continue
            b = x.buf if isinstance(x, V) else x
            if b not in rb:
                rb.append(b)
        wb = []
        for x in writes:
            if x is None:
                continue
            b = x.buf if isinstance(x, V) else x
            if b not in wb:
                wb.append(b)
        for b in rb:
            for d in b.w:
                op.deps.add(d)
                op.raw_same.add(d)
            if b.space == "psum":
                for d in b.r:
                    if self.ops[d].eng != eng:
                        op.deps.add(d)
        for b in wb:
            for d in b.w:
                op.deps.add(d)
            for d in b.r:
                op.deps.add(d)
        for b in rb:
            if b not in wb:
                b.r.append(op.id)
        for b in wb:
            b.w = [op.id]
            b.r = []
        op.deps.discard(op.id)
        self.ops.append(op)
        return op

    def barrier(self):
        last = {}
        pend = []
        for op in self.ops[self._bar_from:]:
            if op.fn is None:
                continue
            if op.is_dma or op.is_cc:
                pend.append(op.id)
            else:
                last[op.eng] = op.id
        self._bar_from = len(self.ops)
        deps = set(pend) | set(last.values())
        for e in ("pe", "act", "dve", "pool", "sp"):
            op = Op(len(self.ops), e, None)
            op.deps = set(deps)
            op.raw_same = set(deps)
            self.ops.append(op)
        self._bar_from = len(self.ops) - 5

    def take(self, name, shape, dtype):
        n = 1
        for s in shape[1:]:
            n *= s
        words = n if dtype in (F32, I32) else (n + 1) // 2
        assert self.aoff + words <= self.awords, (name, self.aoff, words, self.awords)
        ap = self.arena.ap0[0:shape[0], self.aoff:self.aoff + words]
        self.aoff += words
        self.amax = max(self.amax, self.aoff)
        if dtype not in (F32,):
            ap = ap.bitcast(dtype)
        ap = ap[:, 0:n]
        if len(shape) == 3:
            ap = ap.rearrange("p (a b) -> p a b", a=shape[1])
        elif len(shape) == 4:
            ap = ap.rearrange("p (a b c) -> p a b c", a=shape[1], b=shape[2])
        elif len(shape) == 5:
            ap = ap.rearrange("p (a b c d) -> p a b c d", a=shape[1], b=shape[2], c=shape[3])
        return Buf(name, ap, "sbuf")

    def take_at(self, name, shape, dtype, off, pbase=0):
        n = 1
        for x in shape[1:]:
            n *= x
        words = n if dtype in (F32, I32) else (n + 1) // 2
        assert off + words <= self.awords, (name, off, words, self.awords)
        self.amax = max(self.amax, off + words)
        ap = self.arena.ap0[pbase:pbase + shape[0], off:off + words]
        if dtype not in (F32,):
            ap = ap.bitcast(dtype)
        ap = ap[:, 0:n]
        if len(shape) == 3:
            ap = ap.rearrange("p (a b) -> p a b", a=shape[1])
        elif len(shape) == 4:
            ap = ap.rearrange("p (a b c) -> p a b c", a=shape[1], b=shape[2])
        elif len(shape) == 5:
            ap = ap.rearrange("p (a b c d) -> p a b c d", a=shape[1], b=shape[2], c=shape[3])
        b = Buf(name, ap, "sbuf")
        b.words = words
        b.off = off
        return b

    def init_arena(self, words):
        self.arena = self.sb("arena", [128, words], F32)
        self.awords = words
        self.aoff = 0
        self.amax = 0

    @staticmethod
    def _a(x):
        return x.ap if isinstance(x, V) else x

    def mm(self, out, lhsT, rhs, start=True, stop=True):
        a = self._a
        return self.rec("pe", lambda e: e.matmul(a(out), a(lhsT), a(rhs), start=start, stop=stop),
                        reads=[lhsT, rhs], writes=[out])

    def transpose(self, out, in_, ident):
        a = self._a
        return self.rec("pe", lambda e: e.transpose(a(out), a(in_), a(ident)), reads=[in_, ident], writes=[out])

    def act(self, out, in_, func, bias=0.0, scale=1.0, accum=None, eng="act"):
        a = self._a
        kw = {}
        if accum is not None:
            kw["accum_out"] = a(accum)
        return self.rec(eng, lambda e: e.activation(a(out), a(in_), func, bias=a(bias), scale=a(scale), **kw),
                        reads=[in_, bias, scale], writes=[out, accum])

    def tt(self, out, in0, in1, op, eng="dve"):
        a = self._a
        return self.rec(eng, lambda e: e.tensor_tensor(a(out), a(in0), a(in1), op), reads=[in0, in1], writes=[out])

    def ts(self, out, in0, s1, s2=None, op0=ALU.mult, op1=None, accum=None, eng="dve"):
        a = self._a
        kw = {}
        if op1 is not None:
            kw["op1"] = op1
        if accum is not None:
            kw["accum_out"] = a(accum)
        return self.rec(eng, lambda e: e.tensor_scalar(a(out), a(in0), a(s1), a(s2) if s2 is not None else None, op0, **kw),
                        reads=[in0, s1, s2], writes=[out, accum])

    def stt(self, out, in0, scalar, in1, op0, op1, eng="dve"):
        a = self._a
        return self.rec(eng, lambda e: e.scalar_tensor_tensor(a(out), a(in0), a(scalar), a(in1), op0, op1),
                        reads=[in0, scalar, in1], writes=[out])

    def copy(self, out, in_, eng="dve"):
        a = self._a
        if eng == "act":
            return self.rec(eng, lambda e: e.copy(a(out), a(in_)), reads=[in_], writes=[out])
        return self.rec(eng, lambda e: e.tensor_copy(a(out), a(in_)), reads=[in_], writes=[out])

    def memset(self, out, val, eng="dve"):
        a = self._a
        return self.rec(eng, lambda e: e.memset(a(out), val), reads=[], writes=[out])

    def recip(self, out, in_):
        a = self._a
        return self.rec("dve", lambda e: e.reciprocal(a(out), a(in_)), reads=[in_], writes=[out])

    def scan(self, out, d0, d1, initial, op0, op1):
        a = self._a
        return self.rec("dve", lambda e: e.tensor_tensor_scan(a(out), a(d0), a(d1), a(initial), op0, op1),
                        reads=[d0, d1, initial], writes=[out])

    def dma(self, out, in_, q="sp", **kw):
        a = self._a
        return self.rec(q, lambda e: e.dma_start(out=a(out), in_=a(in_), **kw), reads=[in_], writes=[out], is_dma=True)

    def cc(self, kind, ins, outs, op=ALU.bypass, groups=None):
        a = self._a
        g = groups or [list(range(8))]
        return self.rec("pool", lambda e: e.collective_compute(kind, op, replica_groups=g,
                                                                  ins=[a(i) for i in ins], outs=[a(o) for o in outs]),
                        reads=list(ins), writes=list(outs), is_cc=True)

    def emit(self, final_wait_ops=()):
        nc = self.nc
        ops = self.ops
        engs = ["pe", "act", "dve", "pool", "sp"]
        for op in ops:
            for d in op.deps:
                dop = ops[d]
                if dop.eng != op.eng:
                    dop.inc = True
                elif dop.is_dma or dop.is_cc:
                    dop.inc = True
                elif op.eng in ("act", "dve", "pool") and not op.is_dma:
                    dop.inc = True
        for d in final_wait_ops:
            d.inc = True
        NDS = {"sp": 24, "pool": 12, "act": 8, "pe": 1, "dve": 1}
        st = self.stack
        csem = {e: st.enter_context(nc.semaphore("c_" + e)) for e in COMPUTE}
        dsem = {e: [st.enter_context(nc.semaphore("d_%s%d" % (e, i))) for i in range(NDS[e])] for e in engs}
        ccsem = st.enter_context(nc.semaphore("ccsem"))
        ccnt = {e: 0 for e in COMPUTE}
        dcnt = {e: [0] * NDS[e] for e in engs}
        drr = {e: 0 for e in engs}
        cccnt = 0
        for op in ops:
            if op.is_dma:
                i = drr[op.eng] % NDS[op.eng]
                drr[op.eng] += 1
                op.sem = ("d", op.eng, i)
                op.prev = dcnt[op.eng][i]
                dcnt[op.eng][i] += 16
                op.semval = dcnt[op.eng][i]
            elif op.is_cc:
                cccnt += 1
                op.sem = ("cc",)
                op.semval = cccnt
            elif op.inc:
                ccnt[op.eng] += 1
                op.sem = ("c", op.eng)
                op.semval = ccnt[op.eng]

        def semh(s):
            if s[0] == "d":
                return dsem[s[1]][s[2]]
            if s[0] == "cc":
                return ccsem
            return csem[s[1]]

        by_eng = {e: [op for op in ops if op.eng == e] for e in engs}
        self.stats = {e: len(by_eng[e]) for e in engs}
        nwaits = {e: 0 for e in engs}

        def run(eng_name, e):
            seen = {}
            for op in by_eng[eng_name]:
                need = {}
                for d in op.deps:
                    dop = ops[d]
                    if dop.sem is None:
                        continue
                    if dop.eng == eng_name and not (dop.is_dma or dop.is_cc):
                        if not (eng_name in ("act", "dve", "pool") and not op.is_dma):
                            continue
                    if need.get(dop.sem, 0) < dop.semval:
                        need[dop.sem] = dop.semval
                if op.is_dma and op.prev > 0:
                    if need.get(op.sem, 0) < op.prev:
                        need[op.sem] = op.prev
                for s, v in need.items():
                    if seen.get(s, 0) >= v:
                        continue
                    e.wait_ge(semh(s), v)
                    nwaits[eng_name] += 1
                    seen[s] = v
                if op.fn is None:
                    continue
                ins = op.fn(e)
                if op.is_dma:
                    ins.then_inc(semh(op.sem), 16)
                elif op.is_cc:
                    ins.then_inc(semh(op.sem), 1)
                elif op.inc:
                    ins.then_inc(semh(op.sem), 1)
            if eng_name == "sp":
                for d in final_wait_ops:
                    e.wait_ge(semh(d.sem), d.semval)

        with nc.Block() as block:
            @block.tensor
            def _(e):
                run("pe", e)

            @block.scalar
            def _(e):
                run("act", e)

            @block.vector
            def _(e):
                run("dve", e)

            @block.gpsimd
            def _(e):
                run("pool", e)

            @block.sync
            def _(e):
                run("sp", e)
        self.stats["waits"] = nwaits
        self.stack.close()
        return nc
D = 1024
SEQ = 16384
NCORE = 8
T = SEQ // NCORE
NT = T // 512
KT = D // 128
FH = 2816
FM = FH // 128
EPS = 1e-6


class StopBuild(Exception):
    pass


class Prog:
    def __init__(self, cfg):
        self.cfg = cfg
        self.k = KB()
        self.ins = {}
        self.outs = []
        self.out_ops = []
        self.dbg_ops = []
        k = self.k
        k.init_arena(52000)
        if cfg.get("kind") == "attncore":
            self.PS2 = [k.ps("pss%d" % i, [128, 1024], F32) for i in range(2)]
            self.PS = [None] * 4 + [k.ps("ps%d" % i, [128, 512], F32) for i in range(4, 8)]
        else:
            self.PS = [k.ps("ps%d" % i, [128, 512], F32) for i in range(8)]

    def din(self, name, shape, dtype=F32):
        b = self.k.dram(name, shape, dtype, kind="ExternalInput")
        self.ins[name] = (tuple(shape), dtype)
        return b

    def dout(self, name, shape, dtype=F32):
        b = self.k.dram(name, shape, dtype, kind="ExternalOutput")
        self.outs.append(name)
        return b

    def checkpoint(self, name, dump=None):
        if self.cfg.get("stop") != name:
            return
        k = self.k
        k.barrier()
        if dump is not None:
            n = dump.ap.shape[1]
            np_ = dump.ap.shape[0]
            dbg = self.dout("dbg", [128, 4096], F32)
            tmp = k.take_at("dbgtmp", [128, 4096], F32, k.awords - 4096)
            k.memset(tmp.v, 0.0)
            k.copy(tmp[0:np_, 0:n], dump)
            self.dbg_ops.append(k.dma(dbg.v, tmp.v))
        raise StopBuild()

    def finish(self):
        self.nc = self.k.emit(final_wait_ops=self.out_ops + self.dbg_ops)
        return self


class Tok:
    def __init__(self, P):
        self.P = P
        k = P.k
        self.k = k
        xin = P.din("xT", [KT, 128, T])
        gains = P.din("gains", [128, 9, KT])
        self.x = k.take("x", [128, KT, T], F32)
        self.gn = k.take("gn", [128, 9, KT], F32)
        self.ones = k.take("ones", [128, 128], BF16)
        k.dma(self.gn.v, gains.v)
        for kt in range(KT):
            k.dma(self.x[:, kt, :], xin[kt], q="sp" if kt % 2 == 0 else "act")
        k.memset(self.ones.v, 1.0)

    def store_x(self, name="outT"):
        out = self.P.dout(name, [KT, 128, T], F32)
        for kt in range(KT):
            self.P.out_ops.append(self.k.dma(out[kt], self.x[:, kt, :], q="sp" if kt % 2 == 0 else "act"))

    def rstd_tile(self, n, tag):
        k, x, PS = self.k, self.x, self.P.PS
        if not hasattr(self, "_nrm_" + tag):
            setattr(self, "_nrm_" + tag, ([k.take(tag + "sq%d" % i, [128, 512], BF16) for i in range(2)],
                                         [k.take(tag + "rstd%d" % i, [128, 512], F32) for i in range(2)]))
        sq, rstd = getattr(self, "_nrm_" + tag)
        ts = slice(n * 512, (n + 1) * 512)
        ps = PS[n % 2]
        for kt in range(KT):
            s = sq[kt % 2]
            k.act(s.v, x[:, kt, ts], AF.Square)
            k.mm(ps.v, self.ones.v, s.v, start=(kt == 0), stop=(kt == KT - 1))
        r = rstd[n % 2]
        k.ts(r.v, ps.v, 1.0 / D, EPS, op0=ALU.mult, op1=ALU.add)
        k.act(r.v, r.v, AF.Sqrt)
        k.recip(r.v, r.v)
        return r

    def rmsnorm(self, which, h, tag):
        k = self.k
        for n in range(NT):
            ts = slice(n * 512, (n + 1) * 512)
            r = self.rstd_tile(n, tag)
            for kt in range(KT):
                k.stt(h[:, kt, ts], self.x[:, kt, ts], self.gn[:, which, kt:kt + 1], r.v, ALU.mult, ALU.mult)

    def norm_out(self, which, name):
        k = self.k
        mark = k.aoff
        h = k.take("h_" + name, [128, KT, T], BF16)
        self.rmsnorm(which, h, "no" + name)
        out = self.P.dout(name, [KT, 128, T], BF16)
        for kt in range(KT):
            self.P.out_ops.append(k.dma(out[kt], h[:, kt, :], q="sp" if kt % 2 == 0 else "act"))
        k.barrier()
        k.aoff = mark

    def final_norm(self):
        k = self.k
        mark = k.aoff
        for n in range(NT):
            ts = slice(n * 512, (n + 1) * 512)
            r = self.rstd_tile(n, "fin")
            for kt in range(KT):
                k.stt(self.x[:, kt, ts], self.x[:, kt, ts], self.gn[:, 8, kt:kt + 1], r.v, ALU.mult, ALU.mult)
        k.barrier()
        k.aoff = mark

    def proj_gated(self, name, src, nk, nmo, combine, w2=None):
        k, PS = self.k, self.P.PS
        wd = self.P.din(name, [nmo, 128, nk * 256])
        if w2 is None:
            w2 = [k.take(name + "w%d" % i, [128, nk, 2, 128], BF16) for i in range(2)]
        cnt = 0
        for mo in range(nmo):
            w = w2[mo % 2]
            k.dma(w.v.rr("p a b c -> p (a b c)"), wd[mo], q="pool")
            for n in range(NT):
                ts = slice(n * 512, (n + 1) * 512)
                pa = PS[2 + 2 * (cnt % 2)]
                pb = PS[3 + 2 * (cnt % 2)]
                for kt in range(nk):
                    k.mm(pa.v, w[:, kt, 0, :], src[:, kt, ts], start=(kt == 0), stop=(kt == nk - 1))
                for kt in range(nk):
                    k.mm(pb.v, w[:, kt, 1, :], src[:, kt, ts], start=(kt == 0), stop=(kt == nk - 1))
                combine(mo, n, ts, pa, pb, cnt)
                cnt += 1

    def proj_acc(self, name, src, nk, nmo, sink, w2=None):
        k, PS = self.k, self.P.PS
        wd = self.P.din(name, [nmo, 128, nk * 128])
        if w2 is None:
            w2 = [k.take(name + "w%d" % i, [128, nk, 128], BF16) for i in range(2)]
        cnt = 0
        for mo in range(nmo):
            w = w2[mo % 2]
            k.dma(w.v.rr("p a b -> p (a b)"), wd[mo], q="pool")
            for n in range(NT):
                ts = slice(n * 512, (n + 1) * 512)
                po = PS[6 + (cnt % 2)]
                for kt in range(nk):
                    k.mm(po.v, w[:, kt, :], src[:, kt, ts], start=(kt == 0), stop=(kt == nk - 1))
                sink(mo, n, ts, po, cnt)
                cnt += 1

    def add_to_x(self, mo, n, ts, po, cnt):
        self.k.tt(self.x[:, mo, ts], self.x[:, mo, ts], po.v, ALU.add)

    def ffn(self, l):
        k = self.k
        mark = k.aoff
        HG = FM // 2
        h = k.take("h", [128, KT, T], BF16)
        a = k.take("a", [128, HG, T], BF16)
        sg = [k.take("sg%d" % i, [128, 512], F32) for i in range(2)]
        self.rmsnorm(4 + l, h, "ffn%d" % l)
        wg2 = [k.take("wg2_%d" % i, [128, KT, 2, 128], BF16) for i in range(2)]
        wd2 = [k.take("wd2_%d" % i, [128, HG, 128], BF16) for i in range(2)]
        for grp in range(2):
            def comb(mi, n, ts, pg, pu, cnt):
                s = sg[cnt % 2]
                k.act(s.v, pg.v, AF.Silu)
                k.tt(a[:, mi, ts], s.v, pu.v, ALU.mult)
            self.proj_gated("wgu%d_%d" % (l, grp), h, KT, HG, comb, wg2)
            self.proj_acc("wdn%d_%d" % (l, grp), a, HG, KT, self.add_to_x, wd2)
        k.barrier()
        k.aoff = mark

    def s5post(self, idx):
        k = self.k
        mark = k.aoff
        zin = self.P.din("zf", [KT, 128, T], BF16)
        zf = k.take("zf", [128, KT, T], BF16)
        for kt in range(KT):
            k.dma(zf[:, kt, :], zin[kt], q="sp" if kt % 2 == 0 else "act")
        sg = [k.take("sgl%d" % i, [128, 512], F32) for i in range(2)]

        def comb(mo, n, ts, pa, pb, cnt):
            s = sg[cnt % 2]
            k.act(s.v, pb.v, AF.Sigmoid)
            k.tt(s.v, s.v, pa.v, ALU.mult)
            k.tt(self.x[:, mo, ts], self.x[:, mo, ts], s.v, ALU.add)
        self.proj_gated("s5_%d_wglu" % idx, zf, KT, KT, comb)
        k.barrier()
        k.aoff = mark


STEPS = {}


def build(cfg):
    P = Prog(cfg)
    try:
        if cfg["kind"] == "tok":
            tk = Tok(P)
            for st in cfg["steps"]:
                nm, _, arg = st.partition(":")
                if nm == "norm_out":
                    which, name = arg.split(",")
                    tk.norm_out(int(which), name)
                elif nm == "ffn":
                    tk.ffn(int(arg))
                elif nm == "s5post":
                    tk.s5post(int(arg))
                elif nm == "final":
                    tk.final_norm()
                elif nm == "store_x":
                    tk.store_x(arg or "outT")
                else:
                    STEPS[nm](tk, arg)
        else:
            STEPS[cfg["kind"]](P, cfg)
    except StopBuild:
        pass
    return P.finish()


def _pair_tiles(w, nk, nmo):
    w = w.reshape(nk, 128, 2, nmo, 128).transpose(3, 1, 0, 2, 4)
    return np.ascontiguousarray(w).reshape(nmo, 128, nk * 256)


def _acc_tiles(w, nk, nmo):
    w = w.reshape(nk, 128, nmo, 128).transpose(2, 1, 0, 3)
    return np.ascontiguousarray(w).reshape(nmo, 128, nk * 128)


class Host:
    def __init__(self, inp):
        f = np.float32
        self.inp = inp
        g = np.stack([np.asarray(inp["norm_mix"], f)[i] for i in range(4)]
                     + [np.asarray(inp["norm_ffn"], f)[i] for i in range(4)]
                     + [np.asarray(inp["norm_final"], f)], 0)
        self.common = {"gains": np.ascontiguousarray(g.reshape(9, KT, 128).transpose(2, 0, 1))}
        self.percore = [dict() for _ in range(NCORE)]
        X = np.asarray(inp["x"], f)[0]
        for c in range(NCORE):
            self.percore[c]["xT"] = np.ascontiguousarray(X[c * T:(c + 1) * T].T).reshape(KT, 128, T)
        for fn in HOST_EXTRA:
            fn(self)

    def weight(self, name):
        f = np.float32
        inp = self.inp
        if name.startswith("wgu"):
            l, grp = int(name[3]), int(name[5])
            w = np.asarray(inp["ffn_w_gate_up"], f)[l]
            HG = FM // 2
            cols = np.concatenate([np.arange(grp * HG * 128, (grp + 1) * HG * 128),
                                   FH + np.arange(grp * HG * 128, (grp + 1) * HG * 128)])
            return _pair_tiles(w[:, cols], KT, HG)
        if name.startswith("wdn"):
            l, grp = int(name[3]), int(name[5])
            w = np.asarray(inp["ffn_w_down"], f)[l]
            HG = FM // 2
            return _acc_tiles(w[grp * HG * 128:(grp + 1) * HG * 128], HG, KT)
        if name.startswith("s5_") and name.endswith("wglu"):
            idx = int(name[3])
            return _pair_tiles(np.asarray(inp["s5_w_glu"], f)[idx], KT, KT)
        for pre, fn in WEIGHT_EXTRA.items():
            if name.startswith(pre):
                return fn(self, name)
        raise KeyError(name)

    def set(self, name, per_core_list):
        for c in range(NCORE):
            self.percore[c][name] = per_core_list[c]

    def get(self, name, c):
        if name in self.percore[c]:
            return self.percore[c][name]
        if name not in self.common:
            self.common[name] = self.weight(name)
        return self.common[name]


_PROGS = {}


def launch(host, cfg):
    key = repr(sorted(cfg.items(), key=str))
    if key not in _PROGS:
        _PROGS[key] = build(cfg)
    P = _PROGS[key]
    maps = [{n: host.get(n, c) for n in P.ins} for c in range(NCORE)]
    res = run_bass_kernel_spmd(P.nc, maps, core_ids=list(range(NCORE)))
    return [res.results[c] for c in range(NCORE)]


def tok_to_chan(outs, name):
    return [np.ascontiguousarray(np.concatenate([np.asarray(outs[r][name])[c] for r in range(NCORE)], axis=1))
            for c in range(NCORE)]


def chan_to_tok(outs, name):
    return [np.ascontiguousarray(np.stack([np.asarray(outs[c][name])[:, r * T:(r + 1) * T] for c in range(NCORE)], 0))
            for r in range(NCORE)]


HOST_EXTRA = []
WEIGHT_EXTRA = {}


import math as _math
ATT_LAYER = 2
LAMBDA_INIT = 0.8 - 0.6 * _math.exp(-0.3 * ATT_LAYER)
ROPE_THETA_ = 500000.0


def _sin_reduced(k, dst, src, tmpi, tmpf, scr):
    k.ts(tmpi, src, 1.0 / TWO_PI, None, op0=ALU.mult)
    k.copy(scr, tmpi)
    k.stt(scr, scr, -TWO_PI, src, ALU.mult, ALU.add)
    k.ts(tmpf, scr, PI, TWO_PI, op0=ALU.is_gt, op1=ALU.mult)
    k.tt(scr, scr, tmpf, ALU.subtract)
    k.ts(tmpf, scr, -PI, -TWO_PI, op0=ALU.is_lt, op1=ALU.mult)
    k.tt(scr, scr, tmpf, ALU.subtract)
    k.act(dst, scr, AF.Sin)


def attn_pre(tk, arg):
    P, k, PS = tk.P, tk.k, tk.P.PS
    mark = k.aoff
    h = k.take("h", [128, KT, T], BF16)
    tk.rmsnorm(2, h, "att")
    o_q = P.dout("aqT", [8, 128, T], BF16)
    o_k = P.dout("akT", [8, 128, T], BF16)
    o_v = P.dout("avtm", [16, 128, 1024], BF16)
    d_pos = P.din("pos_rep", [128, T], I32)
    d_invf = P.din("rope_invf", [128, 1])
    d_perm = P.din("rope_perm", [128, 128])
    invf = k.take("invf", [128, 1], F32)
    perm = k.take("perm", [128, 128], BF16)
    k.dma(invf.v, d_invf.v)
    k.dma(perm.v, d_perm.v, q="pool")
    COS = k.take("COS", [128, T], F32)
    SIN = k.take("SIN", [128, T], F32)
    COSq = k.take("COSq", [128, T], F32)
    SINq = k.take("SINq", [128, T], F32)
    m2 = k.aoff
    posi = k.take("posi", [128, T], I32)
    k.dma(posi.v, d_pos.v)
    ang = k.take("ang", [128, T], F32)
    tmpf = k.take("rtmp", [128, T], F32)
    scr = k.take("rscr", [128, T], F32)
    tmpi = V(tmpf, tmpf.ap0.bitcast(I32))
    k.copy(ang.v, posi.v)
    k.ts(ang.v, ang.v, invf[:, 0:1], None, op0=ALU.mult)
    _sin_reduced(k, SIN.v, ang.v, tmpi, tmpf.v, scr.v)
    k.ts(ang.v, ang.v, 0.5 * PI, None, op0=ALU.add)
    _sin_reduced(k, COS.v, ang.v, tmpi, tmpf.v, scr.v)
    k.ts(COSq.v, COS.v, 0.125, None, op0=ALU.mult)
    k.ts(SINq.v, SIN.v, 0.125, None, op0=ALU.mult)
    k.barrier()
    k.aoff = m2
    P.checkpoint("a_tab", COS.v)
    stq = [k.take("astq%d" % i, [128, T], BF16) for i in range(2)]
    qb = [k.take("aqb%d" % i, [128, 512], BF16) for i in range(2)]
    t1 = [k.take("at1%d" % i, [128, 512], F32) for i in range(2)]
    t2 = [k.take("at2%d" % i, [128, 512], F32) for i in range(2)]

    def sink_qk(mo, n, ts, po, cnt):
        st = stq[mo % 2]
        b = qb[cnt % 2]
        k.copy(b.v, po.v, eng="act")
        pp = PS[cnt % 2]
        k.mm(pp.v, perm.v, b.v)
        c_, s_ = (COSq, SINq) if mo < 8 else (COS, SIN)
        a1, a2 = t1[cnt % 2], t2[cnt % 2]
        k.tt(a1.v, b.v, c_[:, ts], ALU.mult)
        k.tt(a2.v, pp.v, s_[:, ts], ALU.mult)
        k.tt(st[:, ts], a1.v, a2.v, ALU.add, eng="pool")
        if n == NT - 1:
            dst = o_q[mo] if mo < 8 else o_k[mo - 8]
            P.out_ops.append(k.dma(dst, st.v, q="sp"))
    tk.proj_acc("att_wqk", h, KT, 16, sink_qk)
    P.checkpoint("a_qk", stq[1].v)
    d_wv = P.din("att_wv", [2, 128, KT * 512])
    wv = [k.take("awv%d" % i, [128, KT, 512], BF16) for i in range(2)]
    tst = [k.take("atst%d" % i, [128, 512], BF16) for i in range(4)]
    cnt = 0
    for ci in range(2):
        w = wv[ci]
        k.dma(w.v.rr("p a b -> p (a b)"), d_wv[ci], q="pool")
        for tt in range(16):
            ps = PS[2 + cnt % 2]
            for kt in range(KT):
                k.mm(ps.v, h[:, kt, tt * 128:(tt + 1) * 128], w[:, kt, :], start=(kt == 0), stop=(kt == KT - 1))
            st = tst[cnt % 4]
            k.copy(st.v, ps.v, eng="act" if cnt % 2 == 0 else "dve")
            P.out_ops.append(k.dma(o_v[tt][:, ci * 512:(ci + 1) * 512], st.v, q="sp" if cnt % 2 == 0 else "act"))
            cnt += 1
    k.barrier()
    k.aoff = mark


STEPS["attnpre"] = attn_pre


def attn_core(P, cfg):
    k, PS = P.k, P.PS
    d_q = P.din("a_qT", [128, SEQ], BF16)
    d_k = P.din("a_kT", [128, SEQ], BF16)
    d_v = P.din("a_v", [128, 128 * 128], BF16)
    d_lam = P.din("a_lamv", [128, 4, 64])
    d_sg = P.din("a_subg", [128, 1])
    d_pq = P.din("a_posq", [128, 512], I32)
    d_pk = P.din("a_posk", [128, 4], I32)
    d_o = P.dout("aoT", [128, SEQ], BF16)
    QQ = [k.take("Q%d" % i, [128, SEQ // 4], BF16) for i in range(4)]
    KQ = [k.take("K%d" % i, [128, SEQ // 4], BF16) for i in range(4)]
    VQ = [k.take("V%d" % i, [128, 32, 128], BF16) for i in range(4)]
    oT = k.take("oT", [128, 2, 512], BF16)
    ones = k.take("ones", [128, 128], BF16)
    k.memset(ones.v, 1.0)
    for i in range(4):
        sl = slice(i * 4096, (i + 1) * 4096)
        k.dma(KQ[i].v, d_k[:, sl], q="act")
        k.dma(QQ[i].v, d_q[:, sl], q="sp")
        k.dma(VQ[i].v.rr("p a b -> p (a b)"), d_v[:, sl], q="sp" if i % 2 else "act")
    lv = k.take("lv", [128, 4, 64], F32)
    sgl = k.take("sgl", [128, 1], F32)
    k.dma(lv.v, d_lam.v)
    k.dma(sgl.v, d_sg.v)
    lp = k.take("lp", [128, 2, 64], F32)
    ls = k.take("ls", [128, 2], F32)
    lam = k.take("lam", [128, 1], F32)
    k.tt(lp[:, 0, :], lv[:, 0, :], lv[:, 1, :], ALU.mult)
    k.tt(lp[:, 1, :], lv[:, 2, :], lv[:, 3, :], ALU.mult)
    k.ts(lp[:, 0, :], lp[:, 0, :], 1.0, 0.0, op0=ALU.mult, op1=ALU.add, accum=ls[:, 0:1])
    k.ts(lp[:, 1, :], lp[:, 1, :], 1.0, 0.0, op0=ALU.mult, op1=ALU.add, accum=ls[:, 1:2])
    k.act(ls.v, ls.v, AF.Exp)
    k.tt(lam.v, ls[:, 0:1], ls[:, 1:2], ALU.subtract)
    k.ts(lam.v, lam.v, LAMBDA_INIT, None, op0=ALU.add)
    k.ts(sgl.v, sgl.v, 1.0 - LAMBDA_INIT, None, op0=ALU.mult)
    pq = k.take("pq", [128, 512], I32)
    pk = k.take("pk", [128, 4], I32)
    k.dma(pq.v, d_pq.v)
    k.dma(pk.v, d_pk.v)
    k.ts(pq.v, pq.v, 6, None, op0=ALU.arith_shift_right)
    k.ts(pk.v, pk.v, 6, None, op0=ALU.arith_shift_right)
    cq = k.take("cq", [128, 512], F32)
    ck = k.take("ck", [128, 4], F32)
    k.copy(cq.v, pq.v)
    k.copy(ck.v, pk.v)
    M = [k.take("M%d" % t, [128, 512], BF16) for t in range(4)]
    for t in range(4):
        k.ts(M[t].v, cq.v, ck[:, t:t + 1], None, op0=ALU.is_ge)
    E = [k.take("E%d" % i, [128, 2, 512], BF16) for i in range(3)]
    M2 = [k.take("M2_%d" % t, [128, 2, 512], BF16) for t in range(4)]
    for t in range(4):
        k.copy(M2[t][:, 0, :], M[t].v)
        k.copy(M2[t][:, 1, :], M[t].v)
    rinv = [[k.take("rinv%d_%d" % (i, s_), [128, 512], F32) for s_ in range(2)] for i in range(2)]
    Os = [[k.take("Os%d_%d" % (i, s_), [128, 512], F32) for s_ in range(2)] for i in range(2)]
    ofs = [k.take("of%d" % i, [128, 512], F32) for i in range(2)]
    sqb = [k.take("sqb%d" % i, [128, 512], BF16) for i in range(2)]
    pending = {}
    Eacc = [k.take("Eacc%d" % i, [128, 512], F32) for i in range(2)]
    ones32 = k.take("ones32", [128, 128], F32)
    k.memset(ones32.v, 1.0)
    O = [PS[4], PS[5]]
    R = [PS[6], PS[7]]
    steps = [(qi, kj) for qi in range(SEQ // 512) for kj in range(4 * qi + 4)]

    def emit_qk(si):
        qi, kj = steps[si]
        qs = slice((qi % 8) * 512, (qi % 8 + 1) * 512)
        ks = slice((kj % 32) * 128, (kj % 32 + 1) * 128)
        for s in range(2):
            rows = slice(64 * s, 64 * s + 64)
            k.mm(P.PS2[si % 2][:, s * 512:(s + 1) * 512], KQ[kj // 32][rows, ks], QQ[qi // 8][rows, qs])

    emit_qk(0)
    for si, (qi, kj) in enumerate(steps):
        qs = slice(qi * 512, (qi + 1) * 512)
        nk = 4 * qi + 4
        if si + 1 < len(steps):
            emit_qk(si + 1)
        Ex = E[si % 3]
        k.act(Ex.v.rr("p a b -> p (a b)"), P.PS2[si % 2].v, AF.Exp)
        if kj >= 4 * qi:
            k.tt(Ex.v, Ex.v, M2[kj - 4 * qi].v, ALU.mult)
        for s in range(2):
            k.mm(O[s].v, VQ[kj // 32][:, kj % 32, :], Ex[:, s, :], start=(kj == 0), stop=(kj == nk - 1))
            if s == 0:
                if kj == 0:
                    k.copy(Eacc[0].v, Ex[:, 0, :])
                else:
                    k.tt(Eacc[0].v, Eacc[0].v, Ex[:, 0, :], ALU.add)
            else:
                k.mm(R[1].v, ones.v, Ex[:, 1, :], start=(kj == 0), stop=(kj == nk - 1))
        for fn in pending.pop(si, []):
            fn()
        if kj != nk - 1:
            continue
        par = qi % 2
        Osb = [Os[par][0], Os[par][1]]
        rv = [rinv[par][0], rinv[par][1]]
        k.mm(R[0].v, ones32.v, Eacc[0].v)
        k.copy(Osb[0].v, O[0].v, eng="act")
        k.copy(Osb[1].v, O[1].v, eng="act")
        k.act(rv[1].v, R[1].v, AF.Ln)
        k.act(rv[0].v, R[0].v, AF.Ln)
        k.act(rv[1].v, rv[1].v, AF.Exp, scale=-1.0)
        k.act(rv[0].v, rv[0].v, AF.Exp, scale=-1.0)

        def stage_b(qi=qi, par=par, Osb=Osb, rv=rv):
            of = ofs[par]
            k.tt(of.v, Osb[0].v, rv[0].v, ALU.mult)
            k.stt(Osb[1].v, Osb[1].v, lam[:, 0:1], rv[1].v, ALU.mult, ALU.mult)
            k.tt(of.v, of.v, Osb[1].v, ALU.subtract)
            k.act(sqb[par].v, of.v, AF.Square)
            k.mm(R[0].v, ones.v, sqb[par].v)
            k.ts(rv[0].v, R[0].v, 1.0 / 128.0, 1e-5, op0=ALU.mult, op1=ALU.add)

        def stage_c(qi=qi, par=par, rv=rv):
            of = ofs[par]
            qs_ = slice(qi * 512, (qi + 1) * 512)
            k.act(rv[0].v, rv[0].v, AF.Ln)
            k.act(rv[0].v, rv[0].v, AF.Exp, scale=-0.5)
            ob = oT[:, par, :]
            k.stt(ob, of.v, sgl[:, 0:1], rv[0].v, ALU.mult, ALU.mult)
            P.out_ops.append(k.dma(d_o[:, qs_], ob, q="sp" if par == 0 else "act"))
        if si + 4 < len(steps):
            pending.setdefault(si + 2, []).append(stage_b)
            pending.setdefault(si + 4, []).append(stage_c)
        else:
            stage_b()
            stage_c()


STEPS["attncore"] = attn_core


def attn_post(tk, arg):
    P, k = tk.P, tk.k
    mark = k.aoff
    d_o = P.din("aoTt", [8, 128, T], BF16)
    o = k.take("ao", [128, 8, T], BF16)
    for kt in range(8):
        k.dma(o[:, kt, :], d_o[kt], q="sp" if kt % 2 == 0 else "act")
    tk.proj_acc("att_wo", o, KT, KT, tk.add_to_x)
    k.barrier()
    k.aoff = mark


STEPS["attnpost"] = attn_post


def host_attn(host):
    f = np.float32
    inp = host.inp
    w = np.asarray(inp["diff_w_qkv"], f)[0]
    host.common["att_wqk"] = _acc_tiles(w[:, 0:2048], KT, 16)
    host.common["att_wv"] = np.ascontiguousarray(w[:, 2048:3072].reshape(KT, 128, 2, 512).transpose(2, 1, 0, 3)).reshape(2, 128, KT * 512)
    host.common["att_wo"] = _acc_tiles(np.asarray(inp["diff_w_o"], f)[0], KT, KT)
    d = np.arange(128) % 64
    half = 8
    invf = np.where(d < 16, ROPE_THETA_ ** (-(d % half).astype(np.float64) / half), 0.0).astype(f)
    host.common["rope_invf"] = np.ascontiguousarray(invf[:, None])
    perm = np.zeros((128, 128), f)
    for p in range(128):
        dd = p % 64
        if dd < 8:
            perm[p + 8, p] = -1.0
        elif dd < 16:
            perm[p - 8, p] = 1.0
    host.common["rope_perm"] = perm
    pos = np.asarray(inp["positions"])[0].astype(np.int32)
    for c in range(NCORE):
        host.percore[c]["pos_rep"] = np.ascontiguousarray(np.broadcast_to(pos[None, c * T:(c + 1) * T], (128, T)))
        lamv = np.stack([np.asarray(inp[n], f)[0] for n in ("diff_lam_q1", "diff_lam_k1", "diff_lam_q2", "diff_lam_k2")], 0)
        host.percore[c]["a_lamv"] = np.ascontiguousarray(np.broadcast_to(lamv[None], (128, 4, 64)))
        host.percore[c]["a_subg"] = np.ascontiguousarray(np.asarray(inp["diff_subln"], f)[0][:, None])
        host.percore[c]["a_posq"] = np.ascontiguousarray(np.broadcast_to(pos[None, 0:512], (128, 512)))
        host.percore[c]["a_posk"] = np.ascontiguousarray(pos[0:512].reshape(4, 128).T)


HOST_EXTRA.append(host_attn)


GLA_H = 4
GLA_DKH = 128
GLA_DVH = 256


def gla_pre(tk, arg):
    P, k, PS = tk.P, tk.k, tk.P.PS
    mark = k.aoff
    h = k.take("h", [128, KT, T], BF16)
    tk.rmsnorm(1, h, "gla")
    o_q = P.dout("qT", [4, 128, T], BF16)
    o_g = P.dout("gT", [8, 128, T], BF16)
    o_k = P.dout("ktm", [16, 128, 512], BF16)
    o_v = P.dout("vtm", [16, 128, 1024], BF16)
    o_la = P.dout("latm", [16, 128, 512], BF16)
    stq = [k.take("stq%d" % i, [128, T], BF16) for i in range(2)]

    def sink_qg(mo, n, ts, po, cnt):
        st = stq[mo % 2]
        k.copy(st[:, ts], po.v, eng="act" if cnt % 2 == 0 else "dve")
        if n == NT - 1:
            dst = o_q[mo] if mo < 4 else o_g[mo - 4]
            P.out_ops.append(k.dma(dst, st.v, q="sp"))
    tk.proj_acc("gla_wqg", h, KT, 12, sink_qg)

    d_wa = P.din("gla_walo", [128, KT * 16])
    wa = k.take("wa", [128, KT, 16], BF16)
    k.dma(wa.v.rr("p a b -> p (a b)"), d_wa.v, q="pool")
    alo = k.take("alo", [16, T], BF16)
    for n in range(NT):
        ts = slice(n * 512, (n + 1) * 512)
        ps = PS[n % 2]
        for kt in range(KT):
            k.mm(ps[0:16, :], wa[:, kt, :], h[:, kt, ts], start=(kt == 0), stop=(kt == KT - 1))
        k.copy(alo[:, ts], ps[0:16, :])

    d_wkv = P.din("gla_wkv", [3, 128, KT * 512])
    d_wa2 = P.din("gla_wa2", [16, 512])
    d_ba = P.din("gla_ba", [1, 512])
    wa2 = k.take("wa2", [16, 512], BF16)
    ba = k.take("ba", [1, 512], BF16)
    k.dma(wa2.v, d_wa2.v, q="pool")
    k.dma(ba.v, d_ba.v, q="pool")
    wkv = [k.take("wkv%d" % i, [128, KT, 512], BF16) for i in range(2)]
    tst = [k.take("tst%d" % i, [128, 512], BF16) for i in range(4)]
    cnt = 0
    for ci in range(3):
        w = wkv[ci % 2]
        k.dma(w.v.rr("p a b -> p (a b)"), d_wkv[ci], q="pool")
        for tt in range(16):
            ps = PS[2 + cnt % 2]
            for kt in range(KT):
                k.mm(ps.v, h[:, kt, tt * 128:(tt + 1) * 128], w[:, kt, :], start=(kt == 0), stop=(kt == KT - 1))
            st = tst[cnt % 4]
            k.copy(st.v, ps.v, eng="act" if cnt % 2 == 0 else "dve")
            dst = o_k[tt] if ci == 0 else o_v[tt][:, (ci - 1) * 512:ci * 512]
            P.out_ops.append(k.dma(dst, st.v, q="sp" if cnt % 2 == 0 else "act"))
            cnt += 1
    lt = [k.take("lt%d" % i, [128, 512], F32) for i in range(2)]
    for tt in range(16):
        ps = PS[4 + tt % 2]
        k.mm(ps.v, alo[:, tt * 128:(tt + 1) * 128], wa2.v, start=True, stop=False)
        k.mm(ps.v, tk.ones[0:1, 0:128], ba.v, start=False, stop=True)
        t_ = lt[tt % 2]
        k.act(t_.v, ps.v, AF.Exp, scale=-1.0)
        k.ts(t_.v, t_.v, 1.0, None, op0=ALU.add)
        k.act(t_.v, t_.v, AF.Ln)
        st = tst[tt % 4]
        k.ts(st.v, t_.v, -1.0 / 16.0, None, op0=ALU.mult)
        P.out_ops.append(k.dma(o_la[tt], st.v, q="sp" if tt % 2 == 0 else "act"))
    k.barrier()
    k.aoff = mark


STEPS["glapre"] = gla_pre


def gla_core(P, cfg):
    k, PS = P.k, P.PS
    d_q = P.din("g_qT", [128, SEQ], BF16)
    d_k = P.din("g_k", [128, 128 * 128], BF16)
    d_la = P.din("g_la", [128, 128 * 128], BF16)
    d_v = P.din("g_v", [128, 128 * 128], BF16)
    d_u2 = P.din("g_u2", [128, 128])
    d_ind = P.din("g_ind", [128, 2])
    d_o = P.dout("oT", [128, SEQ], BF16)
    qQ = [k.take("q%d" % i, [128, SEQ // 4], BF16) for i in range(4)]
    kQ = [k.take("kk%d" % i, [128, 32, 128], BF16) for i in range(4)]
    lQ = [k.take("la%d" % i, [128, 32, 128], BF16) for i in range(4)]
    vQ = [k.take("v%d" % i, [128, 32, 128], BF16) for i in range(4)]
    oT = k.take("oT", [128, SEQ], BF16)
    u2 = k.take("u2", [128, 128], BF16)
    ind = k.take("ind", [128, 2], BF16)
    state = k.take("state", [128, 128], F32)
    stb = [k.take("stb%d" % i, [128, 128], BF16) for i in range(2)]
    er = [k.take("er%d" % i, [128, 128], F32) for i in range(2)]
    kd = [k.take("kd%d" % i, [128, 128], BF16) for i in range(2)]
    dec = [k.take("dec%d" % i, [128, 2], F32) for i in range(2)]
    k.dma(u2.v, d_u2.v, q="pool")
    k.dma(ind.v, d_ind.v, q="pool")
    for i in range(4):
        sl = slice(i * 4096, (i + 1) * 4096)
        k.dma(lQ[i].v.rr("p a b -> p (a b)"), d_la[:, sl], q="sp")
        k.dma(kQ[i].v.rr("p a b -> p (a b)"), d_k[:, sl], q="act")
        k.dma(vQ[i].v.rr("p a b -> p (a b)"), d_v[:, sl], q="sp")
        k.dma(qQ[i].v, d_q[:, sl], q="act")
    k.memset(state.v, 0.0)
    SC = float(GLA_DKH) ** -0.5
    def tile_prep(j):
        prv = PS[j % 2]
        ptt = PS[2 + j % 2]
        k.mm(prv[:, 0:128], u2.v, lQ[j // 32][:, j % 32, :])
        k.mm(ptt[:, 0:2], lQ[j // 32][:, j % 32, :], ind.v)
        e = er[j % 2]
        k.act(e.v, prv[:, 0:128], AF.Exp)
        k.tt(kd[j % 2].v, e.v, kQ[j // 32][:, j % 32, :], ALU.mult)
        k.act(dec[j % 2].v, ptt[:, 0:2], AF.Exp)

    def upd(c):
        j, ch = c // 2, c % 2
        rows = slice(64 * ch, 64 * ch + 64)
        k.mm(PS[4 + c % 2][:, 0:128], kd[j % 2][rows, :], vQ[j // 32][rows, j % 32, :])

    tile_prep(0)
    upd(0)
    NCHK = 256
    for c in range(NCHK):
        j, ch = c // 2, c % 2
        if ch == 0 and j + 1 < 128:
            tile_prep(j + 1)
        if c + 1 < NCHK:
            upd(c + 1)
        po = PS[6 + (c // 8) % 2]
        k.stt(state.v, state.v, dec[j % 2][:, ch:ch + 1], PS[4 + c % 2][:, 0:128], ALU.mult, ALU.add)
        sb_ = stb[c % 2]
        k.copy(sb_.v, state.v, eng="act")
        off = (c % 8) * 64
        k.mm(po[:, off:off + 64], sb_.v, qQ[c // 64][:, (c % 64) * 64:(c % 64 + 1) * 64])
        if c % 8 == 7:
            n = c // 8
            k.ts(oT[:, n * 512:(n + 1) * 512], po.v, SC, None, op0=ALU.mult)
    for i in range(4):
        sl = slice(i * 4096, (i + 1) * 4096)
        P.out_ops.append(k.dma(d_o[:, sl], oT[:, sl], q="sp" if i % 2 == 0 else "act"))


STEPS["glacore"] = gla_core


def gla_post(tk, arg):
    P, k, PS = tk.P, tk.k, tk.P.PS
    mark = k.aoff
    d_o = P.din("goT", [8, 128, T], BF16)
    d_g = P.din("gT", [8, 128, T], BF16)
    d_ng = P.din("gla_ng", [128, 8])
    o = k.take("o", [128, 8, T], BF16)
    g = k.take("g", [128, 8, T], BF16)
    og = k.take("og", [128, 8, T], BF16)
    ng = k.take("ng", [128, 8], F32)
    k.dma(ng.v, d_ng.v)
    for kt in range(8):
        k.dma(o[:, kt, :], d_o[kt], q="sp")
        k.dma(g[:, kt, :], d_g[kt], q="act")
    sq = [k.take("gsq%d" % i, [128, 512], BF16) for i in range(2)]
    rs = [k.take("grs%d" % i, [128, 512], F32) for i in range(2)]
    sg = [k.take("gsg%d" % i, [128, 512], F32) for i in range(2)]
    cnt = 0
    for hd in range(4):
        for n in range(NT):
            ts = slice(n * 512, (n + 1) * 512)
            ps = PS[cnt % 2]
            for e in range(2):
                s = sq[e]
                k.act(s.v, o[:, 2 * hd + e, ts], AF.Square)
                k.mm(ps.v, tk.ones.v, s.v, start=(e == 0), stop=(e == 1))
            r = rs[cnt % 2]
            k.ts(r.v, ps.v, 1.0 / GLA_DVH, EPS, op0=ALU.mult, op1=ALU.add)
            k.act(r.v, r.v, AF.Sqrt)
            k.recip(r.v, r.v)
            for e in range(2):
                kt = 2 * hd + e
                s_ = sg[e]
                k.act(s_.v, g[:, kt, ts], AF.Silu)
                k.stt(s_.v, o[:, kt, ts], ng[:, kt:kt + 1], s_.v, ALU.mult, ALU.mult)
                k.tt(og[:, kt, ts], s_.v, r.v, ALU.mult)
            cnt += 1
    tk.proj_acc("gla_wo", og, KT, KT, tk.add_to_x)
    k.barrier()
    k.aoff = mark


STEPS["glapost"] = gla_post


def host_gla(host):
    f = np.float32
    inp = host.inp
    u2 = np.zeros((128, 128), f)
    for t2 in range(128):
        for t in range(128):
            if t2 > t and t2 // 64 == t // 64:
                u2[t2, t] = 1.0
    host.common["g_u2"] = u2
    host.common["g_ind"] = (np.arange(128)[:, None] // 64 == np.arange(2)[None, :]).astype(f)
    win = np.asarray(inp["gla_w_in"], f)[0]
    host.common["gla_wqg"] = _acc_tiles(np.concatenate([win[:, 0:512], win[:, 2048:3072]], 1), KT, 12)
    host.common["gla_walo"] = np.ascontiguousarray(win[:, 3072:3088].reshape(KT, 128, 16).transpose(1, 0, 2)).reshape(128, KT * 16)
    kv = win[:, 512:2048]
    host.common["gla_wkv"] = np.ascontiguousarray(kv.reshape(KT, 128, 3, 512).transpose(2, 1, 0, 3)).reshape(3, 128, KT * 512)
    host.common["gla_wa2"] = np.ascontiguousarray(np.asarray(inp["gla_w_a2"], f)[0])
    host.common["gla_ba"] = np.ascontiguousarray(np.asarray(inp["gla_b_a"], f)[0][None, :])
    host.common["gla_ng"] = np.ascontiguousarray(np.asarray(inp["gla_norm"], f)[0].reshape(8, 128).T)
    host.common["gla_wo"] = _acc_tiles(np.asarray(inp["gla_w_o"], f)[0], KT, KT)


HOST_EXTRA.append(host_gla)


def _tokmajor_to_core(outs, name, c0, ncol=128):
    a = np.concatenate([np.asarray(outs[r][name])[:, :, c0:c0 + ncol] for r in range(NCORE)], 0)
    return np.ascontiguousarray(a.transpose(1, 0, 2)).reshape(128, 128 * ncol)


TWO_PI = 6.283185307179586
PI = 3.141592653589793


def s5_core(P, idx, PS):
    k = P.k
    s5_consts(P)
    nm = "s5_%d_" % idx
    d_lamP = P.din(nm + "lamP", [64, 2, 8])
    d_dtP = P.din(nm + "dtP", [64, 8])
    d_bP = P.din(nm + "bP", [64, 2, 8, 16])
    d_cP = P.din(nm + "cP", [64, 2, 8, 16])
    d_lamR = P.din(nm + "lamR", [128, 2, 64])
    d_dtR = P.din(nm + "dtR", [128, 1])
    d_bR = P.din(nm + "bR", [128, 2, 64])
    d_dv = P.din(nm + "dvec", [128, 1])
    d_u = P.din("u", [128, SEQ], BF16)
    d_z = k.dram("z", [128, SEQ], BF16, kind="ExternalOutput")
    jt, bmask, mbd, ident = P.jtab, P.bmask, P.mbd, P.ident
    b0 = k.aoff
    oA = b0
    oB = oA + 8192
    oC = oB + 8192
    oD = oC + 2048
    oE = oD + 4096
    oF = oE + 4224
    assert oF + 6400 <= k.awords, (oF, k.awords)

    class Cur:
        def __init__(self, off):
            self.off = off

        def take(self, name, shape, dtype):
            b = k.take_at(name, shape, dtype, self.off)
            self.off += b.words
            return b

    u = k.take_at("u", [128, SEQ], BF16, oA)
    for i in range(8):
        k.dma(u[:, i * T:(i + 1) * T], d_u[:, i * T:(i + 1) * T], q="sp" if i % 2 == 0 else "act")

    def powtable(cur, np_, lam, ldt_ap, shape3, tag, lo, nl):
        a, b = shape3
        dt = cur.take(tag + "dt", [np_, a, b], F32)
        k.act(dt.v, ldt_ap, AF.Exp)
        lrdt = cur.take(tag + "lrdt", [np_, a, b], F32)
        lidt = cur.take(tag + "lidt", [np_, a, b], F32)
        k.tt(lrdt.v, lam[0], dt.v, ALU.mult)
        k.tt(lidt.v, lam[1], dt.v, ALU.mult)
        mag = cur.take(tag + "mag", [np_, a, nl, b], F32)
        Pre = cur.take(tag + "Pre", [np_, a, nl, b], F32)
        tmp = cur.take(tag + "tmp", [np_, a, nl, b], F32)
        Pim = cur.take(tag + "Pim", [np_, a, nl, b], F32)
        for ai in range(a):
            jv = V(jt, jt.ap0[0:np_, lo:lo + nl].unsqueeze(2).to_broadcast([np_, nl, b]))
            lr_b = V(lrdt, lrdt.ap0[:, ai, :].unsqueeze(1).to_broadcast([np_, nl, b]))
            li_b = V(lidt, lidt.ap0[:, ai, :].unsqueeze(1).to_broadcast([np_, nl, b]))
            k.tt(mag[:, ai, :, :], jv, lr_b, ALU.mult)
            k.tt(Pre[:, ai, :, :], jv, li_b, ALU.mult)
        magf = mag.v.rr("p a l b -> p (a l b)")
        tmpf = tmp.v.rr("p a l b -> p (a l b)")
        pref = Pre.v.rr("p a l b -> p (a l b)")
        pimf = Pim.v.rr("p a l b -> p (a l b)")
        k.act(magf, magf, AF.Exp)
        scr = cur.take(tag + "scr", [np_, a, nl, b], F32)
        scrf = scr.v.rr("p a l b -> p (a l b)")
        tmpi = V(tmp, tmp.ap0.bitcast(I32).rearrange("p a l b -> p (a l b)"))

        def reduce_sin(dst, src):
            k.ts(tmpi, src, 1.0 / TWO_PI, None, op0=ALU.mult)
            k.copy(scrf, tmpi)
            k.stt(scrf, scrf, -TWO_PI, src, ALU.mult, ALU.add)
            k.ts(tmpf, scrf, PI, TWO_PI, op0=ALU.is_gt, op1=ALU.mult)
            k.tt(scrf, scrf, tmpf, ALU.subtract)
            k.ts(tmpf, scrf, -PI, -TWO_PI, op0=ALU.is_lt, op1=ALU.mult)
            k.tt(scrf, scrf, tmpf, ALU.subtract)
            k.act(dst, scrf, AF.Sin)
        reduce_sin(pimf, pref)
        k.tt(pimf, pimf, magf, ALU.mult)
        k.ts(pref, pref, 0.5 * PI, None, op0=ALU.add)
        reduce_sin(pref, pref)
        k.tt(pref, pref, magf, ALU.mult)
        return Pre, Pim

    def bbar(cur, np_, lam, abre, abim, bre, bim, shp, tag, bcast, out_re, out_im):
        a, b = lam[0].ap.shape[1], lam[0].ap.shape[2]
        t = [cur.take(tag + "f%d" % i, [np_, a, b], F32) for i in range(5)]
        lr, li = lam
        k.tt(t[0].v, lr, lr, ALU.mult)
        k.tt(t[1].v, li, li, ALU.mult)
        k.tt(t[0].v, t[0].v, t[1].v, ALU.add)
        k.recip(t[0].v, t[0].v)
        k.ts(t[1].v, abre, -1.0, None, op0=ALU.add)
        k.tt(t[2].v, t[1].v, lr, ALU.mult)
        k.tt(t[3].v, abim, li, ALU.mult)
        k.tt(t[2].v, t[2].v, t[3].v, ALU.add)
        k.tt(t[2].v, t[2].v, t[0].v, ALU.mult)
        k.tt(t[3].v, abim, lr, ALU.mult)
        k.tt(t[4].v, t[1].v, li, ALU.mult)
        k.tt(t[3].v, t[3].v, t[4].v, ALU.subtract)
        k.tt(t[3].v, t[3].v, t[0].v, ALU.mult)
        tm = cur.take(tag + "bbt", shp, F32)
        fre, fim = bcast(t[2]), bcast(t[3])
        k.tt(out_re.v, fre, bre, ALU.mult)
        k.tt(tm.v, fim, bim, ALU.mult)
        k.tt(out_re.v, out_re.v, tm.v, ALU.subtract)
        k.tt(out_im.v, fre, bim, ALU.mult)
        k.tt(tm.v, fim, bre, ALU.mult)
        k.tt(out_im.v, out_im.v, tm.v, ALU.add)

    cB = Cur(oB)
    XT = [cB.take("XT%d" % i, [128, 64, 64], BF16) for i in range(2)]
    c2 = Cur(oD)
    lamR = c2.take("lamR", [128, 2, 64], F32)
    dtR = c2.take("dtR", [128, 1], F32)
    bR = c2.take("bR", [128, 2, 64], F32)
    bbRre = c2.take("bbRre", [128, 1, 64], F32)
    bbRim = c2.take("bbRim", [128, 1, 64], F32)
    k.dma(lamR.v, d_lamR.v)
    k.dma(dtR.v, d_dtR.v)
    k.dma(bR.v, d_bR.v)
    lamRv = (lamR[:, 0:1, :], lamR[:, 1:2, :])
    dtRb = V(dtR, dtR.ap0.unsqueeze(2).to_broadcast([128, 1, 64]))
    c2b = Cur(c2.off)
    P1re, P1im = powtable(c2b, 128, lamRv, dtRb, (1, 64), "R1", 1, 1)
    bbar(c2b, 128, lamRv, P1re[:, :, 0, :], P1im[:, :, 0, :], bR[:, 0:1, :], bR[:, 1:2, :], [128, 1, 64], "R", lambda t: t.v, bbRre, bbRim)
    k.barrier()
    for hf in range(2):
        c2c = Cur(c2.off)
        PreR, PimR = powtable(c2c, 128, lamRv, dtRb, (1, 64), "R%d" % hf, 32 * hf, 32)
        t1 = c2c.take("Rt1", [128, 32, 64], F32)
        t2 = c2c.take("Rt2", [128, 32, 64], F32)
        bre_b = V(bbRre, bbRre.ap0[:, 0, :].unsqueeze(1).to_broadcast([128, 32, 64]))
        bim_b = V(bbRim, bbRim.ap0[:, 0, :].unsqueeze(1).to_broadcast([128, 32, 64]))
        ls = slice(32 * hf, 32 * hf + 32)
        k.tt(t1.v, PreR[:, 0, :, :], bre_b, ALU.mult)
        k.tt(t2.v, PimR[:, 0, :, :], bim_b, ALU.mult)
        k.tt(XT[0][:, ls, :], t1.v, t2.v, ALU.subtract)
        k.tt(t1.v, PreR[:, 0, :, :], bim_b, ALU.mult)
        k.tt(t2.v, PimR[:, 0, :, :], bre_b, ALU.mult)
        k.tt(XT[1][:, ls, :], t1.v, t2.v, ALU.add)
        k.barrier()

    P.checkpoint("tabR", XT[0].v.rr("p l q -> p (l q)"))
    cC = Cur(oC)
    lamP = cC.take("lamP", [64, 2, 8], F32)
    dtP = cC.take("dtP", [64, 8], F32)
    bP = cC.take("bP", [64, 2, 8, 16], F32)
    cP = cC.take("cP", [64, 2, 8, 16], F32)
    dv = cC.take("dv", [128, 1], F32)
    bbPre = cC.take("bbPre", [64, 8, 16], F32)
    bbPim = cC.take("bbPim", [64, 8, 16], F32)
    A64 = cC.take("A64", [64, 3, 8], F32)
    cb = cC.take("cb", [64, 2, 128], BF16)
    assert cC.off <= oD
    k.dma(lamP.v, d_lamP.v)
    k.dma(dtP.v, d_dtP.v)
    k.dma(bP.v.rr("p a b c -> p (a b c)"), d_bP.v.rr("p a b c -> p (a b c)"))
    k.dma(cP.v.rr("p a b c -> p (a b c)"), d_cP.v.rr("p a b c -> p (a b c)"))
    k.dma(dv.v, d_dv.v)
    lamPv = (lamP[:, 0, :].rr("p (g o) -> p g o", o=1), lamP[:, 1, :].rr("p (g o) -> p g o", o=1))
    dtPv = dtP.v.rr("p (g o) -> p g o", o=1)
    c3 = Cur(oE)
    PreP, PimP = powtable(c3, 64, lamPv, dtPv, (8, 1), "P", 0, 65)
    k.copy(A64[:, 0, :], PreP[:, :, 64, 0])
    k.copy(A64[:, 1, :], PimP[:, :, 64, 0])
    k.ts(A64[:, 2, :], PimP[:, :, 64, 0], -1.0, None, op0=ALU.mult)
    bbar(c3, 64, lamPv, PreP[:, :, 1, :], PimP[:, :, 1, :], bP[:, 0, :, :], bP[:, 1, :, :], [64, 8, 16], "P",
         lambda t: V(t, t.ap0[:, :, 0].unsqueeze(2).to_broadcast([64, 8, 16])), bbPre, bbPim)
    k.copy(cb[:, 0, :], cP[:, 0, :, :].rr("p g i -> p (g i)"))
    k.ts(cb[:, 1, :], cP[:, 1, :, :].rr("p g i -> p (g i)"), -1.0, None, op0=ALU.mult)
    PreQ = cC.take("PreQ", [64, 8, 65, 1], F32)
    PimQ = cC.take("PimQ", [64, 8, 65, 1], F32)
    assert cC.off <= oD, (cC.off, oD)
    k.copy(PreQ.v, PreP.v)
    k.copy(PimQ.v, PimP.v)
    k.barrier()
    P.checkpoint("tabP", PreQ.v.rr("p g l o -> p (g l o)"))

    Kb = k.take_at("Kb", [128, 64, 128], BF16, oD)
    c6 = Cur(oE)
    kt0 = c6.take("kt0", [128, 128], F32)
    Xc = [[c6.take("Xc%d_%d" % (i, ri), [64, 16, 8, 16], BF16) for ri in range(2)] for i in range(2)]
    xt1 = c6.take("xt1", [64, 16, 8, 16], F32)
    xt2 = c6.take("xt2", [64, 16, 8, 16], F32)
    for qq in range(4):
        X = Xc[qq % 2]
        lo = 16 * qq

        def pw_b(Pt):
            return V(Pt, Pt.ap0[:, :, lo:lo + 16, 0].rearrange("p g l -> p l g").unsqueeze(3).to_broadcast([64, 16, 8, 16]))

        def gi_b(Bt):
            return V(Bt, Bt.ap0.unsqueeze(1).to_broadcast([64, 16, 8, 16]))
        k.tt(xt1.v, pw_b(PreQ), gi_b(bbPre), ALU.mult)
        k.tt(xt2.v, pw_b(PimQ), gi_b(bbPim), ALU.mult, eng="pool")
        k.tt(X[0].v, xt1.v, xt2.v, ALU.subtract)
        k.tt(xt1.v, pw_b(PreQ), gi_b(bbPim), ALU.mult)
        k.tt(xt2.v, pw_b(PimQ), gi_b(bbPre), ALU.mult, eng="pool")
        k.tt(X[1].v, xt1.v, xt2.v, ALU.add)
        for li in range(16):
            lag = lo + li
            ps = PS[lag % 2]
            k.mm(ps[:, 0:128], X[0][:, li, :, :].rr("p g i -> p (g i)"), cb[:, 0, :], start=True, stop=False)
            k.mm(ps[:, 0:128], X[1][:, li, :, :].rr("p g i -> p (g i)"), cb[:, 1, :], start=False, stop=True)
            if lag == 0:
                k.tt(kt0.v, ps[:, 0:128], mbd.v, ALU.mult)
                k.stt(Kb[:, 0, :], ident.v, dv[:, 0:1], kt0.v, ALU.mult, ALU.add)
            else:
                k.tt(Kb[:, lag, :], ps[:, 0:128], mbd.v, ALU.mult)
    k.barrier()
    P.checkpoint("kb", Kb.v.rr("p l q -> p (l q)")[:, 0:4096])

    NCH = SEQ // 64
    S = k.take_at("S", [64, 2, 8, NCH + 1], F32, oE)
    c4 = Cur(oF)
    wbd = [c4.take("wbd%d" % i, [128, 4, 2, 8, 64], BF16) for i in range(2)]
    Bch = c4.take("Bch", [128, 2, 2, 512], F32)
    st1 = c4.take("st1", [64, 2, 8], F32)
    st2 = c4.take("st2", [64, 2, 8], F32)
    bm_b = V(bmask, bmask.ap0.unsqueeze(2).to_broadcast([128, 8, 64]))
    psb = [[PS[4 + blk * 2 + ri] for ri in range(2)] for blk in range(2)]
    for q in range(16):
        w = wbd[q % 2]
        for li in range(4):
            lag = 4 * q + li
            for ri in range(2):
                k.tt(w[:, li, ri, :, :], V(XT[ri], XT[ri].ap0[:, lag, :].unsqueeze(1).to_broadcast([128, 8, 64])), bm_b, ALU.mult,
                     eng="dve" if ri == 0 else "pool")
        for li in range(4):
            lag = 4 * q + li
            jp = 63 - lag
            for blk in range(2):
                lhs = u.v.rr("p (c j) -> p c j", j=64)[:, blk * 128:(blk + 1) * 128, jp]
                for ri in range(2):
                    k.mm(psb[blk][ri].v, lhs, w[:, li, ri, :, :].rr("p g q -> p (g q)"), start=(lag == 0), stop=(lag == 63))
    for blk in range(2):
        for ri in range(2):
            k.copy(Bch[:, blk, ri, :], psb[blk][ri].v, eng="act" if ri == 0 else "dve")
    k.memset(S[:, :, :, 0:1], 0.0)
    cnt = 0
    for blk in range(2):
        for ri in range(2):
            for g in range(8):
                ps = PS[cnt % 4]
                k.transpose(ps[0:64, 0:128], Bch[:, blk, ri, g * 64:(g + 1) * 64], ident.v)
                k.copy(S[:, ri, g, 1 + blk * 128: 1 + (blk + 1) * 128], ps[0:64, 0:128], eng="act" if cnt % 2 == 0 else "dve")
                cnt += 1
    k.barrier()
    P.checkpoint("states", S.v.rr("p r g c -> p (r g c)")[:, 0:4096])

    ytoep = k.take_at("ytoep", [128, SEQ], BF16, oB)
    cs = Cur(c4.off)
    NB, BM = 16, 16
    sa = cs.take("sa", [64, 2, 8, NB], F32)
    sb2 = cs.take("sb2", [64, 2, 8, NB], F32)
    carry = cs.take("carry", [64, 2, 8, NB], F32)
    Apw = cs.take("Apw", [64, 3, 8], F32)
    apt = cs.take("apt", [64, 3, 8], F32)
    SV = S.v[:, :, :, 1:NCH + 1].rr("p r g (b m) -> p r g b m", m=BM)

    def cmul(dst, src, A, nx):
        ab = V(A, A.ap0[:, 0, :].unsqueeze(1).unsqueeze(3).to_broadcast([64, 2, 8, nx]))
        nim = V(A, A.ap0[:, 2, :].unsqueeze(2).to_broadcast([64, 8, nx]))
        pim = V(A, A.ap0[:, 1, :].unsqueeze(2).to_broadcast([64, 8, nx]))
        k.tt(sa[:, :, :, 0:nx], src, ab, ALU.mult)
        k.tt(sb2[:, 0, :, 0:nx], src[:, 1], nim, ALU.mult)
        k.tt(sb2[:, 1, :, 0:nx], src[:, 0], pim, ALU.mult)
        k.tt(dst, sa[:, :, :, 0:nx], sb2[:, :, :, 0:nx], ALU.add)

    k.copy(Apw.v, A64.v)
    for _ in range(4):
        k.tt(apt[:, 0, :], Apw[:, 0, :], Apw[:, 0, :], ALU.mult)
        k.tt(apt[:, 1, :], Apw[:, 1, :], Apw[:, 1, :], ALU.mult)
        k.tt(apt[:, 2, :], Apw[:, 0, :], Apw[:, 1, :], ALU.mult)
        k.tt(Apw[:, 0, :], apt[:, 0, :], apt[:, 1, :], ALU.subtract)
        k.ts(Apw[:, 1, :], apt[:, 2, :], 2.0, None, op0=ALU.mult)
        k.ts(Apw[:, 2, :], apt[:, 2, :], -2.0, None, op0=ALU.mult)
    for i in range(1, BM):
        cmul(carry.v, SV[:, :, :, :, i - 1], A64, NB)
        k.tt(SV[:, :, :, :, i], SV[:, :, :, :, i], carry.v, ALU.add)
    for blk in range(1, NB):
        cmul(carry[:, :, :, 0:1], SV[:, :, :, blk - 1:blk, BM - 1], Apw, 1)
        k.tt(SV[:, :, :, blk:blk + 1, BM - 1], SV[:, :, :, blk:blk + 1, BM - 1], carry[:, :, :, 0:1], ALU.add)
    k.copy(carry[:, :, :, 0:NB - 1], SV[:, :, :, 0:NB - 1, BM - 1])
    for i in range(BM - 1):
        cmul(carry[:, :, :, 0:NB - 1], carry[:, :, :, 0:NB - 1], A64, NB - 1)
        k.tt(SV[:, :, :, 1:NB, i], SV[:, :, :, 1:NB, i], carry[:, :, :, 0:NB - 1], ALU.add)
    Sb = k.take_at("Sb", [64, 2, 8, NCH], BF16, Bch.off)
    k.copy(Sb.v, S[:, :, :, 0:NCH])
    cw = Cur(wbd[0].off)
    CLg = [[cw.take("CL%d_%d" % (i, ri), [64, 64, 16], BF16) for ri in range(2)] for i in range(2)]
    ct1 = cw.take("ct1", [64, 64, 16], F32)
    ct2 = cw.take("ct2", [64, 64, 16], F32)
    assert cw.off <= wbd[0].off + 4096
    Yst = k.take_at("Yst", [128, 64, 8, 16], BF16, oE)
    ytb = [k.take_at("ytoep%d" % blk, [128, SEQ // 2], BF16, oB + blk * 4096) for blk in range(2)]
    ycnt = [0]

    def ystate_unit(blk, g):
        CL = CLg[g % 2]
        pre1 = V(PreQ, PreQ.ap0[:, g, 1:65, 0].unsqueeze(2).to_broadcast([64, 64, 16]))
        pim1 = V(PimQ, PimQ.ap0[:, g, 1:65, 0].unsqueeze(2).to_broadcast([64, 64, 16]))
        cre_b = V(cP, cP.ap0[:, 0, g, :].unsqueeze(1).to_broadcast([64, 64, 16]))
        cim_b = V(cP, cP.ap0[:, 1, g, :].unsqueeze(1).to_broadcast([64, 64, 16]))
        k.tt(ct1.v, pre1, cre_b, ALU.mult)
        k.tt(ct2.v, pim1, cim_b, ALU.mult, eng="pool")
        k.tt(CL[0].v, ct1.v, ct2.v, ALU.subtract)
        k.tt(ct1.v, pim1, cre_b, ALU.mult)
        k.tt(ct2.v, pre1, cim_b, ALU.mult, eng="pool")
        k.tt(ct1.v, ct1.v, ct2.v, ALU.add)
        k.ts(CL[1].v, ct1.v, -1.0, None, op0=ALU.mult)
        for pc in range(2):
            ps = PS[4 + ycnt[0] % 2]
            for ri in range(2):
                k.mm(ps.v, Sb[:, ri, g, blk * 128:(blk + 1) * 128],
                     CL[ri][:, pc * 32:(pc + 1) * 32, :].rr("p j i -> p (j i)"), start=(ri == 0), stop=(ri == 1))
            k.copy(Yst[:, pc * 32:(pc + 1) * 32, g, :], V(ps, ps.ap0.rearrange("p (j i) -> p j i", i=16)), eng="act")
            ycnt[0] += 1

    def ystate_fold(blk):
        for j4 in range(16):
            pst = PS[6 + (j4 % 2)]
            pstb = V(pst, pst.ap0.bitcast(BF16)[:, 0:512].rearrange("p (a b) -> p a b", a=4))
            for jj in range(4):
                j = 4 * j4 + jj
                k.transpose(pstb[:, jj, :], Yst[:, j, :, :].rr("p g i -> p (g i)"), P.identb.v)
            dst = ytb[blk].v.rr("p (c j) -> p j c", j=64)[:, 4 * j4:4 * j4 + 4, :]
            k.tt(dst, dst, pstb, ALU.add)

    for n in range(SEQ // 512):
        ps = PS[n % 4]
        ts_ = slice(n * 512, (n + 1) * 512)
        pv = V(ps, ps.ap0.rearrange("p (c j) -> p c j", j=64))
        uv = u[:, ts_].rr("p (c j) -> p c j", j=64)
        for lag in range(64):
            k.mm(pv[:, :, lag:64], Kb[:, lag, :], uv[:, :, 0:64 - lag], start=(lag == 0), stop=(lag == 63))
        blk = n // 16
        k.copy(ytb[blk][:, (n % 16) * 512:(n % 16 + 1) * 512], ps.v, eng="act")
        if n % 2 == 1:
            ystate_unit(blk, (n // 2) % 8)
        if n % 16 == 15:
            ystate_fold(blk)
    k.barrier()
    ytoep = k.take_at("ytoep", [128, SEQ], BF16, oB)
    P.checkpoint("ystate", ytoep[:, 0:4096])

    c7 = Cur(oE)
    y2 = [c7.take("y2%d" % i, [128, 512], F32) for i in range(2)]
    GC = 0.7978845608028654
    for n in range(SEQ // 512):
        ts_ = slice(n * 512, (n + 1) * 512)
        y = ytoep[:, ts_]
        w = y2[n % 2]
        k.act(w.v, y, AF.Square)
        k.ts(w.v, w.v, 0.044715, 1.0, op0=ALU.mult, op1=ALU.add)
        k.tt(w.v, w.v, y, ALU.mult, eng="pool")
        k.act(w.v, w.v, AF.Sigmoid, scale=2.0 * GC)
        k.tt(u[:, ts_], y, w.v, ALU.mult)
    k.barrier()
    z = u


    P.checkpoint("toep", z[:, 0:4096])
    for i in range(8):
        P.out_ops.append(k.dma(d_z[:, i * T:(i + 1) * T], z[:, i * T:(i + 1) * T], q="sp" if i % 2 == 0 else "act"))


def s5_consts(P):
    k = P.k
    if hasattr(P, "s5c"):
        return
    P.s5c = True
    idd = P.din("ident", [128, 128])
    P.ident = k.take("ident", [128, 128], F32)
    k.dma(P.ident.v, idd.v)
    P.identb = k.take("identb", [128, 128], BF16)
    k.copy(P.identb.v, P.ident.v)
    d_j = P.din("jtab", [128, 65])
    d_bm = P.din("blockmask", [128, 8])
    d_mbd = P.din("maskbd", [128, 128])
    P.jtab = k.take("jtab", [128, 65], F32)
    P.bmask = k.take("bmask", [128, 8], F32)
    P.mbd = k.take("mbd", [128, 128], F32)
    k.dma(P.jtab.v, d_j.v)
    k.dma(P.bmask.v, d_bm.v)
    k.dma(P.mbd.v, d_mbd.v)


def host_s5(host):
    f = np.float32
    inp = host.inp
    eye = np.eye(128, dtype=f)
    host.common["ident"] = eye
    host.common["jtab"] = np.broadcast_to(np.arange(65, dtype=f)[None, :], (128, 65)).copy()
    host.common["blockmask"] = (np.arange(128)[:, None] // 16 == np.arange(8)[None, :]).astype(f)
    host.common["maskbd"] = (np.arange(128)[:, None] // 16 == np.arange(128)[None, :] // 16).astype(f)
    for c in range(NCORE):
        m = host.percore[c]
        gs = slice(8 * c, 8 * c + 8)
        for idx in range(2):
            nm = "s5_%d_" % idx
            lre = np.asarray(inp["s5_lam_re"], f)[idx, gs]
            lim = np.asarray(inp["s5_lam_im"], f)[idx, gs]
            ldt = np.asarray(inp["s5_log_dt"], f)[idx, gs]
            bre = np.asarray(inp["s5_b_re"], f)[idx, gs]
            bim = np.asarray(inp["s5_b_im"], f)[idx, gs]
            cre = np.asarray(inp["s5_c_re"], f)[idx, gs]
            cim = np.asarray(inp["s5_c_im"], f)[idx, gs]
            m[nm + "lamP"] = np.ascontiguousarray(np.stack([lre.T, lim.T], 1))
            m[nm + "dtP"] = np.ascontiguousarray(np.broadcast_to(ldt[None, :], (64, 8)))
            m[nm + "bP"] = np.ascontiguousarray(np.stack([bre.transpose(1, 0, 2), bim.transpose(1, 0, 2)], 1))
            m[nm + "cP"] = np.ascontiguousarray(np.stack([cre.transpose(2, 0, 1), cim.transpose(2, 0, 1)], 1))
            m[nm + "lamR"] = np.ascontiguousarray(np.stack([np.repeat(lre, 16, 0), np.repeat(lim, 16, 0)], 1))
            m[nm + "dtR"] = np.ascontiguousarray(np.repeat(ldt, 16)[:, None])
            m[nm + "bR"] = np.ascontiguousarray(np.stack([bre.transpose(0, 2, 1).reshape(128, 64),
                                                          bim.transpose(0, 2, 1).reshape(128, 64)], 1))
            m[nm + "dvec"] = np.ascontiguousarray(np.asarray(inp["s5_d"], f)[idx, 128 * c:128 * c + 128][:, None])


HOST_EXTRA.append(host_s5)
STEPS["s5core"] = lambda P, cfg: s5_core(P, cfg["idx"], P.PS)


def _set_x(host, o):
    host.set("xT", [np.asarray(o[c]["outT"]) for c in range(NCORE)])


def _set_gla_inputs(host, o):
    host.set("gT", [np.asarray(o[c]["gT"]) for c in range(NCORE)])
    host.set("g_qT", [np.ascontiguousarray(np.concatenate([np.asarray(o[r]["qT"])[c // 2] for r in range(NCORE)], 1))
                      for c in range(NCORE)])
    host.set("g_k", [_tokmajor_to_core(o, "ktm", (c // 2) * 128) for c in range(NCORE)])
    host.set("g_la", [_tokmajor_to_core(o, "latm", (c // 2) * 128) for c in range(NCORE)])
    host.set("g_v", [_tokmajor_to_core(o, "vtm", (c // 2) * 256 + (c % 2) * 128) for c in range(NCORE)])


def _set_attn_inputs(host, o):
    host.set("a_qT", [np.ascontiguousarray(np.concatenate([np.asarray(o[r]["aqT"])[c] for r in range(NCORE)], 1)) for c in range(NCORE)])
    host.set("a_kT", [np.ascontiguousarray(np.concatenate([np.asarray(o[r]["akT"])[c] for r in range(NCORE)], 1)) for c in range(NCORE)])
    host.set("a_v", [_tokmajor_to_core(o, "avtm", c * 128) for c in range(NCORE)])


def run_s5_layer(host, l, last=False):
    idx = l // 3
    o = launch(host, {"kind": "tok", "steps": ("norm_out:%d,h" % l,)})
    host.set("u", tok_to_chan(o, "h"))
    o = launch(host, {"kind": "s5core", "idx": idx})
    host.set("zf", chan_to_tok(o, "z"))
    steps = ["s5post:%d" % idx, "ffn:%d" % l]
    if last:
        steps.append("final")
    steps.append("store_x:outT")
    _set_x(host, launch(host, {"kind": "tok", "steps": tuple(steps)}))


def run_gla_layer(host, l):
    o = launch(host, {"kind": "tok", "steps": ("glapre",)})
    _set_gla_inputs(host, o)
    o = launch(host, {"kind": "glacore"})
    host.set("goT", chan_to_tok(o, "oT"))
    _set_x(host, launch(host, {"kind": "tok", "steps": ("glapost", "ffn:%d" % l, "store_x:outT")}))


def run_attn_layer(host, l):
    o = launch(host, {"kind": "tok", "steps": ("attnpre",)})
    _set_attn_inputs(host, o)
    o = launch(host, {"kind": "attncore"})
    host.set("aoTt", chan_to_tok(o, "aoT"))
    _set_x(host, launch(host, {"kind": "tok", "steps": ("attnpost", "ffn:%d" % l, "store_x:outT")}))


def gather_out(host):
    outs = [np.asarray(host.percore[c]["xT"], np.float32).reshape(D, T).T for c in range(NCORE)]
    return np.ascontiguousarray(np.concatenate(outs, 0))[None]


def kernel(**inputs):
    host = Host(inputs)
    o = launch(host, {"kind": "tok", "steps": ("norm_out:0,h",)})
    host.set("u", tok_to_chan(o, "h"))
    o = launch(host, {"kind": "s5core", "idx": 0})
    host.set("zf", chan_to_tok(o, "z"))
    o = launch(host, {"kind": "tok", "steps": ("s5post:0", "ffn:0", "glapre", "store_x:outT")})
    _set_x(host, o)
    _set_gla_inputs(host, o)
    o = launch(host, {"kind": "glacore"})
    host.set("goT", chan_to_tok(o, "oT"))
    o = launch(host, {"kind": "tok", "steps": ("glapost", "ffn:1", "attnpre", "store_x:outT")})
    _set_x(host, o)
    _set_attn_inputs(host, o)
    o = launch(host, {"kind": "attncore"})
    host.set("aoTt", chan_to_tok(o, "aoT"))
    o = launch(host, {"kind": "tok", "steps": ("attnpost", "ffn:2", "norm_out:3,h", "store_x:outT")})
    _set_x(host, o)
    host.set("u", tok_to_chan(o, "h"))
    o = launch(host, {"kind": "s5core", "idx": 1})
    host.set("zf", chan_to_tok(o, "z"))
    o = launch(host, {"kind": "tok", "steps": ("s5post:1", "ffn:3", "final", "store_x:outT")})
    _set_x(host, o)
    return gather_out(host)
```

```python
import numpy as np
from concourse.bass_utils import run_bass_kernel_spmd
import numpy as np
from contextlib import ExitStack
import concourse.bass as bass
import concourse.mybir as mybir

F32 = mybir.dt.float32
BF16 = mybir.dt.bfloat16
I32 = mybir.dt.int32
AF = mybir.ActivationFunctionType
ALU = mybir.AluOpType
AX = mybir.AxisListType

COMPUTE = ("pe", "act", "dve", "pool")


class Buf:
    def __init__(self, name, ap0, space):
        self.name = name
        self.ap0 = ap0
        self.space = space
        self.w = []
        self.r = []

    def __getitem__(self, idx):
        return V(self, self.ap0[idx])

    @property
    def v(self):
        return V(self, self.ap0)


class V:
    def __init__(self, buf, ap):
        self.buf = buf
        self.ap = ap

    def __getitem__(self, idx):
        return V(self.buf, self.ap[idx])

    def rr(self, pat, **kw):
        return V(self.buf, self.ap.rearrange(pat, **kw))


class Op:
    __slots__ = ("id", "eng", "fn", "deps", "is_dma", "is_cc", "inc", "semval", "sem", "raw_same", "prev")

    def __init__(self, id, eng, fn, is_dma=False, is_cc=False):
        self.id = id
        self.eng = eng
        self.fn = fn
        self.deps = set()
        self.raw_same = set()
        self.is_dma = is_dma
        self.is_cc = is_cc
        self.inc = False
        self.semval = None
        self.sem = None


class KB:
    def __init__(self):
        self.nc = bass.Bass("TRN2", target_bir_lowering=False)
        self.ops = []
        self.stack = ExitStack()
        self.nbuf = 0
        self.sb_bytes = 0
        self._bar_from = 0

    def dram(self, name, shape, dtype, kind="Internal"):
        if kind == "Internal":
            t = self.nc.dram_tensor(name, list(shape), dtype)
        else:
            t = self.nc.dram_tensor(name, list(shape), dtype, kind=kind)
        return Buf(name, t.ap(), "dram")

    def sb(self, name, shape, dtype):
        t = self.stack.enter_context(self.nc.sbuf_tensor(name, list(shape), dtype))
        n = 1
        for s in shape[1:]:
            n *= s
        self.sb_bytes += n * (4 if dtype in (F32, I32) else 2)
        return Buf(name, t[:], "sbuf")

    def ps(self, name, shape, dtype=F32):
        t = self.stack.enter_context(self.nc.psum_tensor(name, list(shape), dtype))
        return Buf(name, t[:], "psum")

    def rec(self, eng, fn, reads=(), writes=(), is_dma=False, is_cc=False):
        op = Op(len(self.ops), eng, fn, is_dma, is_cc)
        rb = []
        for x in reads:
            if x is None or isinstance(x, (int, float)):
                continue
            b = x.buf if isinstance(x, V) else x
            if b not in rb:
                rb.append(b)
        wb = []
        for x in writes:
            if x is None:
                continue
            b = x.buf if isinstance(x, V) else x
            if b not in wb:
                wb.append(b)
        for b in rb:
            for d in b.w:
                op.deps.add(d)
                op.raw_same.add(d)
            if b.space == "psum":
                for d in b.r:
                    if self.ops[d].eng != eng:
                        op.deps.add(d)
        for b in wb:
            for d in b.w:
                op.deps.add(d)
            for d in b.r:
                op.deps.add(d)
        for b in rb:
            if b not in wb:
                b.r.append(op.id)
        for b in wb:
            b.w = [op.id]
            b.r = []
        op.deps.discard(op.id)
        self.ops.append(op)
        return op

    def barrier(self):
        last = {}
        pend = []
        for op in self.ops[self._bar_from:]:
            if op.fn is None:
                continue
            if op.is_dma or op.is_cc:
                pend.append(op.id)
            else:
                last[op.eng] = op.id
        self._bar_from = len(self.ops)
        deps = set(pend) | set(last.values())
        for e in ("pe", "act", "dve", "pool", "sp"):
            op = Op(len(self.ops), e, None)
            op.deps = set(deps)
            op.raw_same = set(deps)
            self.ops.append(op)
        self._bar_from = len(self.ops) - 5

    def take(self, name, shape, dtype):
        n = 1
        for s in shape[1:]:
            n *= s
        words = n if dtype in (F32, I32) else (n + 1) // 2
        assert self.aoff + words <= self.awords, (name, self.aoff, words, self.awords)
        ap = self.arena.ap0[0:shape[0], self.aoff:self.aoff + words]
        self.aoff += words
        self.amax = max(self.amax, self.aoff)
        if dtype not in (F32,):
            ap = ap.bitcast(dtype)
        ap = ap[:, 0:n]
        if len(shape) == 3:
            ap = ap.rearrange("p (a b) -> p a b", a=shape[1])
        elif len(shape) == 4:
            ap = ap.rearrange("p (a b c) -> p a b c", a=shape[1], b=shape[2])
        elif len(shape) == 5:
            ap = ap.rearrange("p (a b c d) -> p a b c d", a=shape[1], b=shape[2], c=shape[3])
        return Buf(name, ap, "sbuf")

    def take_at(self, name, shape, dtype, off, pbase=0):
        n = 1
        for x in shape[1:]:
            n *= x
        words = n if dtype in (F32, I32) else (n + 1) // 2
        assert off + words <= self.awords, (name, off, words, self.awords)
        self.amax = max(self.amax, off + words)
        ap = self.arena.ap0[pbase:pbase + shape[0], off:off + words]
        if dtype not in (F32,):
            ap = ap.bitcast(dtype)
        ap = ap[:, 0:n]
        if len(shape) == 3:
            ap = ap.rearrange("p (a b) -> p a b", a=shape[1])
        elif len(shape) == 4:
            ap = ap.rearrange("p (a b c) -> p a b c", a=shape[1], b=shape[2])
        elif len(shape) == 5:
            ap = ap.rearrange("p (a b c d) -> p a b c d", a=shape[1], b=shape[2], c=shape[3])
        b = Buf(name, ap, "sbuf")
        b.words = words
        b.off = off
        return b

    def init_arena(self, words):
        self.arena = self.sb("arena", [128, words], F32)
        self.awords = words
        self.aoff = 0
        self.amax = 0

    @staticmethod
    def _a(x):
        return x.ap if isinstance(x, V) else x

    def mm(self, out, lhsT, rhs, start=True, stop=True):
        a = self._a
        return self.rec("pe", lambda e: e.matmul(a(out), a(lhsT), a(rhs), start=start, stop=stop),
                        reads=[lhsT, rhs], writes=[out])

    def transpose(self, out, in_, ident):
        a = self._a
        return self.rec("pe", lambda e: e.transpose(a(out), a(in_), a(ident)), reads=[in_, ident], writes=[out])

    def act(self, out, in_, func, bias=0.0, scale=1.0, accum=None, eng="act"):
        a = self._a
        kw = {}
        if accum is not None:
            kw["accum_out"] = a(accum)
        return self.rec(eng, lambda e: e.activation(a(out), a(in_), func, bias=a(bias), scale=a(scale), **kw),
                        reads=[in_, bias, scale], writes=[out, accum])

    def tt(self, out, in0, in1, op, eng="dve"):
        a = self._a
        return self.rec(eng, lambda e: e.tensor_tensor(a(out), a(in0), a(in1), op), reads=[in0, in1], writes=[out])

    def ts(self, out, in0, s1, s2=None, op0=ALU.mult, op1=None, accum=None, eng="dve"):
        a = self._a
        kw = {}
        if op1 is not None:
            kw["op1"] = op1
        if accum is not None:
            kw["accum_out"] = a(accum)
        return self.rec(eng, lambda e: e.tensor_scalar(a(out), a(in0), a(s1), a(s2) if s2 is not None else None, op0, **kw),
                        reads=[in0, s1, s2], writes=[out, accum])

    def stt(self, out, in0, scalar, in1, op0, op1, eng="dve"):
        a = self._a
        return self.rec(eng, lambda e: e.scalar_tensor_tensor(a(out), a(in0), a(scalar), a(in1), op0, op1),
                        reads=[in0, scalar, in1], writes=[out])

    def copy(self, out, in_, eng="dve"):
        a = self._a
        if eng == "act":
            return self.rec(eng, lambda e: e.copy(a(out), a(in_)), reads=[in_], writes=[out])
        return self.rec(eng, lambda e: e.tensor_copy(a(out), a(in_)), reads=[in_], writes=[out])

    def memset(self, out, val, eng="dve"):
        a = self._a
        return self.rec(eng, lambda e: e.memset(a(out), val), reads=[], writes=[out])

    def recip(self, out, in_):
        a = self._a
        return self.rec("dve", lambda e: e.reciprocal(a(out), a(in_)), reads=[in_], writes=[out])

    def scan(self, out, d0, d1, initial, op0, op1):
        a = self._a
        return self.rec("dve", lambda e: e.tensor_tensor_scan(a(out), a(d0), a(d1), a(initial), op0, op1),
                        reads=[d0, d1, initial], writes=[out])

    def dma(self, out, in_, q="sp", **kw):
        a = self._a
        return self.rec(q, lambda e: e.dma_start(out=a(out), in_=a(in_), **kw), reads=[in_], writes=[out], is_dma=True)

    def cc(self, kind, ins, outs, op=ALU.bypass, groups=None):
        a = self._a
        g = groups or [list(range(8))]
        return self.rec("pool", lambda e: e.collective_compute(kind, op, replica_groups=g,
                                                                  ins=[a(i) for i in ins], outs=[a(o) for o in outs]),
                        reads=list(ins), writes=list(outs), is_cc=True)

    def emit(self, final_wait_ops=()):
        nc = self.nc
        ops = self.ops
        engs = ["pe", "act", "dve", "pool", "sp"]
        for op in ops:
            for d in op.deps:
                dop = ops[d]
                if dop.eng != op.eng:
                    dop.inc = True
                elif dop.is_dma or dop.is_cc:
                    dop.inc = True
                elif op.eng in ("act", "dve", "pool") and not op.is_dma:
                    dop.inc = True
        for d in final_wait_ops:
            d.inc = True
        NDS = {"sp": 24, "pool": 12, "act": 8, "pe": 1, "dve": 1}
        st = self.stack
        csem = {e: st.enter_context(nc.semaphore("c_" + e)) for e in COMPUTE}
        dsem = {e: [st.enter_context(nc.semaphore("d_%s%d" % (e, i))) for i in range(NDS[e])] for e in engs}
        ccsem = st.enter_context(nc.semaphore("ccsem"))
        ccnt = {e: 0 for e in COMPUTE}
        dcnt = {e: [0] * NDS[e] for e in engs}
        drr = {e: 0 for e in engs}
        cccnt = 0
        for op in ops:
            if op.is_dma:
                i = drr[op.eng] % NDS[op.eng]
                drr[op.eng] += 1
                op.sem = ("d", op.eng, i)
                op.prev = dcnt[op.eng][i]
                dcnt[op.eng][i] += 16
                op.semval = dcnt[op.eng][i]
            elif op.is_cc:
                cccnt += 1
                op.sem = ("cc",)
                op.semval = cccnt
            elif op.inc:
                ccnt[op.eng] += 1
                op.sem = ("c", op.eng)
                op.semval = ccnt[op.eng]

        def semh(s):
            if s[0] == "d":
                return dsem[s[1]][s[2]]
            if s[0] == "cc":
                return ccsem
            return csem[s[1]]

        by_eng = {e: [op for op in ops if op.eng == e] for e in engs}
        self.stats = {e: len(by_eng[e]) for e in engs}
        nwaits = {e: 0 for e in engs}

        def run(eng_name, e):
            seen = {}
            for op in by_eng[eng_name]:
                need = {}
                for d in op.deps:
                    dop = ops[d]
                    if dop.sem is None:
                        continue
                    if dop.eng == eng_name and not (dop.is_dma or dop.is_cc):
                        if not (eng_name in ("act", "dve", "pool") and not op.is_dma):
                            continue
                    if need.get(dop.sem, 0) < dop.semval:
                        need[dop.sem] = dop.semval
                if op.is_dma and op.prev > 0:
                    if need.get(op.sem, 0) < op.prev:
                        need[op.sem] = op.prev
                for s, v in need.items():
                    if seen.get(s, 0) >= v:
                        continue
                    e.wait_ge(semh(s), v)
                    nwaits[eng_name] += 1
                    seen[s] = v
                if op.fn is None:
                    continue
                ins = op.fn(e)
                if op.is_dma:
                    ins.then_inc(semh(op.sem), 16)
                elif op.is_cc:
                    ins.then_inc(semh(op.sem), 1)
                elif op.inc:
                    ins.then_inc(semh(op.sem), 1)
            if eng_name == "sp":
                for d in final_wait_ops:
                    e.wait_ge(semh(d.sem), d.semval)

        with nc.Block() as block:
            @block.tensor
            def _(e):
                run("pe", e)

            @block.scalar
            def _(e):
                run("act", e)

            @block.vector
            def _(e):
                run("dve", e)

            @block.gpsimd
            def _(e):
                run("pool", e)

            @block.sync
            def _(e):
                run("sp", e)
        self.stats["waits"] = nwaits
        self.stack.close()
        return nc
D = 1024
SEQ = 16384
NCORE = 8
T = SEQ // NCORE
NT = T // 512
KT = D // 128
FH = 2816
FM = FH // 128
EPS = 1e-6


class StopBuild(Exception):
    pass


class Prog:
    def __init__(self, cfg):
        self.cfg = cfg
        self.k = KB()
        self.ins = {}
        self.outs = []
        self.out_ops = []
        self.dbg_ops = []
        k = self.k
        k.init_arena(52000)
        if cfg.get("kind") == "attncore":
            self.PS2 = [k.ps("pss%d" % i, [128, 1024], F32) for i in range(2)]
            self.PS = [None] * 4 + [k.ps("ps%d" % i, [128, 512], F32) for i in range(4, 8)]
        else:
            self.PS = [k.ps("ps%d" % i, [128, 512], F32) for i in range(8)]

    def din(self, name, shape, dtype=F32):
        b = self.k.dram(name, shape, dtype, kind="ExternalInput")
        self.ins[name] = (tuple(shape), dtype)
        return b

    def dout(self, name, shape, dtype=F32):
        b = self.k.dram(name, shape, dtype, kind="ExternalOutput")
        self.outs.append(name)
        return b

    def checkpoint(self, name, dump=None):
        if self.cfg.get("stop") != name:
            return
        k = self.k
        k.barrier()
        if dump is not None:
            n = dump.ap.shape[1]
            np_ = dump.ap.shape[0]
            dbg = self.dout("dbg", [128, 4096], F32)
            tmp = k.take_at("dbgtmp", [128, 4096], F32, k.awords - 4096)
            k.memset(tmp.v, 0.0)
            k.copy(tmp[0:np_, 0:n], dump)
            self.dbg_ops.append(k.dma(dbg.v, tmp.v))
        raise StopBuild()

    def finish(self):
        self.nc = self.k.emit(final_wait_ops=self.out_ops + self.dbg_ops)
        return self


class Tok:
    def __init__(self, P):
        self.P = P
        k = P.k
        self.k = k
        xin = P.din("xT", [KT, 128, T])
        gains = P.din("gains", [128, 9, KT])
        self.x = k.take("x", [128, KT, T], F32)
        self.gn = k.take("gn", [128, 9, KT], F32)
        self.ones = k.take("ones", [128, 128], BF16)
        k.dma(self.gn.v, gains.v)
        for kt in range(KT):
            k.dma(self.x[:, kt, :], xin[kt], q="sp" if kt % 2 == 0 else "act")
        k.memset(self.ones.v, 1.0)

    def store_x(self, name="outT"):
        out = self.P.dout(name, [KT, 128, T], F32)
        for kt in range(KT):
            self.P.out_ops.append(self.k.dma(out[kt], self.x[:, kt, :], q="sp" if kt % 2 == 0 else "act"))

    def rstd_tile(self, n, tag):
        k, x, PS = self.k, self.x, self.P.PS
        if not hasattr(self, "_nrm_" + tag):
            setattr(self, "_nrm_" + tag, ([k.take(tag + "sq%d" % i, [128, 512], BF16) for i in range(2)],
                                         [k.take(tag + "rstd%d" % i, [128, 512], F32) for i in range(2)]))
        sq, rstd = getattr(self, "_nrm_" + tag)
        ts = slice(n * 512, (n + 1) * 512)
        ps = PS[n % 2]
        for kt in range(KT):
            s = sq[kt % 2]
            k.act(s.v, x[:, kt, ts], AF.Square)
            k.mm(ps.v, self.ones.v, s.v, start=(kt == 0), stop=(kt == KT - 1))
        r = rstd[n % 2]
        k.ts(r.v, ps.v, 1.0 / D, EPS, op0=ALU.mult, op1=ALU.add)
        k.act(r.v, r.v, AF.Sqrt)
        k.recip(r.v, r.v)
        return r

    def rmsnorm(self, which, h, tag):
        k = self.k
        for n in range(NT):
            ts = slice(n * 512, (n + 1) * 512)
            r = self.rstd_tile(n, tag)
            for kt in range(KT):
                k.stt(h[:, kt, ts], self.x[:, kt, ts], self.gn[:, which, kt:kt + 1], r.v, ALU.mult, ALU.mult)

    def norm_out(self, which, name):
        k = self.k
        mark = k.aoff
        h = k.take("h_" + name, [128, KT, T], BF16)
        self.rmsnorm(which, h, "no" + name)
        out = self.P.dout(name, [KT, 128, T], BF16)
        for kt in range(KT):
            self.P.out_ops.append(k.dma(out[kt], h[:, kt, :], q="sp" if kt % 2 == 0 else "act"))
        k.barrier()
        k.aoff = mark

    def final_norm(self):
        k = self.k
        mark = k.aoff
        for n in range(NT):
            ts = slice(n * 512, (n + 1) * 512)
            r = self.rstd_tile(n, "fin")
            for kt in range(KT):
                k.stt(self.x[:, kt, ts], self.x[:, kt, ts], self.gn[:, 8, kt:kt + 1], r.v, ALU.mult, ALU.mult)
        k.barrier()
        k.aoff = mark

    def proj_gated(self, name, src, nk, nmo, combine, w2=None):
        k, PS = self.k, self.P.PS
        wd = self.P.din(name, [nmo, 128, nk * 256])
        if w2 is None:
            w2 = [k.take(name + "w%d" % i, [128, nk, 2, 128], BF16) for i in range(2)]
        cnt = 0
        for mo in range(nmo):
            w = w2[mo % 2]
            k.dma(w.v.rr("p a b c -> p (a b c)"), wd[mo], q="pool")
            for n in range(NT):
                ts = slice(n * 512, (n + 1) * 512)
                pa = PS[2 + 2 * (cnt % 2)]
                pb = PS[3 + 2 * (cnt % 2)]
                for kt in range(nk):
                    k.mm(pa.v, w[:, kt, 0, :], src[:, kt, ts], start=(kt == 0), stop=(kt == nk - 1))
                for kt in range(nk):
                    k.mm(pb.v, w[:, kt, 1, :], src[:, kt, ts], start=(kt == 0), stop=(kt == nk - 1))
                combine(mo, n, ts, pa, pb, cnt)
                cnt += 1

    def proj_acc(self, name, src, nk, nmo, sink, w2=None):
        k, PS = self.k, self.P.PS
        wd = self.P.din(name, [nmo, 128, nk * 128])
        if w2 is None:
            w2 = [k.take(name + "w%d" % i, [128, nk, 128], BF16) for i in range(2)]
        cnt = 0
        for mo in range(nmo):
            w = w2[mo % 2]
            k.dma(w.v.rr("p a b -> p (a b)"), wd[mo], q="pool")
            for n in range(NT):
                ts = slice(n * 512, (n + 1) * 512)
                po = PS[6 + (cnt % 2)]
                for kt in range(nk):
                    k.mm(po.v, w[:, kt, :], src[:, kt, ts], start=(kt == 0), stop=(kt == nk - 1))
                sink(mo, n, ts, po, cnt)
                cnt += 1

    def add_to_x(self, mo, n, ts, po, cnt):
        self.k.tt(self.x[:, mo, ts], self.x[:, mo, ts], po.v, ALU.add)

    def ffn(self, l):
        k = self.k
        mark = k.aoff
        HG = FM // 2
        h = k.take("h", [128, KT, T], BF16)
        a = k.take("a", [128, HG, T], BF16)
        sg = [k.take("sg%d" % i, [128, 512], F32) for i in range(2)]
        self.rmsnorm(4 + l, h, "ffn%d" % l)
        wg2 = [k.take("wg2_%d" % i, [128, KT, 2, 128], BF16) for i in range(2)]
        wd2 = [k.take("wd2_%d" % i, [128, HG, 128], BF16) for i in range(2)]
        for grp in range(2):
            def comb(mi, n, ts, pg, pu, cnt):
                s = sg[cnt % 2]
                k.act(s.v, pg.v, AF.Silu)
                k.tt(a[:, mi, ts], s.v, pu.v, ALU.mult)
            self.proj_gated("wgu%d_%d" % (l, grp), h, KT, HG, comb, wg2)
            self.proj_acc("wdn%d_%d" % (l, grp), a, HG, KT, self.add_to_x, wd2)
        k.barrier()
        k.aoff = mark

    def s5post(self, idx):
        k = self.k
        mark = k.aoff
        zin = self.P.din("zf", [KT, 128, T], BF16)
        zf = k.take("zf", [128, KT, T], BF16)
        for kt in range(KT):
            k.dma(zf[:, kt, :], zin[kt], q="sp" if kt % 2 == 0 else "act")
        sg = [k.take("sgl%d" % i, [128, 512], F32) for i in range(2)]

        def comb(mo, n, ts, pa, pb, cnt):
            s = sg[cnt % 2]
            k.act(s.v, pb.v, AF.Sigmoid)
            k.tt(s.v, s.v, pa.v, ALU.mult)
            k.tt(self.x[:, mo, ts], self.x[:, mo, ts], s.v, ALU.add)
        self.proj_gated("s5_%d_wglu" % idx, zf, KT, KT, comb)
        k.barrier()
        k.aoff = mark


STEPS = {}


def build(cfg):
    P = Prog(cfg)
    try:
        if cfg["kind"] == "tok":
            tk = Tok(P)
            for st in cfg["steps"]:
                nm, _, arg = st.partition(":")
                if nm == "norm_out":
                    which, name = arg.split(",")
                    tk.norm_out(int(which), name)
                elif nm == "ffn":
                    tk.ffn(int(arg))
                elif nm == "s5post":
                    tk.s5post(int(arg))
                elif nm == "final":
                    tk.final_norm()
                elif nm == "store_x":
                    tk.store_x(arg or "outT")
                else:
                    STEPS[nm](tk, arg)
        else:
            STEPS[cfg["kind"]](P, cfg)
    except StopBuild:
        pass
    return P.finish()


def _pair_tiles(w, nk, nmo):
    w = w.reshape(nk, 128, 2, nmo, 128).transpose(3, 1, 0, 2, 4)
    return np.ascontiguousarray(w).reshape(nmo, 128, nk * 256)


def _acc_tiles(w, nk, nmo):
    w = w.reshape(nk, 128, nmo, 128).transpose(2, 1, 0, 3)
    return np.ascontiguousarray(w).reshape(nmo, 128, nk * 128)


class Host:
    def __init__(self, inp):
        f = np.float32
        self.inp = inp
        g = np.stack([np.asarray(inp["norm_mix"], f)[i] for i in range(4)]
                     + [np.asarray(inp["norm_ffn"], f)[i] for i in range(4)]
                     + [np.asarray(inp["norm_final"], f)], 0)
        self.common = {"gains": np.ascontiguousarray(g.reshape(9, KT, 128).transpose(2, 0, 1))}
        self.percore = [dict() for _ in range(NCORE)]
        X = np.asarray(inp["x"], f)[0]
        for c in range(NCORE):
            self.percore[c]["xT"] = np.ascontiguousarray(X[c * T:(c + 1) * T].T).reshape(KT, 128, T)
        for fn in HOST_EXTRA:
            fn(self)

    def weight(self, name):
        f = np.float32
        inp = self.inp
        if name.startswith("wgu"):
            l, grp = int(name[3]), int(name[5])
            w = np.asarray(inp["ffn_w_gate_up"], f)[l]
            HG = FM // 2
            cols = np.concatenate([np.arange(grp * HG * 128, (grp + 1) * HG * 128),
                                   FH + np.arange(grp * HG * 128, (grp + 1) * HG * 128)])
            return _pair_tiles(w[:, cols], KT, HG)
        if name.startswith("wdn"):
            l, grp = int(name[3]), int(name[5])
            w = np.asarray(inp["ffn_w_down"], f)[l]
            HG = FM // 2
            return _acc_tiles(w[grp * HG * 128:(grp + 1) * HG * 128], HG, KT)
        if name.startswith("s5_") and name.endswith("wglu"):
            idx = int(name[3])
            return _pair_tiles(np.asarray(inp["s5_w_glu"], f)[idx], KT, KT)
        for pre, fn in WEIGHT_EXTRA.items():
            if name.startswith(pre):
                return fn(self, name)
        raise KeyError(name)

    def set(self, name, per_core_list):
        for c in range(NCORE):
            self.percore[c][name] = per_core_list[c]

    def get(self, name, c):
        if name in self.percore[c]:
            return self.percore[c][name]
        if name not in self.common:
            self.common[name] = self.weight(name)
        return self.common[name]


_PROGS = {}


def launch(host, cfg):
    key = repr(sorted(cfg.items(), key=str))
    if key not in _PROGS:
        _PROGS[key] = build(cfg)
    P = _PROGS[key]
    maps = [{n: host.get(n, c) for n in P.ins} for c in range(NCORE)]
    res = run_bass_kernel_spmd(P.nc, maps, core_ids=list(range(NCORE)))
    return [res.results[c] for c in range(NCORE)]


def tok_to_chan(outs, name):
    return [np.ascontiguousarray(np.concatenate([np.asarray(outs[r][name])[c] for r in range(NCORE)], axis=1))
            for c in range(NCORE)]


def chan_to_tok(outs, name):
    return [np.ascontiguousarray(np.stack([np.asarray(outs[c][name])[:, r * T:(r + 1) * T] for c in range(NCORE)], 0))
            for r in range(NCORE)]


HOST_EXTRA = []
WEIGHT_EXTRA = {}


import math as _math
ATT_LAYER = 2
LAMBDA_INIT = 0.8 - 0.6 * _math.exp(-0.3 * ATT_LAYER)
ROPE_THETA_ = 500000.0


def _sin_reduced(k, dst, src, tmpi, tmpf, scr):
    k.ts(tmpi, src, 1.0 / TWO_PI, None, op0=ALU.mult)
    k.copy(scr, tmpi)
    k.stt(scr, scr, -TWO_PI, src, ALU.mult, ALU.add)
    k.ts(tmpf, scr, PI, TWO_PI, op0=ALU.is_gt, op1=ALU.mult)
    k.tt(scr, scr, tmpf, ALU.subtract)
    k.ts(tmpf, scr, -PI, -TWO_PI, op0=ALU.is_lt, op1=ALU.mult)
    k.tt(scr, scr, tmpf, ALU.subtract)
    k.act(dst, scr, AF.Sin)


def attn_pre(tk, arg):
    P, k, PS = tk.P, tk.k, tk.P.PS
    mark = k.aoff
    h = k.take("h", [128, KT, T], BF16)
    tk.rmsnorm(2, h, "att")
    o_q = P.dout("aqT", [8, 128, T], BF16)
    o_k = P.dout("akT", [8, 128, T], BF16)
    o_v = P.dout("avtm", [16, 128, 1024], BF16)
    d_pos = P.din("pos_rep", [128, T], I32)
    d_invf = P.din("rope_invf", [128, 1])
    d_perm = P.din("rope_perm", [128, 128])
    invf = k.take("invf", [128, 1], F32)
    perm = k.take("perm", [128, 128], BF16)
    k.dma(invf.v, d_invf.v)
    k.dma(perm.v, d_perm.v, q="pool")
    COS = k.take("COS", [128, T], F32)
    SIN = k.take("SIN", [128, T], F32)
    COSq = k.take("COSq", [128, T], F32)
    SINq = k.take("SINq", [128, T], F32)
    m2 = k.aoff
    posi = k.take("posi", [128, T], I32)
    k.dma(posi.v, d_pos.v)
    ang = k.take("ang", [128, T], F32)
    tmpf = k.take("rtmp", [128, T], F32)
    scr = k.take("rscr", [128, T], F32)
    tmpi = V(tmpf, tmpf.ap0.bitcast(I32))
    k.copy(ang.v, posi.v)
    k.ts(ang.v, ang.v, invf[:, 0:1], None, op0=ALU.mult)
    _sin_reduced(k, SIN.v, ang.v, tmpi, tmpf.v, scr.v)
    k.ts(ang.v, ang.v, 0.5 * PI, None, op0=ALU.add)
    _sin_reduced(k, COS.v, ang.v, tmpi, tmpf.v, scr.v)
    k.ts(COSq.v, COS.v, 0.125, None, op0=ALU.mult)
    k.ts(SINq.v, SIN.v, 0.125, None, op0=ALU.mult)
    k.barrier()
    k.aoff = m2
    P.checkpoint("a_tab", COS.v)
    stq = [k.take("astq%d" % i, [128, T], BF16) for i in range(2)]
    qb = [k.take("aqb%d" % i, [128, 512], BF16) for i in range(2)]
    t1 = [k.take("at1%d" % i, [128, 512], F32) for i in range(2)]
    t2 = [k.take("at2%d" % i, [128, 512], F32) for i in range(2)]

    def sink_qk(mo, n, ts, po, cnt):
        st = stq[mo % 2]
        b = qb[cnt % 2]
        k.copy(b.v, po.v, eng="act")
        pp = PS[cnt % 2]
        k.mm(pp.v, perm.v, b.v)
        c_, s_ = (COSq, SINq) if mo < 8 else (COS, SIN)
        a1, a2 = t1[cnt % 2], t2[cnt % 2]
        k.tt(a1.v, b.v, c_[:, ts], ALU.mult)
        k.tt(a2.v, pp.v, s_[:, ts], ALU.mult)
        k.tt(st[:, ts], a1.v, a2.v, ALU.add, eng="pool")
        if n == NT - 1:
            dst = o_q[mo] if mo < 8 else o_k[mo - 8]
            P.out_ops.append(k.dma(dst, st.v, q="sp"))
    tk.proj_acc("att_wqk", h, KT, 16, sink_qk)
    P.checkpoint("a_qk", stq[1].v)
    d_wv = P.din("att_wv", [2, 128, KT * 512])
    wv = [k.take("awv%d" % i, [128, KT, 512], BF16) for i in range(2)]
    tst = [k.take("atst%d" % i, [128, 512], BF16) for i in range(4)]
    cnt = 0
    for ci in range(2):
        w = wv[ci]
        k.dma(w.v.rr("p a b -> p (a b)"), d_wv[ci], q="pool")
        for tt in range(16):
            ps = PS[2 + cnt % 2]
            for kt in range(KT):
                k.mm(ps.v, h[:, kt, tt * 128:(tt + 1) * 128], w[:, kt, :], start=(kt == 0), stop=(kt == KT - 1))
            st = tst[cnt % 4]
            k.copy(st.v, ps.v, eng="act" if cnt % 2 == 0 else "dve")
            P.out_ops.append(k.dma(o_v[tt][:, ci * 512:(ci + 1) * 512], st.v, q="sp" if cnt % 2 == 0 else "act"))
            cnt += 1
    k.barrier()
    k.aoff = mark


STEPS["attnpre"] = attn_pre


def attn_core(P, cfg):
    k, PS = P.k, P.PS
    d_q = P.din("a_qT", [128, SEQ], BF16)
    d_k = P.din("a_kT", [128, SEQ], BF16)
    d_v = P.din("a_v", [128, 128 * 128], BF16)
    d_lam = P.din("a_lamv", [128, 4, 64])
    d_sg = P.din("a_subg", [128, 1])
    d_pq = P.din("a_posq", [128, 512], I32)
    d_pk = P.din("a_posk", [128, 4], I32)
    d_o = P.dout("aoT", [128, SEQ], BF16)
    QQ = [k.take("Q%d" % i, [128, SEQ // 4], BF16) for i in range(4)]
    KQ = [k.take("K%d" % i, [128, SEQ // 4], BF16) for i in range(4)]
    VQ = [k.take("V%d" % i, [128, 32, 128], BF16) for i in range(4)]
    oT = k.take("oT", [128, 2, 512], BF16)
    ones = k.take("ones", [128, 128], BF16)
    k.memset(ones.v, 1.0)
    for i in range(4):
        sl = slice(i * 4096, (i + 1) * 4096)
        k.dma(KQ[i].v, d_k[:, sl], q="act")
        k.dma(QQ[i].v, d_q[:, sl], q="sp")
        k.dma(VQ[i].v.rr("p a b -> p (a b)"), d_v[:, sl], q="sp" if i % 2 else "act")
    lv = k.take("lv", [128, 4, 64], F32)
    sgl = k.take("sgl", [128, 1], F32)
    k.dma(lv.v, d_lam.v)
    k.dma(sgl.v, d_sg.v)
    lp = k.take("lp", [128, 2, 64], F32)
    ls = k.take("ls", [128, 2], F32)
    lam = k.take("lam", [128, 1], F32)
    k.tt(lp[:, 0, :], lv[:, 0, :], lv[:, 1, :], ALU.mult)
    k.tt(lp[:, 1, :], lv[:, 2, :], lv[:, 3, :], ALU.mult)
    k.ts(lp[:, 0, :], lp[:, 0, :], 1.0, 0.0, op0=ALU.mult, op1=ALU.add, accum=ls[:, 0:1])
    k.ts(lp[:, 1, :], lp[:, 1, :], 1.0, 0.0, op0=ALU.mult, op1=ALU.add, accum=ls[:, 1:2])
    k.act(ls.v, ls.v, AF.Exp)
    k.tt(lam.v, ls[:, 0:1], ls[:, 1:2], ALU.subtract)
    k.ts(lam.v, lam.v, LAMBDA_INIT, None, op0=ALU.add)
    k.ts(sgl.v, sgl.v, 1.0 - LAMBDA_INIT, None, op0=ALU.mult)
    pq = k.take("pq", [128, 512], I32)
    pk = k.take("pk", [128, 4], I32)
    k.dma(pq.v, d_pq.v)
    k.dma(pk.v, d_pk.v)
    k.ts(pq.v, pq.v, 6, None, op0=ALU.arith_shift_right)
    k.ts(pk.v, pk.v, 6, None, op0=ALU.arith_shift_right)
    cq = k.take("cq", [128, 512], F32)
    ck = k.take("ck", [128, 4], F32)
    k.copy(cq.v, pq.v)
    k.copy(ck.v, pk.v)
    M = [k.take("M%d" % t, [128, 512], BF16) for t in range(4)]
    for t in range(4):
        k.ts(M[t].v, cq.v, ck[:, t:t + 1], None, op0=ALU.is_ge)
    E = [k.take("E%d" % i, [128, 2, 512], BF16) for i in range(3)]
    M2 = [k.take("M2_%d" % t, [128, 2, 512], BF16) for t in range(4)]
    for t in range(4):
        k.copy(M2[t][:, 0, :], M[t].v)
        k.copy(M2[t][:, 1, :], M[t].v)
    rinv = [[k.take("rinv%d_%d" % (i, s_), [128, 512], F32) for s_ in range(2)] for i in range(2)]
    Os = [[k.take("Os%d_%d" % (i, s_), [128, 512], F32) for s_ in range(2)] for i in range(2)]
    ofs = [k.take("of%d" % i, [128, 512], F32) for i in range(2)]
    sqb = [k.take("sqb%d" % i, [128, 512], BF16) for i in range(2)]
    pending = {}
    Eacc = [k.take("Eacc%d" % i, [128, 512], F32) for i in range(2)]
    ones32 = k.take("ones32", [128, 128], F32)
    k.memset(ones32.v, 1.0)
    O = [PS[4], PS[5]]
    R = [PS[6], PS[7]]
    steps = [(qi, kj) for qi in range(SEQ // 512) for kj in range(4 * qi + 4)]

    def emit_qk(si):
        qi, kj = steps[si]
        qs = slice((qi % 8) * 512, (qi % 8 + 1) * 512)
        ks = slice((kj % 32) * 128, (kj % 32 + 1) * 128)
        for s in range(2):
            rows = slice(64 * s, 64 * s + 64)
            k.mm(P.PS2[si % 2][:, s * 512:(s + 1) * 512], KQ[kj // 32][rows, ks], QQ[qi // 8][rows, qs])

    def emit_exp(si):
        qi, kj = steps[si]
        Ex = E[si % 3]
        k.act(Ex.v.rr("p a b -> p (a b)"), P.PS2[si % 2].v, AF.Exp)
        if kj >= 4 * qi:
            k.tt(Ex.v, Ex.v, M2[kj - 4 * qi].v, ALU.mult)
        ea = Eacc[qi % 2]
        if kj == 0:
            k.copy(ea.v, Ex[:, 0, :])
        else:
            k.tt(ea.v, ea.v, Ex[:, 0, :], ALU.add)

    def emit_av(si):
        qi, kj = steps[si]
        nk = 4 * qi + 4
        Ex = E[si % 3]
        for s in range(2):
            k.mm(O[s].v, VQ[kj // 32][:, kj % 32, :], Ex[:, s, :], start=(kj == 0), stop=(kj == nk - 1))
        k.mm(R[1].v, ones.v, Ex[:, 1, :], start=(kj == 0), stop=(kj == nk - 1))

    NS = len(steps)
    emit_qk(0)
    for it in range(NS + 1):
        if it + 1 < NS:
            emit_qk(it + 1)
        if it < NS:
            emit_exp(it)
        si = it - 1
        if si < 0:
            continue
        emit_av(si)
        qi, kj = steps[si]
        nk = 4 * qi + 4
        for fn in pending.pop(si, []):
            fn()
        if kj != nk - 1:
            continue
        par = qi % 2
        Osb = [Os[par][0], Os[par][1]]
        rv = [rinv[par][0], rinv[par][1]]
        k.mm(R[0].v, ones32.v, Eacc[par].v)
        k.copy(Osb[0].v, O[0].v, eng="act")
        k.copy(Osb[1].v, O[1].v, eng="act")
        k.act(rv[1].v, R[1].v, AF.Ln)
        k.act(rv[0].v, R[0].v, AF.Ln)
        k.act(rv[1].v, rv[1].v, AF.Exp, scale=-1.0)
        k.act(rv[0].v, rv[0].v, AF.Exp, scale=-1.0)

        def stage_b(qi=qi, par=par, Osb=Osb, rv=rv):
            of = ofs[par]
            k.tt(of.v, Osb[0].v, rv[0].v, ALU.mult)
            k.stt(Osb[1].v, Osb[1].v, lam[:, 0:1], rv[1].v, ALU.mult, ALU.mult)
            k.tt(of.v, of.v, Osb[1].v, ALU.subtract)
            k.act(sqb[par].v, of.v, AF.Square)
            k.mm(R[0].v, ones.v, sqb[par].v)
            k.ts(rv[0].v, R[0].v, 1.0 / 128.0, 1e-5, op0=ALU.mult, op1=ALU.add)

        def stage_c(qi=qi, par=par, rv=rv):
            of = ofs[par]
            qs_ = slice(qi * 512, (qi + 1) * 512)
            k.act(rv[0].v, rv[0].v, AF.Ln)
            k.act(rv[0].v, rv[0].v, AF.Exp, scale=-0.5)
            ob = oT[:, par, :]
            k.stt(ob, of.v, sgl[:, 0:1], rv[0].v, ALU.mult, ALU.mult)
            P.out_ops.append(k.dma(d_o[:, qs_], ob, q="sp" if par == 0 else "act"))
        if si + 4 < len(steps):
            pending.setdefault(si + 2, []).append(stage_b)
            pending.setdefault(si + 4, []).append(stage_c)
        else:
            stage_b()
            stage_c()


STEPS["attncore"] = attn_core


def attn_post(tk, arg):
    P, k = tk.P, tk.k
    mark = k.aoff
    d_o = P.din("aoTt", [8, 128, T], BF16)
    o = k.take("ao", [128, 8, T], BF16)
    for kt in range(8):
        k.dma(o[:, kt, :], d_o[kt], q="sp" if kt % 2 == 0 else "act")
    tk.proj_acc("att_wo", o, KT, KT, tk.add_to_x)
    k.barrier()
    k.aoff = mark


STEPS["attnpost"] = attn_post


def host_attn(host):
    f = np.float32
    inp = host.inp
    w = np.asarray(inp["diff_w_qkv"], f)[0]
    host.common["att_wqk"] = _acc_tiles(w[:, 0:2048], KT, 16)
    host.common["att_wv"] = np.ascontiguousarray(w[:, 2048:3072].reshape(KT, 128, 2, 512).transpose(2, 1, 0, 3)).reshape(2, 128, KT * 512)
    host.common["att_wo"] = _acc_tiles(np.asarray(inp["diff_w_o"], f)[0], KT, KT)
    d = np.arange(128) % 64
    half = 8
    invf = np.where(d < 16, ROPE_THETA_ ** (-(d % half).astype(np.float64) / half), 0.0).astype(f)
    host.common["rope_invf"] = np.ascontiguousarray(invf[:, None])
    perm = np.zeros((128, 128), f)
    for p in range(128):
        dd = p % 64
        if dd < 8:
            perm[p + 8, p] = -1.0
        elif dd < 16:
            perm[p - 8, p] = 1.0
    host.common["rope_perm"] = perm
    pos = np.asarray(inp["positions"])[0].astype(np.int32)
    for c in range(NCORE):
        host.percore[c]["pos_rep"] = np.ascontiguousarray(np.broadcast_to(pos[None, c * T:(c + 1) * T], (128, T)))
        lamv = np.stack([np.asarray(inp[n], f)[0] for n in ("diff_lam_q1", "diff_lam_k1", "diff_lam_q2", "diff_lam_k2")], 0)
        host.percore[c]["a_lamv"] = np.ascontiguousarray(np.broadcast_to(lamv[None], (128, 4, 64)))
        host.percore[c]["a_subg"] = np.ascontiguousarray(np.asarray(inp["diff_subln"], f)[0][:, None])
        host.percore[c]["a_posq"] = np.ascontiguousarray(np.broadcast_to(pos[None, 0:512], (128, 512)))
        host.percore[c]["a_posk"] = np.ascontiguousarray(pos[0:512].reshape(4, 128).T)


HOST_EXTRA.append(host_attn)


GLA_H = 4
GLA_DKH = 128
GLA_DVH = 256


def gla_pre(tk, arg):
    P, k, PS = tk.P, tk.k, tk.P.PS
    mark = k.aoff
    h = k.take("h", [128, KT, T], BF16)
    tk.rmsnorm(1, h, "gla")
    o_q = P.dout("qT", [4, 128, T], BF16)
    o_g = P.dout("gT", [8, 128, T], BF16)
    o_k = P.dout("ktm", [16, 128, 512], BF16)
    o_v = P.dout("vtm", [16, 128, 1024], BF16)
    o_la = P.dout("latm", [16, 128, 512], BF16)
    stq = [k.take("stq%d" % i, [128, T], BF16) for i in range(2)]

    def sink_qg(mo, n, ts, po, cnt):
        st = stq[mo % 2]
        k.copy(st[:, ts], po.v, eng="act" if cnt % 2 == 0 else "dve")
        if n == NT - 1:
            dst = o_q[mo] if mo < 4 else o_g[mo - 4]
            P.out_ops.append(k.dma(dst, st.v, q="sp"))
    tk.proj_acc("gla_wqg", h, KT, 12, sink_qg)

    d_wa = P.din("gla_walo", [128, KT * 16])
    wa = k.take("wa", [128, KT, 16], BF16)
    k.dma(wa.v.rr("p a b -> p (a b)"), d_wa.v, q="pool")
    alo = k.take("alo", [16, T], BF16)
    for n in range(NT):
        ts = slice(n * 512, (n + 1) * 512)
        ps = PS[n % 2]
        for kt in range(KT):
            k.mm(ps[0:16, :], wa[:, kt, :], h[:, kt, ts], start=(kt == 0), stop=(kt == KT - 1))
        k.copy(alo[:, ts], ps[0:16, :])

    d_wkv = P.din("gla_wkv", [3, 128, KT * 512])
    d_wa2 = P.din("gla_wa2", [16, 512])
    d_ba = P.din("gla_ba", [1, 512])
    wa2 = k.take("wa2", [16, 512], BF16)
    ba = k.take("ba", [1, 512], BF16)
    k.dma(wa2.v, d_wa2.v, q="pool")
    k.dma(ba.v, d_ba.v, q="pool")
    wkv = [k.take("wkv%d" % i, [128, KT, 512], BF16) for i in range(2)]
    tst = [k.take("tst%d" % i, [128, 512], BF16) for i in range(4)]
    cnt = 0
    for ci in range(3):
        w = wkv[ci % 2]
        k.dma(w.v.rr("p a b -> p (a b)"), d_wkv[ci], q="pool")
        for tt in range(16):
            ps = PS[2 + cnt % 2]
            for kt in range(KT):
                k.mm(ps.v, h[:, kt, tt * 128:(tt + 1) * 128], w[:, kt, :], start=(kt == 0), stop=(kt == KT - 1))
            st = tst[cnt % 4]
            k.copy(st.v, ps.v, eng="act" if cnt % 2 == 0 else "dve")
            dst = o_k[tt] if ci == 0 else o_v[tt][:, (ci - 1) * 512:ci * 512]
            P.out_ops.append(k.dma(dst, st.v, q="sp" if cnt % 2 == 0 else "act"))
            cnt += 1
    lt = [k.take("lt%d" % i, [128, 512], F32) for i in range(2)]
    for tt in range(16):
        ps = PS[4 + tt % 2]
        k.mm(ps.v, alo[:, tt * 128:(tt + 1) * 128], wa2.v, start=True, stop=False)
        k.mm(ps.v, tk.ones[0:1, 0:128], ba.v, start=False, stop=True)
        t_ = lt[tt % 2]
        k.act(t_.v, ps.v, AF.Exp, scale=-1.0)
        k.ts(t_.v, t_.v, 1.0, None, op0=ALU.add)
        k.act(t_.v, t_.v, AF.Ln)
        st = tst[tt % 4]
        k.ts(st.v, t_.v, -1.0 / 16.0, None, op0=ALU.mult)
        P.out_ops.append(k.dma(o_la[tt], st.v, q="sp" if tt % 2 == 0 else "act"))
    k.barrier()
    k.aoff = mark


STEPS["glapre"] = gla_pre


def gla_core(P, cfg):
    k, PS = P.k, P.PS
    d_q = P.din("g_qT", [128, SEQ], BF16)
    d_k = P.din("g_k", [128, 128 * 128], BF16)
    d_la = P.din("g_la", [128, 128 * 128], BF16)
    d_v = P.din("g_v", [128, 128 * 128], BF16)
    d_u2 = P.din("g_u2", [128, 128])
    d_ind = P.din("g_ind", [128, 2])
    d_o = P.dout("oT", [128, SEQ], BF16)
    qQ = [k.take("q%d" % i, [128, SEQ // 4], BF16) for i in range(4)]
    kQ = [k.take("kk%d" % i, [128, 32, 128], BF16) for i in range(4)]
    lQ = [k.take("la%d" % i, [128, 32, 128], BF16) for i in range(4)]
    vQ = [k.take("v%d" % i, [128, 32, 128], BF16) for i in range(4)]
    oT = k.take("oT", [128, SEQ], BF16)
    u2 = k.take("u2", [128, 128], BF16)
    ind = k.take("ind", [128, 2], BF16)
    state = k.take("state", [128, 128], F32)
    stb = [k.take("stb%d" % i, [128, 128], BF16) for i in range(2)]
    er = [k.take("er%d" % i, [128, 128], F32) for i in range(2)]
    kd = [k.take("kd%d" % i, [128, 128], BF16) for i in range(2)]
    dec = [k.take("dec%d" % i, [128, 2], F32) for i in range(2)]
    k.dma(u2.v, d_u2.v, q="pool")
    k.dma(ind.v, d_ind.v, q="pool")
    for i in range(4):
        sl = slice(i * 4096, (i + 1) * 4096)
        k.dma(lQ[i].v.rr("p a b -> p (a b)"), d_la[:, sl], q="sp")
        k.dma(kQ[i].v.rr("p a b -> p (a b)"), d_k[:, sl], q="act")
        k.dma(vQ[i].v.rr("p a b -> p (a b)"), d_v[:, sl], q="sp")
        k.dma(qQ[i].v, d_q[:, sl], q="act")
    k.memset(state.v, 0.0)
    SC = float(GLA_DKH) ** -0.5
    def tile_prep(j):
        prv = PS[j % 2]
        ptt = PS[2 + j % 2]
        k.mm(prv[:, 0:128], u2.v, lQ[j // 32][:, j % 32, :])
        k.mm(ptt[:, 0:2], lQ[j // 32][:, j % 32, :], ind.v)
        e = er[j % 2]
        k.act(e.v, prv[:, 0:128], AF.Exp)
        k.tt(kd[j % 2].v, e.v, kQ[j // 32][:, j % 32, :], ALU.mult)
        k.act(dec[j % 2].v, ptt[:, 0:2], AF.Exp)

    def upd(c):
        j, ch = c // 2, c % 2
        rows = slice(64 * ch, 64 * ch + 64)
        k.mm(PS[4 + c % 2][:, 0:128], kd[j % 2][rows, :], vQ[j // 32][rows, j % 32, :])

    tile_prep(0)
    upd(0)
    NCHK = 256
    for c in range(NCHK):
        j, ch = c // 2, c % 2
        if ch == 0 and j + 1 < 128:
            tile_prep(j + 1)
        if c + 1 < NCHK:
            upd(c + 1)
        po = PS[6 + (c // 8) % 2]
        k.stt(state.v, state.v, dec[j % 2][:, ch:ch + 1], PS[4 + c % 2][:, 0:128], ALU.mult, ALU.add)
        sb_ = stb[c % 2]
        k.copy(sb_.v, state.v, eng="act")
        off = (c % 8) * 64
        k.mm(po[:, off:off + 64], sb_.v, qQ[c // 64][:, (c % 64) * 64:(c % 64 + 1) * 64])
        if c % 8 == 7:
            n = c // 8
            k.ts(oT[:, n * 512:(n + 1) * 512], po.v, SC, None, op0=ALU.mult)
    for i in range(4):
        sl = slice(i * 4096, (i + 1) * 4096)
        P.out_ops.append(k.dma(d_o[:, sl], oT[:, sl], q="sp" if i % 2 == 0 else "act"))


STEPS["glacore"] = gla_core


def gla_post(tk, arg):
    P, k, PS = tk.P, tk.k, tk.P.PS
    mark = k.aoff
    d_o = P.din("goT", [8, 128, T], BF16)
    d_g = P.din("gT", [8, 128, T], BF16)
    d_ng = P.din("gla_ng", [128, 8])
    o = k.take("o", [128, 8, T], BF16)
    g = k.take("g", [128, 8, T], BF16)
    og = k.take("og", [128, 8, T], BF16)
    ng = k.take("ng", [128, 8], F32)
    k.dma(ng.v, d_ng.v)
    for kt in range(8):
        k.dma(o[:, kt, :], d_o[kt], q="sp")
        k.dma(g[:, kt, :], d_g[kt], q="act")
    sq = [k.take("gsq%d" % i, [128, 512], BF16) for i in range(2)]
    rs = [k.take("grs%d" % i, [128, 512], F32) for i in range(2)]
    sg = [k.take("gsg%d" % i, [128, 512], F32) for i in range(2)]
    cnt = 0
    for hd in range(4):
        for n in range(NT):
            ts = slice(n * 512, (n + 1) * 512)
            ps = PS[cnt % 2]
            for e in range(2):
                s = sq[e]
                k.act(s.v, o[:, 2 * hd + e, ts], AF.Square)
                k.mm(ps.v, tk.ones.v, s.v, start=(e == 0), stop=(e == 1))
            r = rs[cnt % 2]
            k.ts(r.v, ps.v, 1.0 / GLA_DVH, EPS, op0=ALU.mult, op1=ALU.add)
            k.act(r.v, r.v, AF.Sqrt)
            k.recip(r.v, r.v)
            for e in range(2):
                kt = 2 * hd + e
                s_ = sg[e]
                k.act(s_.v, g[:, kt, ts], AF.Silu)
                k.stt(s_.v, o[:, kt, ts], ng[:, kt:kt + 1], s_.v, ALU.mult, ALU.mult)
                k.tt(og[:, kt, ts], s_.v, r.v, ALU.mult)
            cnt += 1
    tk.proj_acc("gla_wo", og, KT, KT, tk.add_to_x)
    k.barrier()
    k.aoff = mark


STEPS["glapost"] = gla_post


def host_gla(host):
    f = np.float32
    inp = host.inp
    u2 = np.zeros((128, 128), f)
    for t2 in range(128):
        for t in range(128):
            if t2 > t and t2 // 64 == t // 64:
                u2[t2, t] = 1.0
    host.common["g_u2"] = u2
    host.common["g_ind"] = (np.arange(128)[:, None] // 64 == np.arange(2)[None, :]).astype(f)
    win = np.asarray(inp["gla_w_in"], f)[0]
    host.common["gla_wqg"] = _acc_tiles(np.concatenate([win[:, 0:512], win[:, 2048:3072]], 1), KT, 12)
    host.common["gla_walo"] = np.ascontiguousarray(win[:, 3072:3088].reshape(KT, 128, 16).transpose(1, 0, 2)).reshape(128, KT * 16)
    kv = win[:, 512:2048]
    host.common["gla_wkv"] = np.ascontiguousarray(kv.reshape(KT, 128, 3, 512).transpose(2, 1, 0, 3)).reshape(3, 128, KT * 512)
    host.common["gla_wa2"] = np.ascontiguousarray(np.asarray(inp["gla_w_a2"], f)[0])
    host.common["gla_ba"] = np.ascontiguousarray(np.asarray(inp["gla_b_a"], f)[0][None, :])
    host.common["gla_ng"] = np.ascontiguousarray(np.asarray(inp["gla_norm"], f)[0].reshape(8, 128).T)
    host.common["gla_wo"] = _acc_tiles(np.asarray(inp["gla_w_o"], f)[0], KT, KT)


HOST_EXTRA.append(host_gla)


def _tokmajor_to_core(outs, name, c0, ncol=128):
    a = np.concatenate([np.asarray(outs[r][name])[:, :, c0:c0 + ncol] for r in range(NCORE)], 0)
    return np.ascontiguousarray(a.transpose(1, 0, 2)).reshape(128, 128 * ncol)


TWO_PI = 6.283185307179586
PI = 3.141592653589793


def s5_core(P, idx, PS):
    k = P.k
    s5_consts(P)
    nm = "s5_%d_" % idx
    d_lamP = P.din(nm + "lamP", [64, 2, 8])
    d_dtP = P.din(nm + "dtP", [64, 8])
    d_bP = P.din(nm + "bP", [64, 2, 8, 16])
    d_cP = P.din(nm + "cP", [64, 2, 8, 16])
    d_lamR = P.din(nm + "lamR", [128, 2, 64])
    d_dtR = P.din(nm + "dtR", [128, 1])
    d_bR = P.din(nm + "bR", [128, 2, 64])
    d_dv = P.din(nm + "dvec", [128, 1])
    d_u = P.din("u", [128, SEQ], BF16)
    d_z = k.dram("z", [128, SEQ], BF16, kind="ExternalOutput")
    jt, bmask, mbd, ident = P.jtab, P.bmask, P.mbd, P.ident
    b0 = k.aoff
    oA = b0
    oB = oA + 8192
    oC = oB + 8192
    oD = oC + 2048
    oE = oD + 4096
    oF = oE + 4224
    assert oF + 6400 <= k.awords, (oF, k.awords)

    class Cur:
        def __init__(self, off):
            self.off = off

        def take(self, name, shape, dtype):
            b = k.take_at(name, shape, dtype, self.off)
            self.off += b.words
            return b

    u = k.take_at("u", [128, SEQ], BF16, oA)
    for i in range(8):
        k.dma(u[:, i * T:(i + 1) * T], d_u[:, i * T:(i + 1) * T], q="sp" if i % 2 == 0 else "act")

    def powtable(cur, np_, lam, ldt_ap, shape3, tag, lo, nl):
        a, b = shape3
        dt = cur.take(tag + "dt", [np_, a, b], F32)
        k.act(dt.v, ldt_ap, AF.Exp)
        lrdt = cur.take(tag + "lrdt", [np_, a, b], F32)
        lidt = cur.take(tag + "lidt", [np_, a, b], F32)
        k.tt(lrdt.v, lam[0], dt.v, ALU.mult)
        k.tt(lidt.v, lam[1], dt.v, ALU.mult)
        mag = cur.take(tag + "mag", [np_, a, nl, b], F32)
        Pre = cur.take(tag + "Pre", [np_, a, nl, b], F32)
        tmp = cur.take(tag + "tmp", [np_, a, nl, b], F32)
        Pim = cur.take(tag + "Pim", [np_, a, nl, b], F32)
        for ai in range(a):
            jv = V(jt, jt.ap0[0:np_, lo:lo + nl].unsqueeze(2).to_broadcast([np_, nl, b]))
            lr_b = V(lrdt, lrdt.ap0[:, ai, :].unsqueeze(1).to_broadcast([np_, nl, b]))
            li_b = V(lidt, lidt.ap0[:, ai, :].unsqueeze(1).to_broadcast([np_, nl, b]))
            k.tt(mag[:, ai, :, :], jv, lr_b, ALU.mult)
            k.tt(Pre[:, ai, :, :], jv, li_b, ALU.mult)
        magf = mag.v.rr("p a l b -> p (a l b)")
        tmpf = tmp.v.rr("p a l b -> p (a l b)")
        pref = Pre.v.rr("p a l b -> p (a l b)")
        pimf = Pim.v.rr("p a l b -> p (a l b)")
        k.act(magf, magf, AF.Exp)
        scr = cur.take(tag + "scr", [np_, a, nl, b], F32)
        scrf = scr.v.rr("p a l b -> p (a l b)")
        tmpi = V(tmp, tmp.ap0.bitcast(I32).rearrange("p a l b -> p (a l b)"))

        def reduce_sin(dst, src):
            k.ts(tmpi, src, 1.0 / TWO_PI, None, op0=ALU.mult)
            k.copy(scrf, tmpi)
            k.stt(scrf, scrf, -TWO_PI, src, ALU.mult, ALU.add)
            k.ts(tmpf, scrf, PI, TWO_PI, op0=ALU.is_gt, op1=ALU.mult)
            k.tt(scrf, scrf, tmpf, ALU.subtract)
            k.ts(tmpf, scrf, -PI, -TWO_PI, op0=ALU.is_lt, op1=ALU.mult)
            k.tt(scrf, scrf, tmpf, ALU.subtract)
            k.act(dst, scrf, AF.Sin)
        reduce_sin(pimf, pref)
        k.tt(pimf, pimf, magf, ALU.mult)
        k.ts(pref, pref, 0.5 * PI, None, op0=ALU.add)
        reduce_sin(pref, pref)
        k.tt(pref, pref, magf, ALU.mult)
        return Pre, Pim

    def bbar(cur, np_, lam, abre, abim, bre, bim, shp, tag, bcast, out_re, out_im):
        a, b = lam[0].ap.shape[1], lam[0].ap.shape[2]
        t = [cur.take(tag + "f%d" % i, [np_, a, b], F32) for i in range(5)]
        lr, li = lam
        k.tt(t[0].v, lr, lr, ALU.mult)
        k.tt(t[1].v, li, li, ALU.mult)
        k.tt(t[0].v, t[0].v, t[1].v, ALU.add)
        k.recip(t[0].v, t[0].v)
        k.ts(t[1].v, abre, -1.0, None, op0=ALU.add)
        k.tt(t[2].v, t[1].v, lr, ALU.mult)
        k.tt(t[3].v, abim, li, ALU.mult)
        k.tt(t[2].v, t[2].v, t[3].v, ALU.add)
        k.tt(t[2].v, t[2].v, t[0].v, ALU.mult)
        k.tt(t[3].v, abim, lr, ALU.mult)
        k.tt(t[4].v, t[1].v, li, ALU.mult)
        k.tt(t[3].v, t[3].v, t[4].v, ALU.subtract)
        k.tt(t[3].v, t[3].v, t[0].v, ALU.mult)
        tm = cur.take(tag + "bbt", shp, F32)
        fre, fim = bcast(t[2]), bcast(t[3])
        k.tt(out_re.v, fre, bre, ALU.mult)
        k.tt(tm.v, fim, bim, ALU.mult)
        k.tt(out_re.v, out_re.v, tm.v, ALU.subtract)
        k.tt(out_im.v, fre, bim, ALU.mult)
        k.tt(tm.v, fim, bre, ALU.mult)
        k.tt(out_im.v, out_im.v, tm.v, ALU.add)

    cB = Cur(oB)
    XT = [cB.take("XT%d" % i, [128, 64, 64], BF16) for i in range(2)]
    c2 = Cur(oD)
    lamR = c2.take("lamR", [128, 2, 64], F32)
    dtR = c2.take("dtR", [128, 1], F32)
    bR = c2.take("bR", [128, 2, 64], F32)
    bbRre = c2.take("bbRre", [128, 1, 64], F32)
    bbRim = c2.take("bbRim", [128, 1, 64], F32)
    k.dma(lamR.v, d_lamR.v)
    k.dma(dtR.v, d_dtR.v)
    k.dma(bR.v, d_bR.v)
    lamRv = (lamR[:, 0:1, :], lamR[:, 1:2, :])
    dtRb = V(dtR, dtR.ap0.unsqueeze(2).to_broadcast([128, 1, 64]))
    c2b = Cur(c2.off)
    P1re, P1im = powtable(c2b, 128, lamRv, dtRb, (1, 64), "R1", 1, 1)
    bbar(c2b, 128, lamRv, P1re[:, :, 0, :], P1im[:, :, 0, :], bR[:, 0:1, :], bR[:, 1:2, :], [128, 1, 64], "R", lambda t: t.v, bbRre, bbRim)
    k.barrier()
    for hf in range(2):
        c2c = Cur(c2.off)
        PreR, PimR = powtable(c2c, 128, lamRv, dtRb, (1, 64), "R%d" % hf, 32 * hf, 32)
        t1 = c2c.take("Rt1", [128, 32, 64], F32)
        t2 = c2c.take("Rt2", [128, 32, 64], F32)
        bre_b = V(bbRre, bbRre.ap0[:, 0, :].unsqueeze(1).to_broadcast([128, 32, 64]))
        bim_b = V(bbRim, bbRim.ap0[:, 0, :].unsqueeze(1).to_broadcast([128, 32, 64]))
        ls = slice(32 * hf, 32 * hf + 32)
        k.tt(t1.v, PreR[:, 0, :, :], bre_b, ALU.mult)
        k.tt(t2.v, PimR[:, 0, :, :], bim_b, ALU.mult)
        k.tt(XT[0][:, ls, :], t1.v, t2.v, ALU.subtract)
        k.tt(t1.v, PreR[:, 0, :, :], bim_b, ALU.mult)
        k.tt(t2.v, PimR[:, 0, :, :], bre_b, ALU.mult)
        k.tt(XT[1][:, ls, :], t1.v, t2.v, ALU.add)
        k.barrier()

    P.checkpoint("tabR", XT[0].v.rr("p l q -> p (l q)"))
    cC = Cur(oC)
    lamP = cC.take("lamP", [64, 2, 8], F32)
    dtP = cC.take("dtP", [64, 8], F32)
    bP = cC.take("bP", [64, 2, 8, 16], F32)
    cP = cC.take("cP", [64, 2, 8, 16], F32)
    dv = cC.take("dv", [128, 1], F32)
    bbPre = cC.take("bbPre", [64, 8, 16], F32)
    bbPim = cC.take("bbPim", [64, 8, 16], F32)
    A64 = cC.take("A64", [64, 3, 8], F32)
    cb = cC.take("cb", [64, 2, 128], BF16)
    assert cC.off <= oD
    k.dma(lamP.v, d_lamP.v)
    k.dma(dtP.v, d_dtP.v)
    k.dma(bP.v.rr("p a b c -> p (a b c)"), d_bP.v.rr("p a b c -> p (a b c)"))
    k.dma(cP.v.rr("p a b c -> p (a b c)"), d_cP.v.rr("p a b c -> p (a b c)"))
    k.dma(dv.v, d_dv.v)
    lamPv = (lamP[:, 0, :].rr("p (g o) -> p g o", o=1), lamP[:, 1, :].rr("p (g o) -> p g o", o=1))
    dtPv = dtP.v.rr("p (g o) -> p g o", o=1)
    c3 = Cur(oE)
    PreP, PimP = powtable(c3, 64, lamPv, dtPv, (8, 1), "P", 0, 65)
    k.copy(A64[:, 0, :], PreP[:, :, 64, 0])
    k.copy(A64[:, 1, :], PimP[:, :, 64, 0])
    k.ts(A64[:, 2, :], PimP[:, :, 64, 0], -1.0, None, op0=ALU.mult)
    bbar(c3, 64, lamPv, PreP[:, :, 1, :], PimP[:, :, 1, :], bP[:, 0, :, :], bP[:, 1, :, :], [64, 8, 16], "P",
         lambda t: V(t, t.ap0[:, :, 0].unsqueeze(2).to_broadcast([64, 8, 16])), bbPre, bbPim)
    k.copy(cb[:, 0, :], cP[:, 0, :, :].rr("p g i -> p (g i)"))
    k.ts(cb[:, 1, :], cP[:, 1, :, :].rr("p g i -> p (g i)"), -1.0, None, op0=ALU.mult)
    PreQ = cC.take("PreQ", [64, 8, 65, 1], F32)
    PimQ = cC.take("PimQ", [64, 8, 65, 1], F32)
    assert cC.off <= oD, (cC.off, oD)
    k.copy(PreQ.v, PreP.v)
    k.copy(PimQ.v, PimP.v)
    k.barrier()
    P.checkpoint("tabP", PreQ.v.rr("p g l o -> p (g l o)"))

    Kb = k.take_at("Kb", [128, 64, 128], BF16, oD)
    c6 = Cur(oE)
    kt0 = c6.take("kt0", [128, 128], F32)
    Xc = [[c6.take("Xc%d_%d" % (i, ri), [64, 16, 8, 16], BF16) for ri in range(2)] for i in range(2)]
    xt1 = c6.take("xt1", [64, 16, 8, 16], F32)
    xt2 = c6.take("xt2", [64, 16, 8, 16], F32)
    for qq in range(4):
        X = Xc[qq % 2]
        lo = 16 * qq

        def pw_b(Pt):
            return V(Pt, Pt.ap0[:, :, lo:lo + 16, 0].rearrange("p g l -> p l g").unsqueeze(3).to_broadcast([64, 16, 8, 16]))

        def gi_b(Bt):
            return V(Bt, Bt.ap0.unsqueeze(1).to_broadcast([64, 16, 8, 16]))
        k.tt(xt1.v, pw_b(PreQ), gi_b(bbPre), ALU.mult)
        k.tt(xt2.v, pw_b(PimQ), gi_b(bbPim), ALU.mult, eng="pool")
        k.tt(X[0].v, xt1.v, xt2.v, ALU.subtract)
        k.tt(xt1.v, pw_b(PreQ), gi_b(bbPim), ALU.mult)
        k.tt(xt2.v, pw_b(PimQ), gi_b(bbPre), ALU.mult, eng="pool")
        k.tt(X[1].v, xt1.v, xt2.v, ALU.add)
        for li in range(16):
            lag = lo + li
            ps = PS[lag % 2]
            k.mm(ps[:, 0:128], X[0][:, li, :, :].rr("p g i -> p (g i)"), cb[:, 0, :], start=True, stop=False)
            k.mm(ps[:, 0:128], X[1][:, li, :, :].rr("p g i -> p (g i)"), cb[:, 1, :], start=False, stop=True)
            if lag == 0:
                k.tt(kt0.v, ps[:, 0:128], mbd.v, ALU.mult)
                k.stt(Kb[:, 0, :], ident.v, dv[:, 0:1], kt0.v, ALU.mult, ALU.add)
            else:
                k.tt(Kb[:, lag, :], ps[:, 0:128], mbd.v, ALU.mult)
    k.barrier()
    P.checkpoint("kb", Kb.v.rr("p l q -> p (l q)")[:, 0:4096])

    NCH = SEQ // 64
    S = k.take_at("S", [64, 2, 8, NCH + 1], F32, oE)
    c4 = Cur(oF)
    wbd = [c4.take("wbd%d" % i, [128, 4, 2, 8, 64], BF16) for i in range(2)]
    Bch = c4.take("Bch", [128, 2, 2, 512], F32)
    st1 = c4.take("st1", [64, 2, 8], F32)
    st2 = c4.take("st2", [64, 2, 8], F32)
    bm_b = V(bmask, bmask.ap0.unsqueeze(2).to_broadcast([128, 8, 64]))
    psb = [[PS[4 + blk * 2 + ri] for ri in range(2)] for blk in range(2)]
    for q in range(16):
        w = wbd[q % 2]
        for li in range(4):
            lag = 4 * q + li
            for ri in range(2):
                k.tt(w[:, li, ri, :, :], V(XT[ri], XT[ri].ap0[:, lag, :].unsqueeze(1).to_broadcast([128, 8, 64])), bm_b, ALU.mult,
                     eng="dve" if ri == 0 else "pool")
        for li in range(4):
            lag = 4 * q + li
            jp = 63 - lag
            for blk in range(2):
                lhs = u.v.rr("p (c j) -> p c j", j=64)[:, blk * 128:(blk + 1) * 128, jp]
                for ri in range(2):
                    k.mm(psb[blk][ri].v, lhs, w[:, li, ri, :, :].rr("p g q -> p (g q)"), start=(lag == 0), stop=(lag == 63))
    for blk in range(2):
        for ri in range(2):
            k.copy(Bch[:, blk, ri, :], psb[blk][ri].v, eng="act" if ri == 0 else "dve")
    k.memset(S[:, :, :, 0:1], 0.0)
    cnt = 0
    for blk in range(2):
        for ri in range(2):
            for g in range(8):
                ps = PS[cnt % 4]
                k.transpose(ps[0:64, 0:128], Bch[:, blk, ri, g * 64:(g + 1) * 64], ident.v)
                k.copy(S[:, ri, g, 1 + blk * 128: 1 + (blk + 1) * 128], ps[0:64, 0:128], eng="act" if cnt % 2 == 0 else "dve")
                cnt += 1
    k.barrier()
    P.checkpoint("states", S.v.rr("p r g c -> p (r g c)")[:, 0:4096])

    ytoep = k.take_at("ytoep", [128, SEQ], BF16, oB)
    cs = Cur(c4.off)
    NB, BM = 16, 16
    sa = cs.take("sa", [64, 2, 8, NB], F32)
    sb2 = cs.take("sb2", [64, 2, 8, NB], F32)
    carry = cs.take("carry", [64, 2, 8, NB], F32)
    Apw = cs.take("Apw", [64, 3, 8], F32)
    apt = cs.take("apt", [64, 3, 8], F32)
    SV = S.v[:, :, :, 1:NCH + 1].rr("p r g (b m) -> p r g b m", m=BM)

    def cmul(dst, src, A, nx):
        ab = V(A, A.ap0[:, 0, :].unsqueeze(1).unsqueeze(3).to_broadcast([64, 2, 8, nx]))
        nim = V(A, A.ap0[:, 2, :].unsqueeze(2).to_broadcast([64, 8, nx]))
        pim = V(A, A.ap0[:, 1, :].unsqueeze(2).to_broadcast([64, 8, nx]))
        k.tt(sa[:, :, :, 0:nx], src, ab, ALU.mult)
        k.tt(sb2[:, 0, :, 0:nx], src[:, 1], nim, ALU.mult)
        k.tt(sb2[:, 1, :, 0:nx], src[:, 0], pim, ALU.mult)
        k.tt(dst, sa[:, :, :, 0:nx], sb2[:, :, :, 0:nx], ALU.add)

    k.copy(Apw.v, A64.v)
    for _ in range(4):
        k.tt(apt[:, 0, :], Apw[:, 0, :], Apw[:, 0, :], ALU.mult)
        k.tt(apt[:, 1, :], Apw[:, 1, :], Apw[:, 1, :], ALU.mult)
        k.tt(apt[:, 2, :], Apw[:, 0, :], Apw[:, 1, :], ALU.mult)
        k.tt(Apw[:, 0, :], apt[:, 0, :], apt[:, 1, :], ALU.subtract)
        k.ts(Apw[:, 1, :], apt[:, 2, :], 2.0, None, op0=ALU.mult)
        k.ts(Apw[:, 2, :], apt[:, 2, :], -2.0, None, op0=ALU.mult)
    for i in range(1, BM):
        cmul(carry.v, SV[:, :, :, :, i - 1], A64, NB)
        k.tt(SV[:, :, :, :, i], SV[:, :, :, :, i], carry.v, ALU.add)
    for blk in range(1, NB):
        cmul(carry[:, :, :, 0:1], SV[:, :, :, blk - 1:blk, BM - 1], Apw, 1)
        k.tt(SV[:, :, :, blk:blk + 1, BM - 1], SV[:, :, :, blk:blk + 1, BM - 1], carry[:, :, :, 0:1], ALU.add)
    k.copy(carry[:, :, :, 0:NB - 1], SV[:, :, :, 0:NB - 1, BM - 1])
    for i in range(BM - 1):
        cmul(carry[:, :, :, 0:NB - 1], carry[:, :, :, 0:NB - 1], A64, NB - 1)
        k.tt(SV[:, :, :, 1:NB, i], SV[:, :, :, 1:NB, i], carry[:, :, :, 0:NB - 1], ALU.add)
    Sb = k.take_at("Sb", [64, 2, 8, NCH], BF16, Bch.off)
    k.copy(Sb.v, S[:, :, :, 0:NCH])
    cw = Cur(wbd[0].off)
    CLg = [[cw.take("CL%d_%d" % (i, ri), [64, 64, 16], BF16) for ri in range(2)] for i in range(2)]
    ct1 = cw.take("ct1", [64, 64, 16], F32)
    ct2 = cw.take("ct2", [64, 64, 16], F32)
    assert cw.off <= wbd[0].off + 4096
    Yst = k.take_at("Yst", [128, 64, 8, 16], BF16, oE)
    ytb = [k.take_at("ytoep%d" % blk, [128, SEQ // 2], BF16, oB + blk * 4096) for blk in range(2)]
    ycnt = [0]

    def ystate_unit(blk, g):
        CL = CLg[g % 2]
        pre1 = V(PreQ, PreQ.ap0[:, g, 1:65, 0].unsqueeze(2).to_broadcast([64, 64, 16]))
        pim1 = V(PimQ, PimQ.ap0[:, g, 1:65, 0].unsqueeze(2).to_broadcast([64, 64, 16]))
        cre_b = V(cP, cP.ap0[:, 0, g, :].unsqueeze(1).to_broadcast([64, 64, 16]))
        cim_b = V(cP, cP.ap0[:, 1, g, :].unsqueeze(1).to_broadcast([64, 64, 16]))
        k.tt(ct1.v, pre1, cre_b, ALU.mult)
        k.tt(ct2.v, pim1, cim_b, ALU.mult, eng="pool")
        k.tt(CL[0].v, ct1.v, ct2.v, ALU.subtract)
        k.tt(ct1.v, pim1, cre_b, ALU.mult)
        k.tt(ct2.v, pre1, cim_b, ALU.mult, eng="pool")
        k.tt(ct1.v, ct1.v, ct2.v, ALU.add)
        k.ts(CL[1].v, ct1.v, -1.0, None, op0=ALU.mult)
        for pc in range(2):
            ps = PS[4 + ycnt[0] % 2]
            for ri in range(2):
                k.mm(ps.v, Sb[:, ri, g, blk * 128:(blk + 1) * 128],
                     CL[ri][:, pc * 32:(pc + 1) * 32, :].rr("p j i -> p (j i)"), start=(ri == 0), stop=(ri == 1))
            k.copy(Yst[:, pc * 32:(pc + 1) * 32, g, :], V(ps, ps.ap0.rearrange("p (j i) -> p j i", i=16)), eng="act")
            ycnt[0] += 1

    def ystate_fold(blk):
        for j4 in range(16):
            pst = PS[6 + (j4 % 2)]
            pstb = V(pst, pst.ap0.bitcast(BF16)[:, 0:512].rearrange("p (a b) -> p a b", a=4))
            for jj in range(4):
                j = 4 * j4 + jj
                k.transpose(pstb[:, jj, :], Yst[:, j, :, :].rr("p g i -> p (g i)"), P.identb.v)
            dst = ytb[blk].v.rr("p (c j) -> p j c", j=64)[:, 4 * j4:4 * j4 + 4, :]
            k.tt(dst, dst, pstb, ALU.add)

    for n in range(SEQ // 512):
        ps = PS[n % 4]
        ts_ = slice(n * 512, (n + 1) * 512)
        pv = V(ps, ps.ap0.rearrange("p (c j) -> p c j", j=64))
        uv = u[:, ts_].rr("p (c j) -> p c j", j=64)
        for lag in range(64):
            k.mm(pv[:, :, lag:64], Kb[:, lag, :], uv[:, :, 0:64 - lag], start=(lag == 0), stop=(lag == 63))
        blk = n // 16
        k.copy(ytb[blk][:, (n % 16) * 512:(n % 16 + 1) * 512], ps.v, eng="act")
        if n % 2 == 1:
            ystate_unit(blk, (n // 2) % 8)
        if n % 16 == 15:
            ystate_fold(blk)
    k.barrier()
    ytoep = k.take_at("ytoep", [128, SEQ], BF16, oB)
    P.checkpoint("ystate", ytoep[:, 0:4096])

    c7 = Cur(oE)
    y2 = [c7.take("y2%d" % i, [128, 512], F32) for i in range(2)]
    GC = 0.7978845608028654
    for n in range(SEQ // 512):
        ts_ = slice(n * 512, (n + 1) * 512)
        y = ytoep[:, ts_]
        w = y2[n % 2]
        k.act(w.v, y, AF.Square)
        k.ts(w.v, w.v, 0.044715, 1.0, op0=ALU.mult, op1=ALU.add)
        k.tt(w.v, w.v, y, ALU.mult, eng="pool")
        k.act(w.v, w.v, AF.Sigmoid, scale=2.0 * GC)
        k.tt(u[:, ts_], y, w.v, ALU.mult)
    k.barrier()
    z = u


    P.checkpoint("toep", z[:, 0:4096])
    for i in range(8):
        P.out_ops.append(k.dma(d_z[:, i * T:(i + 1) * T], z[:, i * T:(i + 1) * T], q="sp" if i % 2 == 0 else "act"))


def s5_consts(P):
    k = P.k
    if hasattr(P, "s5c"):
        return
    P.s5c = True
    idd = P.din("ident", [128, 128])
    P.ident = k.take("ident", [128, 128], F32)
    k.dma(P.ident.v, idd.v)
    P.identb = k.take("identb", [128, 128], BF16)
    k.copy(P.identb.v, P.ident.v)
    d_j = P.din("jtab", [128, 65])
    d_bm = P.din("blockmask", [128, 8])
    d_mbd = P.din("maskbd", [128, 128])
    P.jtab = k.take("jtab", [128, 65], F32)
    P.bmask = k.take("bmask", [128, 8], F32)
    P.mbd = k.take("mbd", [128, 128], F32)
    k.dma(P.jtab.v, d_j.v)
    k.dma(P.bmask.v, d_bm.v)
    k.dma(P.mbd.v, d_mbd.v)


def host_s5(host):
    f = np.float32
    inp = host.inp
    eye = np.eye(128, dtype=f)
    host.common["ident"] = eye
    host.common["jtab"] = np.broadcast_to(np.arange(65, dtype=f)[None, :], (128, 65)).copy()
    host.common["blockmask"] = (np.arange(128)[:, None] // 16 == np.arange(8)[None, :]).astype(f)
    host.common["maskbd"] = (np.arange(128)[:, None] // 16 == np.arange(128)[None, :] // 16).astype(f)
    for c in range(NCORE):
        m = host.percore[c]
        gs = slice(8 * c, 8 * c + 8)
        for idx in range(2):
            nm = "s5_%d_" % idx
            lre = np.asarray(inp["s5_lam_re"], f)[idx, gs]
            lim = np.asarray(inp["s5_lam_im"], f)[idx, gs]
            ldt = np.asarray(inp["s5_log_dt"], f)[idx, gs]
            bre = np.asarray(inp["s5_b_re"], f)[idx, gs]
            bim = np.asarray(inp["s5_b_im"], f)[idx, gs]
            cre = np.asarray(inp["s5_c_re"], f)[idx, gs]
            cim = np.asarray(inp["s5_c_im"], f)[idx, gs]
            m[nm + "lamP"] = np.ascontiguousarray(np.stack([lre.T, lim.T], 1))
            m[nm + "dtP"] = np.ascontiguousarray(np.broadcast_to(ldt[None, :], (64, 8)))
            m[nm + "bP"] = np.ascontiguousarray(np.stack([bre.transpose(1, 0, 2), bim.transpose(1, 0, 2)], 1))
            m[nm + "cP"] = np.ascontiguousarray(np.stack([cre.transpose(2, 0, 1), cim.transpose(2, 0, 1)], 1))
            m[nm + "lamR"] = np.ascontiguousarray(np.stack([np.repeat(lre, 16, 0), np.repeat(lim, 16, 0)], 1))
            m[nm + "dtR"] = np.ascontiguousarray(np.repeat(ldt, 16)[:, None])
            m[nm + "bR"] = np.ascontiguousarray(np.stack([bre.transpose(0, 2, 1).reshape(128, 64),
                                                          bim.transpose(0, 2, 1).reshape(128, 64)], 1))
            m[nm + "dvec"] = np.ascontiguousarray(np.asarray(inp["s5_d"], f)[idx, 128 * c:128 * c + 128][:, None])


HOST_EXTRA.append(host_s5)
STEPS["s5core"] = lambda P, cfg: s5_core(P, cfg["idx"], P.PS)


def _set_x(host, o):
    host.set("xT", [np.asarray(o[c]["outT"]) for c in range(NCORE)])


def _set_gla_inputs(host, o):
    host.set("gT", [np.asarray(o[c]["gT"]) for c in range(NCORE)])
    host.set("g_qT", [np.ascontiguousarray(np.concatenate([np.asarray(o[r]["qT"])[c // 2] for r in range(NCORE)], 1))
                      for c in range(NCORE)])
    host.set("g_k", [_tokmajor_to_core(o, "ktm", (c // 2) * 128) for c in range(NCORE)])
    host.set("g_la", [_tokmajor_to_core(o, "latm", (c // 2) * 128) for c in range(NCORE)])
    host.set("g_v", [_tokmajor_to_core(o, "vtm", (c // 2) * 256 + (c % 2) * 128) for c in range(NCORE)])


def _set_attn_inputs(host, o):
    host.set("a_qT", [np.ascontiguousarray(np.concatenate([np.asarray(o[r]["aqT"])[c] for r in range(NCORE)], 1)) for c in range(NCORE)])
    host.set("a_kT", [np.ascontiguousarray(np.concatenate([np.asarray(o[r]["akT"])[c] for r in range(NCORE)], 1)) for c in range(NCORE)])
    host.set("a_v", [_tokmajor_to_core(o, "avtm", c * 128) for c in range(NCORE)])


def run_s5_layer(host, l, last=False):
    idx = l // 3
    o = launch(host, {"kind": "tok", "steps": ("norm_out:%d,h" % l,)})
    host.set("u", tok_to_chan(o, "h"))
    o = launch(host, {"kind": "s5core", "idx": idx})
    host.set("zf", chan_to_tok(o, "z"))
    steps = ["s5post:%d" % idx, "ffn:%d" % l]
    if last:
        steps.append("final")
    steps.append("store_x:outT")
    _set_x(host, launch(host, {"kind": "tok", "steps": tuple(steps)}))


def run_gla_layer(host, l):
    o = launch(host, {"kind": "tok", "steps": ("glapre",)})
    _set_gla_inputs(host, o)
    o = launch(host, {"kind": "glacore"})
    host.set("goT", chan_to_tok(o, "oT"))
    _set_x(host, launch(host, {"kind": "tok", "steps": ("glapost", "ffn:%d" % l, "store_x:outT")}))


def run_attn_layer(host, l):
    o = launch(host, {"kind": "tok", "steps": ("attnpre",)})
    _set_attn_inputs(host, o)
    o = launch(host, {"kind": "attncore"})
    host.set("aoTt", chan_to_tok(o, "aoT"))
    _set_x(host, launch(host, {"kind": "tok", "steps": ("attnpost", "ffn:%d" % l, "store_x:outT")}))


def gather_out(host):
    outs = [np.asarray(host.percore[c]["xT"], np.float32).reshape(D, T).T for c in range(NCORE)]
    return np.ascontiguousarray(np.concatenate(outs, 0))[None]


def kernel(**inputs):
    host = Host(inputs)
    o = launch(host, {"kind": "tok", "steps": ("norm_out:0,h",)})
    host.set("u", tok_to_chan(o, "h"))
    o = launch(host, {"kind": "s5core", "idx": 0})
    host.set("zf", chan_to_tok(o, "z"))
    o = launch(host, {"kind": "tok", "steps": ("s5post:0", "ffn:0", "glapre", "store_x:outT")})
    _set_x(host, o)
    _set_gla_inputs(host, o)
    o = launch(host, {"kind": "glacore"})
    host.set("goT", chan_to_tok(o, "oT"))
    o = launch(host, {"kind": "tok", "steps": ("glapost", "ffn:1", "attnpre", "store_x:outT")})
    _set_x(host, o)
    _set_attn_inputs(host, o)
    o = launch(host, {"kind": "attncore"})
    host.set("aoTt", chan_to_tok(o, "aoT"))
    o = launch(host, {"kind": "tok", "steps": ("attnpost", "ffn:2", "norm_out:3,h", "store_x:outT")})
    _set_x(host, o)
    host.set("u", tok_to_chan(o, "h"))
    o = launch(host, {"kind": "s5core", "idx": 1})
    host.set("zf", chan_to_tok(o, "z"))
    o = launch(host, {"kind": "tok", "steps": ("s5post:1", "ffn:3", "final", "store_x:outT")})
    _set_x(host, o)
    return gather_out(host)
```
